# Optimizing a Trainium2 kernel written in Bass

```python
import jax, jax.numpy as jnp
from jax import lax
import numpy as np

D_MODEL = 1024
BATCH = 8
SEQ = 2048
DEPTH = 2

N_META = 16
BLOCK = 128
SSD_HEADS = 8
SSD_HEAD_DIM = 64
SSD_D = SSD_HEADS * SSD_HEAD_DIM
SSD_GROUPS = 2
SSD_STATE = 64
SSD_CONV = 4
SSD_CONV_DIM = SSD_D + 2 * SSD_GROUPS * SSD_STATE
FOX_HEADS = 4
FOX_HEAD_DIM = 64
FOX_D = FOX_HEADS * FOX_HEAD_DIM
MLA_HEADS = 4
MLA_Q_LORA = 256
MLA_KV_LORA = 128
MLA_NOPE = 64
MLA_ROPE = 32
MLA_V = 64
MLA_D = MLA_HEADS * MLA_V
ROPE_THETA = 10000.0
D_MIX = SSD_D + FOX_D + MLA_D
IN_SIZES = [SSD_D, SSD_CONV_DIM, SSD_HEADS,
            FOX_D, FOX_D, FOX_D, FOX_HEADS,
            MLA_Q_LORA, MLA_KV_LORA, MLA_ROPE]
N_IN = sum(IN_SIZES)
IN_SPLITS = [int(s) for s in np.cumsum(IN_SIZES)[:-1]]
D_FF = 2816
ALPHA = (2 * DEPTH) ** 0.25
BETA = (8 * DEPTH) ** -0.25
EPS = 1e-5

kernel_name = "hybrid_ssd_fox_mla_macaron_deepnorm"


def layer_norm(x, g, b):
    xf = x.astype(jnp.float32)
    mu = jnp.mean(xf, -1, keepdims=True)
    var = jnp.mean(jnp.square(xf - mu), -1, keepdims=True)
    return ((xf - mu) * lax.rsqrt(var + EPS) * g + b).astype(x.dtype)


def rms_norm(x, g):
    xf = x.astype(jnp.float32)
    y = xf * lax.rsqrt(jnp.mean(jnp.square(xf), -1, keepdims=True) + EPS)
    return (y * g).astype(x.dtype)


def swiglu(x, w_gate, w_up, w_down):
    return (jax.nn.silu(x @ w_gate) * (x @ w_up)) @ w_down


def rope(x, cos, sin):
    x1, x2 = jnp.split(x.astype(jnp.float32), 2, axis=-1)
    return jnp.concatenate([x1 * cos - x2 * sin, x2 * cos + x1 * sin], -1).astype(x.dtype)


def block_edges(total):
    return sorted(set([0] + list(range(N_META, total, BLOCK)) + [total]))


def blocked_causal_attention(logits_fn, v):
    total = v.shape[1]
    outs = []
    edges = block_edges(total)
    for q0, q1 in zip(edges[:-1], edges[1:]):
        s = logits_fn(q0, q1).astype(jnp.float32)
        causal = jnp.arange(q1)[None, :] <= jnp.arange(q0, q1)[:, None]
        s = jnp.where(causal, s, -jnp.inf)
        p = jax.nn.softmax(s, axis=-1).astype(v.dtype)
        outs.append(jnp.einsum('bhqk,bkhd->bqhd', p, v[:, :q1]))
    return jnp.concatenate(outs, axis=1)


def causal_depthwise_conv(x, w, bias):
    out = lax.conv_general_dilated(
        x, w[:, None, :], window_strides=(1,), padding=[(SSD_CONV - 1, 0)],
        dimension_numbers=('NWC', 'WIO', 'NWC'), feature_group_count=x.shape[-1])
    return out + bias


def ssd_chunked(x, dt, A, Bm, Cm):
    b, l, h, p = x.shape
    n = Bm.shape[-1]
    nc = l // BLOCK
    x = x.reshape(b, nc, BLOCK, h, p)
    dt = dt.reshape(b, nc, BLOCK, h)
    Bm = Bm.reshape(b, nc, BLOCK, h, n)
    Cm = Cm.reshape(b, nc, BLOCK, h, n)
    a = jnp.moveaxis(dt * A, -1, 1)
    a_cum = jnp.cumsum(a, axis=-1)
    xdt = x * dt[..., None]
    idx = jnp.arange(BLOCK)
    causal = idx[:, None] >= idx[None, :]
    seg = jnp.exp(jnp.where(causal, a_cum[..., :, None] - a_cum[..., None, :], -jnp.inf))
    cb = jnp.einsum('bclhn,bcshn->bhcls', Cm, Bm)
    y_diag = jnp.einsum('bhcls,bcshp->bclhp', cb * seg, xdt)
    decay_states = jnp.exp(a_cum[..., -1:] - a_cum)
    states = jnp.einsum('bclhn,bhcl,bclhp->bchpn', Bm, decay_states, xdt)
    chunk_decay = jnp.exp(a_cum[..., -1])

    def step(s, inp):
        st, dec = inp
        return s * dec[..., None, None] + st, s

    init = jnp.zeros((b, h, p, n), x.dtype)
    _, prev = lax.scan(step, init, (jnp.moveaxis(states, 1, 0), jnp.moveaxis(chunk_decay, 2, 0)))
    prev = jnp.moveaxis(prev, 0, 1)
    y_off = jnp.einsum('bclhn,bchpn,bhcl->bclhp', Cm, prev, jnp.exp(a_cum))
    return (y_diag + y_off).reshape(b, l, h, p)


def ssd_mixer(z, xbc, dt_raw, conv_w, conv_b, dt_bias, a_log, d_skip, norm_g):
    b, L, _ = xbc.shape
    f32 = jnp.float32
    xbc = jax.nn.silu(causal_depthwise_conv(xbc, conv_w, conv_b)).astype(f32)
    xs, Bm, Cm = jnp.split(xbc, [SSD_D, SSD_D + SSD_GROUPS * SSD_STATE], axis=-1)
    xs = xs.reshape(b, L, SSD_HEADS, SSD_HEAD_DIM)
    rep = SSD_HEADS // SSD_GROUPS
    Bm = jnp.repeat(Bm.reshape(b, L, SSD_GROUPS, SSD_STATE), rep, axis=2)
    Cm = jnp.repeat(Cm.reshape(b, L, SSD_GROUPS, SSD_STATE), rep, axis=2)
    dt = jax.nn.softplus(dt_raw.astype(f32) + dt_bias.astype(f32))
    A = -jnp.exp(a_log.astype(f32))
    pad = (-L) % BLOCK
    padf = lambda t: jnp.pad(t, ((0, 0), (pad, 0)) + ((0, 0),) * (t.ndim - 2))
    y = ssd_chunked(padf(xs), padf(dt), A, padf(Bm), padf(Cm))[:, pad:]
    y = y + d_skip.astype(f32)[:, None] * xs
    y = y.reshape(b, L, SSD_D) * jax.nn.silu(z.astype(f32))
    y = rms_norm(y.reshape(b, L, SSD_GROUPS, SSD_D // SSD_GROUPS), 1.0).reshape(b, L, SSD_D) * norm_g
    return y.astype(z.dtype)


def fox_mixer(q, k, v, f_raw, f_b):
    b, L, _ = q.shape
    q = q.reshape(b, L, FOX_HEADS, FOX_HEAD_DIM)
    k = k.reshape(b, L, FOX_HEADS, FOX_HEAD_DIM)
    v = v.reshape(b, L, FOX_HEADS, FOX_HEAD_DIM)
    log_f = jax.nn.log_sigmoid(f_raw.astype(jnp.float32) + f_b.astype(jnp.float32))
    c = jnp.cumsum(log_f, axis=1).transpose(0, 2, 1)
    scale = FOX_HEAD_DIM ** -0.5

    def logits(q0, q1):
        s = jnp.einsum('bqhd,bkhd->bhqk', q[:, q0:q1], k[:, :q1]).astype(jnp.float32) * scale
        return s + (c[:, :, q0:q1, None] - c[:, :, None, :q1])

    return blocked_causal_attention(logits, v).reshape(b, L, FOX_D)


def mla_mixer(cq, ckv, k_rope, q_norm_g, w_uq, kv_norm_g, w_ukv, cos, sin):
    b, L, _ = cq.shape
    qh = (rms_norm(cq, q_norm_g) @ w_uq).reshape(b, L, MLA_HEADS, MLA_NOPE + MLA_ROPE)
    q_nope, q_rope = qh[..., :MLA_NOPE], qh[..., MLA_NOPE:]
    q_rope = rope(q_rope, cos[None, :, None, :], sin[None, :, None, :])
    kv = (rms_norm(ckv, kv_norm_g) @ w_ukv).reshape(b, L, MLA_HEADS, MLA_NOPE + MLA_V)
    k_nope, v = kv[..., :MLA_NOPE], kv[..., MLA_NOPE:]
    k_rope = rope(k_rope, cos[None], sin[None])
    scale = (MLA_NOPE + MLA_ROPE) ** -0.5

    def logits(q0, q1):
        s = jnp.einsum('bqhd,bkhd->bhqk', q_nope[:, q0:q1], k_nope[:, :q1])
        s = s + jnp.einsum('bqhr,bkr->bhqk', q_rope[:, q0:q1], k_rope[:, :q1])
        return s * scale

    return blocked_causal_attention(logits, v).reshape(b, L, MLA_D)


def setup_inputs(seed: int = 0) -> dict:
    key = jax.random.key(seed)
    ks = iter(jax.random.split(key, 48))
    f32 = jnp.float32
    Dm, F, NL = D_MODEL, D_FF, DEPTH

    def nrm(shape, scale):
        return jax.random.normal(next(ks), shape, f32) * scale

    def gain(shape):
        return 1.0 + nrm(shape, 0.02)

    u = jax.random.uniform(next(ks), (NL, SSD_HEADS), f32)
    dt0 = jnp.exp(u * (np.log(0.1) - np.log(0.001)) + np.log(0.001))
    dt_bias = dt0 + jnp.log(-jnp.expm1(-dt0))
    a_log = jnp.log(jax.random.uniform(next(ks), (NL, SSD_HEADS), f32, 1.0, 16.0))
    return {
        "x": nrm((BATCH, SEQ, Dm), 1.0),
        "meta": nrm((N_META, Dm), 1.0),
        "ffn1_w_gate": nrm((NL, Dm, F), Dm ** -0.5),
        "ffn1_w_up": nrm((NL, Dm, F), Dm ** -0.5),
        "ffn1_w_down": nrm((NL, F, Dm), F ** -0.5 * BETA),
        "ln1_g": gain((NL, Dm)),
        "ln1_b": nrm((NL, Dm), 0.02),
        "w_in": nrm((NL, Dm, N_IN), Dm ** -0.5),
        "conv_w": nrm((NL, SSD_CONV, SSD_CONV_DIM), SSD_CONV ** -0.5),
        "conv_b": nrm((NL, SSD_CONV_DIM), 0.02),
        "dt_bias": dt_bias,
        "a_log": a_log,
        "d_skip": gain((NL, SSD_HEADS)),
        "ssd_norm_g": gain((NL, SSD_D)),
        "fox_f_b": 3.0 + nrm((NL, FOX_HEADS), 0.5),
        "mla_q_norm_g": gain((NL, MLA_Q_LORA)),
        "mla_w_uq": nrm((NL, MLA_Q_LORA, MLA_HEADS * (MLA_NOPE + MLA_ROPE)), MLA_Q_LORA ** -0.5),
        "mla_kv_norm_g": gain((NL, MLA_KV_LORA)),
        "mla_w_ukv": nrm((NL, MLA_KV_LORA, MLA_HEADS * (MLA_NOPE + MLA_V)), MLA_KV_LORA ** -0.5),
        "w_out": nrm((NL, D_MIX, Dm), D_MIX ** -0.5 * BETA),
        "ln2_g": gain((NL, Dm)),
        "ln2_b": nrm((NL, Dm), 0.02),
        "ffn2_w_gate": nrm((NL, Dm, F), Dm ** -0.5),
        "ffn2_w_up": nrm((NL, Dm, F), Dm ** -0.5),
        "ffn2_w_down": nrm((NL, F, Dm), F ** -0.5 * BETA),
        "ln3_g": gain((NL, Dm)),
        "ln3_b": nrm((NL, Dm), 0.02),
    }


def reference(x, meta, ffn1_w_gate, ffn1_w_up, ffn1_w_down, ln1_g, ln1_b, w_in,
              conv_w, conv_b, dt_bias, a_log, d_skip, ssd_norm_g, fox_f_b,
              mla_q_norm_g, mla_w_uq, mla_kv_norm_g, mla_w_ukv, w_out, ln2_g, ln2_b,
              ffn2_w_gate, ffn2_w_up, ffn2_w_down, ln3_g, ln3_b):
    b = x.shape[0]
    h = jnp.concatenate([jnp.broadcast_to(meta[None].astype(x.dtype), (b, N_META, D_MODEL)), x], axis=1)
    total = h.shape[1]
    pos = jnp.arange(total, dtype=jnp.float32)
    inv_freq = 1.0 / (ROPE_THETA ** (jnp.arange(0, MLA_ROPE, 2, dtype=jnp.float32) / MLA_ROPE))
    ang = pos[:, None] * inv_freq[None, :]
    cos, sin = jnp.cos(ang), jnp.sin(ang)

    for l in range(DEPTH):
        h = layer_norm(ALPHA * h + 0.5 * swiglu(h, ffn1_w_gate[l], ffn1_w_up[l], ffn1_w_down[l]),
                       ln1_g[l], ln1_b[l])
        proj = h @ w_in[l]
        (z, xbc, dt_raw, fq, fk, fv, f_raw, cq, ckv, k_rope) = jnp.split(proj, IN_SPLITS, axis=-1)
        y_ssd = ssd_mixer(z, xbc, dt_raw, conv_w[l], conv_b[l], dt_bias[l], a_log[l],
                          d_skip[l], ssd_norm_g[l])
        y_fox = fox_mixer(fq, fk, fv, f_raw, fox_f_b[l])
        y_mla = mla_mixer(cq, ckv, k_rope, mla_q_norm_g[l], mla_w_uq[l], mla_kv_norm_g[l],
                          mla_w_ukv[l], cos, sin)
        mix = jnp.concatenate([y_ssd, y_fox.astype(h.dtype), y_mla.astype(h.dtype)], axis=-1) @ w_out[l]
        h = layer_norm(ALPHA * h + mix, ln2_g[l], ln2_b[l])
        h = layer_norm(ALPHA * h + 0.5 * swiglu(h, ffn2_w_gate[l], ffn2_w_up[l], ffn2_w_down[l]),
                       ln3_g[l], ln3_b[l])
    return h[:, N_META:]
```

```python
import numpy as np
import ml_dtypes
from contextlib import ExitStack
import concourse.bass as bass
import concourse.mybir as mybir
from concourse.bass_utils import run_bass_kernel_spmd

F32 = mybir.dt.float32
BF16 = mybir.dt.bfloat16
AF = mybir.ActivationFunctionType
ALU = mybir.AluOpType

D = 1024
F = 2816
NFC = 22
KC = 8
N_IN = 2476
DEPTH = 2
ALPHA = float((2 * DEPTH) ** 0.25)
EPS = 1e-5
PADR = 112


class Buf:
    __slots__ = ("w", "r", "name")

    def __init__(self, name=""):
        self.w = {}
        self.r = {}
        self.name = name


class KB:
    NDMA = 40

    def __init__(self, nc):
        self.nc = nc
        self.eng = dict(pe=nc.tensor, act=nc.scalar, dve=nc.vector, pool=nc.gpsimd, sp=nc.sync)
        self.sem = {k: nc.alloc_semaphore("s_" + k) for k in self.eng}
        self.cnt = {k: 0 for k in self.eng}
        self.seen = {k: {} for k in self.eng}
        self.dsem = [nc.alloc_semaphore("d%d" % i) for i in range(self.NDMA)]
        self.dval = [0] * self.NDMA
        self.dq = dict(sp=list(range(0, 24)), pool=list(range(24, self.NDMA)))
        self.dnext = dict(sp=0, pool=0)
        self.nwait = 0
        self.nins = 0

    def _wait(self, e, s, v):
        sid = id(s)
        if self.seen[e].get(sid, 0) < v:
            self.eng[e].wait_ge(s, v)
            self.seen[e][sid] = v
            self.nwait += 1

    def _deps(self, e, reads, writes):
        need = {}
        for b in reads:
            for sid, (s, v) in b.w.items():
                if need.get(sid, (None, 0))[1] < v:
                    need[sid] = (s, v)
        for b in writes:
            for d in (b.w, b.r):
                for sid, (s, v) in d.items():
                    if need.get(sid, (None, 0))[1] < v:
                        need[sid] = (s, v)
        if e == "pe":
            need.pop(id(self.sem["pe"]), None)
        for sid, (s, v) in need.items():
            self._wait(e, s, v)

    def _mark(self, s, v, reads, writes):
        sid = id(s)
        for b in reads:
            b.r[sid] = (s, v)
        for b in writes:
            b.w[sid] = (s, v)

    def op(self, e, fn, reads=(), writes=()):
        self._deps(e, reads, writes)
        ins = fn(self.eng[e])
        self.cnt[e] += 1
        ins.then_inc(self.sem[e], 1)
        self._mark(self.sem[e], self.cnt[e], reads, writes)
        self.nins += 1
        return ins

    def dma(self, q, out, in_, reads=(), writes=(), **kw):
        self._deps(q, reads, writes)
        lst = self.dq[q]
        i = lst[self.dnext[q] % len(lst)]
        self.dnext[q] += 1
        s = self.dsem[i]
        self._wait(q, s, self.dval[i])
        ins = self.eng[q].dma_start(out=out, in_=in_, **kw)
        self.dval[i] += 16
        ins.then_inc(s, 16)
        self._mark(s, self.dval[i], reads, writes)
        self.nins += 1
        return ins

    def barrier(self):
        for e in self.eng:
            for kk in self.eng:
                if kk != e and self.cnt[kk] > 0:
                    self._wait(e, self.sem[kk], self.cnt[kk])
            for i in range(self.NDMA):
                if self.dval[i] > 0:
                    self._wait(e, self.dsem[i], self.dval[i])

    def finish(self, bufs):
        for b in bufs:
            for sid, (s, v) in b.w.items():
                self._wait("sp", s, v)


class PsPool:
    def __init__(self, banks):
        self.banks = banks
        self.i = 0

    def get(self):
        b = self.banks[self.i % len(self.banks)]
        self.i += 1
        return b


def chunk_list(NT):
    out = [[0]]
    t = 1
    while t < NT:
        out.append(list(range(t, min(t + 4, NT))))
        t += 4
    return out


PARAM_SHAPES = [
    ("ffn1_w_gate", [D, F]), ("ffn1_w_up", [D, F]), ("ffn1_w_down", [F, D]),
    ("ln1_g", [D]), ("ln1_b", [D]), ("w_in", [D, N_IN]), ("conv_w", [4, 768]), ("conv_b", [768]),
    ("dt_bias", [8]), ("a_log", [8]), ("d_skip", [8]), ("ssd_norm_g", [512]), ("fox_f_b", [4]),
    ("mla_q_norm_g", [256]), ("mla_w_uq", [256, 384]), ("mla_kv_norm_g", [128]), ("mla_w_ukv", [128, 512]),
    ("w_out", [D, D]), ("ln2_g", [D]), ("ln2_b", [D]),
    ("ffn2_w_gate", [D, F]), ("ffn2_w_up", [D, F]), ("ffn2_w_down", [F, D]),
    ("ln3_g", [D]), ("ln3_b", [D]),
]


def build(NT, depth=DEPTH, dbg=None, stop_after=None, only=None):
    S = NT * 128
    nc = bass.Bass("TRN2", target_bir_lowering=False)
    k = KB(nc)
    uid = [0]

    def un(name):
        uid[0] += 1
        return "%s_%d" % (name, uid[0])

    import os as _os
    _cpstop = int(_os.environ.get("SSD_STOP", "0"))

    class _StopPhase(Exception):
        pass

    def cp(n):
        if _cpstop and n == _cpstop:
            raise _StopPhase()

    def din(name, shape, dt=F32):
        return nc.dram_tensor(name, shape, dt, kind="ExternalInput").ap()

    x_in = din("x", [(NT - 1) * 128, D])
    meta_in = din("meta", [16, D])
    W = {name: din(name, [DEPTH] + shp) for name, shp in PARAM_SHAPES}
    c_ident_bf = din("c_ident_bf", [128, 128], BF16)
    c_ident_f = din("c_ident_f", [128, 128])
    c_tri = din("c_tri", [128, 128])
    c_maskneg = din("c_maskneg", [128, 128], BF16)
    c_maskrep = din("c_maskrep", [128, 512], BF16)
    c_cos = din("c_cos", [32, S])
    c_sin = din("c_sin", [32, S])
    c_aug = din("c_aug", [4, 6])
    out_d = nc.dram_tensor("out", [(NT - 1) * 128, D], F32, kind="ExternalOutput").ap()
    hres = nc.dram_tensor("hres", [NT, 128, D], F32).ap()
    mixd = nc.dram_tensor("mixd", [NT, 128, D], BF16).ap()
    hres_b = [Buf("hres%d" % t) for t in range(NT)]
    mixd_b = [[Buf() for _ in range(3)] for t in range(NT)]
    out_b = [Buf() for t in range(NT)]
    dbg_outs = {}
    if dbg:
        for name in dbg:
            if name.startswith("h"):
                dbg_outs[name] = nc.dram_tensor("dbg_" + name, [NT, 128, D], F32, kind="ExternalOutput").ap()
            else:
                dbg_outs[name] = nc.dram_tensor("dbg_" + name, [NT, 128, D], BF16, kind="ExternalOutput").ap()
    dbg_b = []

    PA = nc.alloc_sbuf_tensor
    hT = PA("hT", [128, KC, S], BF16)
    hT_b = [Buf("hT%d" % t) for t in range(NT)]
    ident_bf = PA("ident_bf", [128, 128], BF16)
    ident_f = PA("ident_f", [128, 128], F32)
    tri = PA("tri", [128, 128], F32)
    ones_f = PA("ones_f", [128, 128], F32)
    maskneg = PA("maskneg", [128, 128], BF16)
    maskrep = PA("maskrep", [128, 512], BF16)
    tri_bf = PA("tri_bf", [128, 128], BF16)
    ones_bf = PA("ones_bf", [128, 128], BF16)
    negh = PA("negh", [128, 512], F32)
    aug = PA("aug", [128, 6], F32)
    lncol = PA("lncol", [128, DEPTH * 6, KC], F32)
    cb = Buf("consts")

    banks = []
    for i in range(8):
        banks.append((nc.alloc_psum_tensor("bank%d" % i, [128, 512], F32), Buf("bank%d" % i)))

    k.dma("sp", ident_bf[:, :], c_ident_bf, writes=[cb])
    k.dma("sp", ident_f[:, :], c_ident_f, writes=[cb])
    k.dma("sp", tri[:, :], c_tri, writes=[cb])
    k.dma("sp", maskneg[:, :], c_maskneg, writes=[cb])
    k.dma("sp", maskrep[:, :], c_maskrep, writes=[cb])
    k.dma("sp", aug[64:68, :], c_aug, writes=[cb])
    k.op("dve", lambda e: e.memset(ones_f[:, :], 1.0), writes=[cb])
    k.op("dve", lambda e: e.memset(ones_bf[:, :], 1.0), writes=[cb])
    k.op("dve", lambda e: e.tensor_copy(out=tri_bf[:, :], in_=tri[:, :]), reads=[cb], writes=[cb])
    k.op("dve", lambda e: e.memset(negh[:, :], -0.5), writes=[cb])
    for l in range(depth):
        for i, nm in enumerate(["ln1_g", "ln1_b", "ln2_g", "ln2_b", "ln3_g", "ln3_b"]):
            k.dma("sp", lncol[:, l * 6 + i, :], W[nm][l].rearrange("(kc p) -> p kc", p=128), writes=[cb],
                  allow_slow_non_contiguous=True)
    k.op("dve", lambda e: e.memset(hT[:, :, 0:PADR], 0.0), writes=[hT_b[0]])

    tile_cols = lambda t: (t * 128, (t + 1) * 128)

    def make_ln_bufs(A):
        bufs = []
        for p in range(2):
            d = dict(hin=A("hin", [128, D], F32), yh=A("yh", [128, D], F32), xnb=A("xnb", [128, D], BF16),
                     st=A("st", [128, 12], F32), mv=A("mv", [128, 4], F32))
            d.update(hin_b=Buf(), yh_b=Buf(), xnb_b=Buf(), st_b=Buf(), mv_b=Buf())
            bufs.append(d)
        return bufs

    def load_h(t, lb):
        k.dma("sp", lb["hin"][:, :], hres[t], reads=[hres_b[t]], writes=[lb["hin_b"]])

    def ln_tile(t, ys, coef, gb, gb_b, ci, lb, final):
        hin, yh, xnb, st, mv = lb["hin"], lb["yh"], lb["xnb"], lb["st"], lb["mv"]
        hin_b, yh_b, xnb_b, st_b, mv_b = lb["hin_b"], lb["yh_b"], lb["xnb_b"], lb["st_b"], lb["mv_b"]
        for hf in range(2):
            k.op("act", lambda e: e.activation(out=yh[:, hf * 512:(hf + 1) * 512], in_=ys[hf][0][:, :],
                                               func=AF.Identity, scale=float(coef)),
                 reads=[ys[hf][1]], writes=[yh_b])
        k.op("dve", lambda e: e.scalar_tensor_tensor(out=yh[:, :], in0=hin[:, :], scalar=ALPHA, in1=yh[:, :],
                                                     op0=ALU.mult, op1=ALU.add), reads=[hin_b, yh_b], writes=[yh_b])
        for hf in range(2):
            k.op("dve", lambda e: e.bn_stats(out=st[:, hf * 6:(hf + 1) * 6], in_=yh[:, hf * 512:(hf + 1) * 512]),
                 reads=[yh_b], writes=[st_b])
        k.op("dve", lambda e: e.bn_aggr(out=mv[:, 0:2], in_=st[:, :]), reads=[st_b], writes=[mv_b])
        k.op("dve", lambda e: e.tensor_scalar(out=mv[:, 2:3], in0=mv[:, 1:2], scalar1=EPS, scalar2=None, op0=ALU.add),
             reads=[mv_b], writes=[mv_b])
        k.op("pool", lambda e: e.tensor_tensor(out=mv[:, 2:3], in0=mv[:, 2:3], in1=negh[:, 0:1], op=ALU.pow),
             reads=[mv_b, cb], writes=[mv_b])
        k.op("dve", lambda e: e.scalar_tensor_tensor(out=mv[:, 3:4], in0=mv[:, 0:1], scalar=-1.0, in1=mv[:, 2:3],
                                                     op0=ALU.mult, op1=ALU.mult), reads=[mv_b], writes=[mv_b])
        k.op("dve", lambda e: e.tensor_scalar(out=hin[:, :], in0=yh[:, :], scalar1=mv[:, 0:1], scalar2=mv[:, 2:3],
                                              op0=ALU.subtract, op1=ALU.mult), reads=[yh_b, mv_b], writes=[hin_b])
        k.op("act", lambda e: e.activation(out=xnb[:, :], in_=yh[:, :], func=AF.Identity, scale=mv[:, 2:3],
                                           bias=mv[:, 3:4]), reads=[yh_b, mv_b], writes=[xnb_b])
        k.op("dve", lambda e: e.tensor_tensor(out=yh[:, :], in0=hin[:, :], in1=gb[:, 0, :], op=ALU.mult),
             reads=[hin_b, gb_b], writes=[yh_b])
        k.op("pool", lambda e: e.tensor_tensor(out=yh[:, :], in0=yh[:, :], in1=gb[:, 1, :], op=ALU.add),
             reads=[yh_b, gb_b], writes=[yh_b])
        r0 = PADR if t == 0 else 0
        k.dma("sp", hres[t, r0:128, :], yh[r0:128, :], reads=[yh_b], writes=[hres_b[t]])
        if final and t > 0:
            k.dma("sp", out_d[(t - 1) * 128:t * 128, :], yh[:, :], reads=[yh_b], writes=[out_b[t]])
        tb, tb_b = TP.get()
        tbv = tb[:, :].bitcast(BF16)
        k.op("pe", lambda e: [e.transpose(out=tbv[:, kc * 128:(kc + 1) * 128], in_=xnb[:, kc * 128:(kc + 1) * 128],
                                          identity=ident_bf[:, :]) for kc in range(KC)][-1],
             reads=[xnb_b, cb], writes=[tb_b])
        c0 = PADR if t == 0 else 0
        for kc in range(KC):
            k.op("act", lambda e: e.activation(out=hT[:, kc, t * 128 + c0:(t + 1) * 128],
                                               in_=tbv[:, kc * 128 + c0:(kc + 1) * 128], func=AF.Identity,
                                               scale=lncol[:, ci, kc:kc + 1], bias=lncol[:, ci + 1, kc:kc + 1]),
                 reads=[tb_b, cb], writes=[hT_b[t]])

    def dump_h(name):
        if dbg and name in dbg_outs:
            k.barrier()
            b = Buf()
            k.dma("sp", dbg_outs[name], hres, reads=hres_b, writes=[b])
            dbg_b.append(b)
            k.barrier()

    def dump_mix(name, cols=(0, D)):
        if dbg and name in dbg_outs:
            k.barrier()
            b = Buf()
            for t in range(NT):
                r0 = PADR if t == 0 else 0
                k.dma("sp", dbg_outs[name][t, r0:128, cols[0]:cols[1]], mixd[t, r0:128, cols[0]:cols[1]], reads=mixd_b[t], writes=[b])
            dbg_b.append(b)
            k.barrier()

    TP = PsPool(banks[6:8])

    def phase_init():
        with ExitStack() as es:
            A = lambda nm, shp, dt: es.enter_context(nc.sbuf_tensor(un(nm), shp, dt))
            zt = A("zt", [128, D], F32)
            zt_b = Buf()
            hin = [A("hin0", [128, D], F32) for _ in range(2)]
            hb = [A("hb0", [128, D], BF16) for _ in range(2)]
            hin_b = [Buf(), Buf()]
            hb_b = [Buf(), Buf()]
            k.op("dve", lambda e: e.memset(zt[:, :], 0.0), writes=[zt_b])
            k.dma("sp", hres[0], zt[:, :], reads=[zt_b], writes=[hres_b[0]])
            k.dma("sp", hres[0, PADR:128, :], meta_in, writes=[hres_b[0]])
            for t in range(1, NT):
                k.dma("sp", hres[t], x_in[(t - 1) * 128:t * 128, :], writes=[hres_b[t]])
            for t in range(NT):
                p = t % 2
                k.dma("sp", hin[p][:, :], hres[t], reads=[hres_b[t]], writes=[hin_b[p]])
                k.op("act", lambda e: e.activation(out=hb[p][:, :], in_=hin[p][:, :], func=AF.Identity),
                     reads=[hin_b[p]], writes=[hb_b[p]])
                tb, tb_b = TP.get()
                tbv = tb[:, :].bitcast(BF16)
                k.op("pe", lambda e: [e.transpose(out=tbv[:, kc * 128:(kc + 1) * 128],
                                                  in_=hb[p][:, kc * 128:(kc + 1) * 128],
                                                  identity=ident_bf[:, :]) for kc in range(KC)][-1],
                     reads=[hb_b[p], cb], writes=[tb_b])
                c0 = PADR if t == 0 else 0
                k.op("dve", lambda e: e.tensor_copy(
                    out=hT[:, :, t * 128 + c0:(t + 1) * 128],
                    in_=tbv.rearrange("p (kc c) -> p kc c", kc=KC)[:, :, c0:128]),
                     reads=[tb_b], writes=[hT_b[t]])
        k.barrier()

    def phase_ffn(l, which, ci, final):
        wg = W["ffn%d_w_gate" % which][l].rearrange("(kc p) f -> p kc f", p=128)
        wu = W["ffn%d_w_up" % which][l].rearrange("(kc p) f -> p kc f", p=128)
        wdn = W["ffn%d_w_down" % which][l]
        lg = W["ln%d_g" % (1 if which == 1 else 3)][l]
        lb_ = W["ln%d_b" % (1 if which == 1 else 3)][l]
        PMAXT = 9
        passes = []
        t = 0
        while t < NT:
            passes.append(list(range(t, min(t + PMAXT, NT))))
            t += PMAXT
        if len(passes) > 1 and len(passes[-1]) < 4:
            allt = list(range(NT))
            h = (NT + 1) // 2
            passes = [allt[:h], allt[h:]]
        PG = PsPool(banks[0:6])
        with ExitStack() as es:
            A = lambda nm, shp, dt: es.enter_context(nc.sbuf_tensor(un(nm), shp, dt))
            actT = A("actT", [128, NFC, PMAXT * 128], BF16)
            actT_b = [Buf() for _ in range(PMAXT)]
            wd = A("wd", [128, NFC, D], BF16)
            wd_b = [Buf() for _ in range(NFC)]
            wgu = [A("wgu", [128, 2, KC, 256], BF16) for _ in range(2)]
            wgu_b = [Buf(), Buf()]
            stmp = [A("stmp", [128, 512], F32) for _ in range(2)]
            stmp_b = [Buf(), Buf()]
            gb = A("gb", [128, 2, D], F32)
            gb_b = Buf()
            lnb = make_ln_bufs(A)
            k.dma("sp", gb[:, 0, :], lg.rearrange("(o d) -> o d", o=1).to_broadcast([128, D]), writes=[gb_b])
            k.dma("sp", gb[:, 1, :], lb_.rearrange("(o d) -> o d", o=1).to_broadcast([128, D]), writes=[gb_b])
            si = 0
            for ptiles in passes:
                p0 = ptiles[0] * 128
                chunks = []
                tl = list(ptiles)
                if tl[0] == 0:
                    chunks.append((0, 128, [0]))
                    tl = tl[1:]
                while tl:
                    grp = tl[:4]
                    tl = tl[4:]
                    chunks.append((grp[0] * 128, len(grp) * 128, grp))
                for fcp in range(NFC // 2):
                    wb, wb_b = wgu[fcp % 2], wgu_b[fcp % 2]
                    k.dma("pool", wb[:, 0], wg[:, :, fcp * 256:(fcp + 1) * 256], writes=[wb_b])
                    k.dma("pool", wb[:, 1], wu[:, :, fcp * 256:(fcp + 1) * 256], writes=[wb_b])
                    for j in range(2):
                        fc = fcp * 2 + j
                        k.dma("pool", wd[:, fc, :], wdn[fc * 128:(fc + 1) * 128, :], writes=[wd_b[fc]])
                    for j in range(2):
                        fc = fcp * 2 + j
                        for (s0, n, tiles) in chunks:
                            pg, pg_b = PG.get()
                            pu, pu_b = PG.get()
                            hb = [hT_b[t] for t in tiles]
                            k.op("pe", lambda e: [e.matmul(pg[:, 0:n], lhsT=wb[:, 0, kc, j * 128:(j + 1) * 128],
                                                           rhs=hT[:, kc, s0:s0 + n], start=(kc == 0),
                                                           stop=(kc == KC - 1)) for kc in range(KC)][-1],
                                 reads=[wb_b] + hb, writes=[pg_b])
                            k.op("pe", lambda e: [e.matmul(pu[:, 0:n], lhsT=wb[:, 1, kc, j * 128:(j + 1) * 128],
                                                           rhs=hT[:, kc, s0:s0 + n], start=(kc == 0),
                                                           stop=(kc == KC - 1)) for kc in range(KC)][-1],
                                 reads=[wb_b] + hb, writes=[pu_b])
                            sp_, sp_b = stmp[si % 2], stmp_b[si % 2]
                            si += 1
                            k.op("act", lambda e: e.activation(out=sp_[:, 0:n], in_=pg[:, 0:n], func=AF.Silu),
                                 reads=[pg_b], writes=[sp_b])
                            k.op("dve", lambda e: e.tensor_tensor(out=actT[:, fc, s0 - p0:s0 - p0 + n], in0=sp_[:, 0:n],
                                                                  in1=pu[:, 0:n], op=ALU.mult),
                                 reads=[sp_b, pu_b], writes=[actT_b[t - ptiles[0]] for t in tiles])
                load_h(ptiles[0], lnb[ptiles[0] % 2])
                for ti, t in enumerate(ptiles):
                    if ti + 1 < len(ptiles):
                        load_h(ptiles[ti + 1], lnb[ptiles[ti + 1] % 2])
                    ys = []
                    for hf in range(2):
                        py, py_b = PG.get()
                        k.op("pe", lambda e: [e.matmul(py[:, :], lhsT=actT[:, fc, ti * 128:(ti + 1) * 128],
                                                       rhs=wd[:, fc, hf * 512:(hf + 1) * 512], start=(fc == 0),
                                                       stop=(fc == NFC - 1)) for fc in range(NFC)][-1],
                             reads=[actT_b[ti]] + wd_b, writes=[py_b])
                        ys.append((py, py_b))
                    ln_tile(t, ys, 0.5, gb, gb_b, ci, lnb[t % 2], final)
        k.barrier()

    def attention(KT, KT_b, QT, QT_b, V, V_b, Kd, scale, tiles, h, otile, otile_b, PT, PT_b, pti, STP, OP):
        first, nt, last = tiles[0], len(tiles), tiles[-1]
        n = nt * 128
        po, po_b = OP.get()
        for kt in range(last + 1):
            jk = kt - first
            q0 = 0 if kt < first else jk * 128
            ps_, ps_b = STP.get()
            kcols = slice(kt * 128, (kt + 1) * 128)
            if kt < first:
                k.op("pe", lambda e: e.matmul(ps_[:, 0:n], lhsT=KT[0:Kd, kcols], rhs=QT[0:Kd, 0:n], start=True, stop=True),
                     reads=[KT_b[kt], QT_b], writes=[ps_b])
            else:
                def f(e):
                    e.matmul(ps_[:, q0:q0 + 128], lhsT=KT[0:Kd, kcols], rhs=QT[0:Kd, q0:q0 + 128], start=True, stop=False)
                    r = e.matmul(ps_[:, q0:q0 + 128], lhsT=ident_bf[:, :], rhs=maskneg[:, :], start=False, stop=True)
                    if q0 + 128 < n:
                        r = e.matmul(ps_[:, q0 + 128:n], lhsT=KT[0:Kd, kcols], rhs=QT[0:Kd, q0 + 128:n], start=True, stop=True)
                    return r
                k.op("pe", f, reads=[KT_b[kt], QT_b, cb], writes=[ps_b])
            pt, pt_b = PT[pti[0] % len(PT)], PT_b[pti[0] % len(PT)]
            pti[0] += 1
            k.op("act", lambda e: e.activation(out=pt[:, q0:n], in_=ps_[:, q0:n], func=AF.Exp, scale=float(scale)),
                 reads=[ps_b], writes=[pt_b])
            j0 = max(0, jk)
            k.op("pe", lambda e: [e.matmul(po[:, j * 128:j * 128 + 66], lhsT=pt[:, j * 128:(j + 1) * 128],
                                           rhs=V[:, kt, h, 0:66], start=(kt == 0 and j == j0), stop=(kt == first + j),
                                           skip_group_check=True)
                                  for j in range(j0, nt)][-1],
                 reads=[pt_b, V_b[kt]], writes=[po_b])
        return po, po_b

    def attn_norm(po, po_b, nt, h, otile, otile_b, den, den_b):
        pov = po[:, :].rearrange("p (j c) -> p j c", c=128)
        k.op("dve", lambda e: e.tensor_scalar(out=den[:, 0:nt], in0=pov[:, 0:nt, 64], scalar1=1e-30, scalar2=None,
                                              op0=ALU.add), reads=[po_b], writes=[den_b])
        k.op("dve", lambda e: e.reciprocal(out=den[:, 0:nt], in_=den[:, 0:nt]), reads=[den_b], writes=[den_b])
        k.op("dve", lambda e: e.tensor_tensor(out=otile[:, 0:nt, h * 64:(h + 1) * 64], in0=pov[:, 0:nt, 0:64],
                                              in1=den[:, 0:nt].unsqueeze(2).to_broadcast([128, nt, 64]), op=ALU.mult),
             reads=[po_b, den_b], writes=[otile_b])

    def phase_ssd(l):
        win = W["w_in"][l].rearrange("(kc p) n -> p kc n", p=128)
        PJ = PsPool(banks[0:2])
        PD = PsPool(banks[2:4])
        b_yd, b_yo, b_st, b_sm = banks[4], banks[5], banks[6], banks[7]
        with ExitStack() as es:
            A = lambda nm, shp, dt: es.enter_context(nc.sbuf_tensor(un(nm), shp, dt))
            wss = A("wss", [128, KC, 1288], BF16); wss_b = Buf()
            cw = A("cw", [128, 6, 4], F32); cbias = A("cbias", [128, 6], F32)
            dtb = A("dtb", [128, 8], F32); Ab = A("Ab", [128, 8], F32); dsk = A("dsk", [128, 8], F32)
            ngb = A("ngb", [128, 512], F32)
            pb = Buf("ssd_params")
            xraw = A("xraw", [128, 6, 515], F32); xraw_b = [Buf() for _ in range(6)]
            acc = [A("acc", [128, 512], F32) for _ in range(2)]; acc_b = [Buf(), Buf()]
            xsT = [A("xsT", [128, 512], F32) for _ in range(2)]; xsT_b = [Buf(), Buf()]
            BT = A("BT", [128, 512], BF16); BT_b = Buf()
            CT = A("CT", [128, 512], BF16); CT_b = Buf()
            BTg = [A("BTg", [128, 512], BF16) for _ in range(2)]; BTg_b = Buf()
            CTg = [A("CTg", [128, 512], BF16) for _ in range(2)]; CTg_b = Buf()
            xs_tm = A("xs_tm", [128, 4, 512], F32); xs_tm_b = [Buf() for _ in range(4)]
            B_tm = A("B_tm", [128, 4, 128], BF16); B_tm_b = Buf()
            sm = A("sm", [128, 64], F32); sm_b = Buf()
            R = [A("R", [128, 1024], BF16) for _ in range(2)]; R_b = Buf()
            negA = [A("negA", [128, 1024], BF16) for _ in range(2)]; negA_b = Buf()
            smb = A("smb", [128, 16], BF16); smb_b = Buf()
            alo = A("alo", [128, 8], F32)
            E = A("E", [128, 1024], F32); E_b = Buf()
            MT = A("MT", [128, 1024], BF16); MT_b = Buf()
            xdt = A("xdt", [128, 512], BF16); xdt_b = Buf()
            xw = A("xw", [128, 512], BF16); xw_b = Buf()
            y1 = A("y1", [128, 512], F32); y1_b = Buf()
            y2 = A("y2", [128, 512], F32); y2_b = Buf()
            Sst = A("Sst", [128, 256], F32); Sst_b = Buf()
            Stmp = A("Stmp", [128, 256], F32); Stmp_b = Buf()
            Sbf = A("Sbf", [128, 256], BF16); Sbf_b = Buf()
            sz = A("sz", [128, 512], F32); sz_b = Buf()
            junk = A("junk", [128, 256], F32); junk_b = Buf()
            ss = A("ss", [128, 2], F32); ss_b = Buf()
            yo = [A("yo", [128, 512], BF16) for _ in range(2)]; yo_b = [Buf(), Buf()]

            k.dma("pool", wss[:, :, :], win[:, :, 0:1288], writes=[wss_b])
            for j in range(4):
                k.dma("sp", cw[:, :, j], W["conv_w"][l, j].rearrange("(cc p) -> p cc", p=128), writes=[pb],
                      allow_slow_non_contiguous=True)
            k.dma("sp", cbias[:, :], W["conv_b"][l].rearrange("(cc p) -> p cc", p=128), writes=[pb],
                  allow_slow_non_contiguous=True)
            bc = lambda ap, n_: ap.rearrange("(o d) -> o d", o=1).to_broadcast([128, n_])
            k.dma("sp", dtb[:, :], bc(W["dt_bias"][l], 8), writes=[pb])
            k.dma("sp", Ab[:, :], bc(W["a_log"][l], 8), writes=[pb])
            k.dma("sp", dsk[:, :], bc(W["d_skip"][l], 8), writes=[pb])
            k.dma("sp", ngb[:, :], bc(W["ssd_norm_g"][l], 512), writes=[pb])
            k.op("act", lambda e: e.activation(out=Ab[:, :], in_=Ab[:, :], func=AF.Exp), reads=[pb], writes=[pb])
            k.op("dve", lambda e: e.tensor_scalar(out=Ab[:, :], in0=Ab[:, :], scalar1=-1.0, scalar2=None, op0=ALU.mult),
                 reads=[pb], writes=[pb])
            k.op("dve", lambda e: e.memset(xraw[:, :, 0:3], 0.0), writes=xraw_b)
            for g in range(2):
                k.op("dve", lambda e: e.memset(BTg[g][:, :], 0.0), writes=[BTg_b])
                k.op("dve", lambda e: e.memset(CTg[g][:, :], 0.0), writes=[CTg_b])
            k.op("dve", lambda e: e.memset(Sst[:, :], 0.0), writes=[Sst_b])
            k.op("dve", lambda e: e.memset(Sbf[:, :], 0.0), writes=[Sbf_b])
            cp(1)

            xi = 0
            for tiles in chunk_list(NT):
                s0, nt = tiles[0] * 128, len(tiles)
                n = nt * 128
                hb = [hT_b[t] for t in tiles]
                for cc in range(6):
                    pj, pj_b = PJ.get()
                    k.op("pe", lambda e: [e.matmul(pj[:, 0:n], lhsT=wss[:, kc, 512 + cc * 128:512 + (cc + 1) * 128],
                                                   rhs=hT[:, kc, s0:s0 + n], start=(kc == 0), stop=(kc == KC - 1))
                                          for kc in range(KC)][-1], reads=[wss_b] + hb, writes=[pj_b])
                    k.op("act", lambda e: e.activation(out=xraw[:, cc, 3:3 + n], in_=pj[:, 0:n], func=AF.Identity),
                         reads=[pj_b], writes=[xraw_b[cc]])
                    ac, ac_b = acc[cc % 2], acc_b[cc % 2]
                    k.op("dve", lambda e: e.tensor_scalar(out=ac[:, 0:n], in0=xraw[:, cc, 3:3 + n], scalar1=cw[:, cc, 3:4],
                                                          scalar2=None, op0=ALU.mult), reads=[xraw_b[cc], pb], writes=[ac_b])
                    for j in (2, 1, 0):
                        k.op("dve", lambda e: e.scalar_tensor_tensor(out=ac[:, 0:n], in0=xraw[:, cc, j:j + n],
                                                                     scalar=cw[:, cc, j:j + 1], in1=ac[:, 0:n],
                                                                     op0=ALU.mult, op1=ALU.add),
                             reads=[xraw_b[cc], pb, ac_b], writes=[ac_b])
                    k.op("dve", lambda e: e.tensor_copy(out=xraw[:, cc, 0:3], in_=xraw[:, cc, n:n + 3]),
                         reads=[xraw_b[cc]], writes=[xraw_b[cc]])
                    if cc < 4:
                        xo, xo_b = xsT[xi % 2], xsT_b[xi % 2]
                        xi += 1
                    elif cc == 4:
                        xo, xo_b = BT, BT_b
                    else:
                        xo, xo_b = CT, CT_b
                    k.op("act", lambda e: e.activation(out=xo[:, 0:n], in_=ac[:, 0:n], func=AF.Silu, bias=cbias[:, cc:cc + 1]),
                         reads=[ac_b, pb], writes=[xo_b])
                    if tiles[0] == 0:
                        k.op("dve", lambda e: e.memset(xo[:, 0:PADR], 0.0), writes=[xo_b])
                    if cc >= 4:
                        tg, tg_b = (BTg, BTg_b) if cc == 4 else (CTg, CTg_b)
                        for g in range(2):
                            k.op("act", lambda e: e.activation(out=tg[g][g * 64:(g + 1) * 64, 0:n], in_=xo[g * 64:(g + 1) * 64, 0:n],
                                                               func=AF.Identity), reads=[xo_b], writes=[tg_b])
                    cp(2)
                    if cc < 4:
                        pt_, pt_b = PJ.get()
                        k.op("pe", lambda e: [e.transpose(out=pt_[:, j * 128:(j + 1) * 128], in_=xo[:, j * 128:(j + 1) * 128],
                                                          identity=ident_f[:, :]) for j in range(nt)][-1],
                             reads=[xo_b, cb], writes=[pt_b])
                        k.op("act", lambda e: e.activation(
                            out=xs_tm[:, 0:nt, cc * 128:(cc + 1) * 128],
                            in_=pt_[:, 0:n].rearrange("p (j c) -> p j c", c=128), func=AF.Identity),
                             reads=[pt_b], writes=xs_tm_b[0:nt])
                        cp(3)
                    elif cc == 4:
                        pt_, pt_b = PJ.get()
                        ptv = pt_[:, :].bitcast(BF16)
                        k.op("pe", lambda e: [e.transpose(out=ptv[:, j * 128:(j + 1) * 128], in_=xo[:, j * 128:(j + 1) * 128],
                                                          identity=ident_bf[:, :]) for j in range(nt)][-1],
                             reads=[xo_b, cb], writes=[pt_b])
                        k.op("act", lambda e: e.activation(out=B_tm[:, 0:nt, :],
                                                           in_=ptv[:, 0:n].rearrange("p (j c) -> p j c", c=128),
                                                           func=AF.Identity), reads=[pt_b], writes=[B_tm_b])
                cp(4)
                for j, t in enumerate(tiles):
                    c0, c1 = t * 128, (t + 1) * 128
                    jc = slice(j * 128, (j + 1) * 128)
                    pz, pz_b = PJ.get()
                    k.op("pe", lambda e: [e.matmul(pz[:, :], lhsT=hT[:, kc, c0:c1], rhs=wss[:, kc, 0:512], start=(kc == 0),
                                                   stop=(kc == KC - 1)) for kc in range(KC)][-1],
                         reads=[wss_b, hT_b[t]], writes=[pz_b])
                    psm, psm_b = b_sm
                    k.op("pe", lambda e: [e.matmul(psm[:, 0:8], lhsT=hT[:, kc, c0:c1], rhs=wss[:, kc, 1280:1288],
                                                   start=(kc == 0), stop=(kc == KC - 1)) for kc in range(KC)][-1],
                         reads=[wss_b, hT_b[t]], writes=[psm_b])
                    dtr, e1, dt_, a_, ct, ea, dd, dec, cds = (sm[:, 0:8], sm[:, 8:16], sm[:, 16:24], sm[:, 24:32],
                                                              sm[:, 32:48], sm[:, 48:56], sm[:, 56:64], None, None)
                    k.op("dve", lambda e: e.tensor_tensor(out=dtr, in0=psm[:, 0:8], in1=dtb[:, :], op=ALU.add),
                         reads=[psm_b, pb], writes=[sm_b])
                    k.op("act", lambda e: e.activation(out=e1, in_=dtr, func=AF.Exp), reads=[sm_b], writes=[sm_b])
                    k.op("act", lambda e: e.activation(out=dt_, in_=e1, func=AF.Ln, bias=1.0), reads=[sm_b], writes=[sm_b])
                    if t == 0:
                        k.op("dve", lambda e: e.memset(sm[0:PADR, 16:24], 0.0), reads=[sm_b], writes=[sm_b])
                    k.op("dve", lambda e: e.tensor_tensor(out=a_, in0=dt_, in1=Ab[:, :], op=ALU.mult),
                         reads=[sm_b, pb], writes=[sm_b])
                    cp(5)
                    k.op("dve", lambda e: e.tensor_copy(out=smb[:, 0:8], in_=a_), reads=[sm_b], writes=[smb_b])
                    k.op("dve", lambda e: e.tensor_tensor(out=alo[:, :], in0=a_, in1=smb[:, 0:8], op=ALU.subtract),
                         reads=[sm_b, smb_b], writes=[smb_b])
                    k.op("dve", lambda e: e.tensor_copy(out=smb[:, 8:16], in_=alo[:, :]), reads=[smb_b], writes=[smb_b])
                    k.op("pe", lambda e: [e.matmul(psm[:, 8:16], lhsT=tri_bf[:, :], rhs=smb[:, 0:8], start=True, stop=False),
                                          e.matmul(psm[:, 8:16], lhsT=tri_bf[:, :], rhs=smb[:, 8:16], start=False, stop=True),
                                          e.matmul(psm[:, 16:24], lhsT=ones_bf[:, :], rhs=smb[:, 0:8], start=True, stop=False),
                                          e.matmul(psm[:, 16:24], lhsT=ones_bf[:, :], rhs=smb[:, 8:16], start=False, stop=True)][-1],
                         reads=[smb_b, cb], writes=[psm_b])
                    k.op("act", lambda e: e.activation(out=ct, in_=psm[:, 8:24], func=AF.Identity), reads=[psm_b], writes=[sm_b])
                    cum, tot = sm[:, 32:40], sm[:, 40:48]
                    k.op("act", lambda e: e.activation(out=ea, in_=cum, func=AF.Exp), reads=[sm_b], writes=[sm_b])
                    k.op("dve", lambda e: e.tensor_tensor(out=dd, in0=tot, in1=cum, op=ALU.subtract), reads=[sm_b], writes=[sm_b])
                    k.op("act", lambda e: e.activation(out=dd, in_=dd, func=AF.Exp), reads=[sm_b], writes=[sm_b])
                    k.op("act", lambda e: e.activation(out=sm[0:64, 8:12], in_=sm[0:64, 40:44], func=AF.Exp), reads=[sm_b], writes=[sm_b])
                    k.op("act", lambda e: e.activation(out=sm[64:128, 8:12], in_=sm[64:128, 44:48], func=AF.Exp), reads=[sm_b], writes=[sm_b])
                    cds_ = sm[:, 8:12]
                    cp(6)
                    for i2 in range(2):
                        a_bc = smb[:, i2 * 8:(i2 + 1) * 8].unsqueeze(2).to_broadcast([128, 8, 128])
                        k.op("dve", lambda e: e.tensor_tensor(out=R[i2][:, :].rearrange("p (h c) -> p h c", h=8),
                                                              in0=tri_bf[:, :].unsqueeze(1).to_broadcast([128, 8, 128]),
                                                              in1=a_bc, op=ALU.mult), reads=[smb_b, cb], writes=[R_b])
                        k.op("dve", lambda e: e.tensor_scalar(out=negA[i2][:, :].rearrange("p (h c) -> p h c", h=8), in0=a_bc,
                                                              scalar1=-1.0, scalar2=None, op0=ALU.mult),
                             reads=[smb_b], writes=[negA_b])
                    cp(7)
                    k.op("pe", lambda e: [e.matmul(psm[:, 256 + g * 128:256 + (g + 1) * 128], lhsT=BTg[g][:, jc],
                                                   rhs=CT[:, jc], start=True, stop=True) for g in range(2)][-1],
                         reads=[BTg_b, CT_b], writes=[psm_b])
                    for g in range(2):
                        pD, pD_b = PD.get()
                        hs = slice(g * 512, (g + 1) * 512)
                        k.op("pe", lambda e: [e.matmul(pD[:, :], lhsT=ones_bf[:, :], rhs=R[0][:, hs], start=True, stop=False),
                                              e.matmul(pD[:, :], lhsT=ones_bf[:, :], rhs=R[1][:, hs], start=False, stop=False),
                                              e.matmul(pD[:, :], lhsT=tri_bf[:, :], rhs=negA[0][:, hs], start=False, stop=False),
                                              e.matmul(pD[:, :], lhsT=tri_bf[:, :], rhs=negA[1][:, hs], start=False, stop=False),
                                              e.matmul(pD[:, :], lhsT=ident_bf[:, :], rhs=maskrep[:, :], start=False, stop=True)][-1],
                             reads=[R_b, negA_b, cb], writes=[pD_b])
                        k.op("act", lambda e: e.activation(out=E[:, hs], in_=pD[:, :], func=AF.Exp), reads=[pD_b], writes=[E_b])
                        for h4 in range(4):
                            hc = slice(g * 512 + h4 * 128, g * 512 + (h4 + 1) * 128)
                            k.op("dve", lambda e: e.tensor_tensor(out=MT[:, hc], in0=E[:, hc],
                                                                  in1=psm[:, 256 + g * 128:256 + (g + 1) * 128],
                                                                  op=ALU.mult), reads=[E_b, psm_b], writes=[MT_b])
                    cp(8)
                    xs3 = xs_tm[:, j, :].rearrange("p (h c) -> p h c", h=8)
                    k.op("dve", lambda e: e.tensor_tensor(out=xdt[:, :].rearrange("p (h c) -> p h c", h=8), in0=xs3,
                                                           in1=dt_.unsqueeze(2).to_broadcast([128, 8, 64]), op=ALU.mult),
                         reads=[xs_tm_b[j], sm_b], writes=[xdt_b])
                    k.op("dve", lambda e: e.tensor_tensor(out=xw[:, :].rearrange("p (h c) -> p h c", h=8),
                                                           in0=xdt[:, :].rearrange("p (h c) -> p h c", h=8),
                                                           in1=dd.unsqueeze(2).to_broadcast([128, 8, 64]), op=ALU.mult),
                         reads=[xdt_b, sm_b], writes=[xw_b])
                    pyd, pyd_b = b_yd
                    k.op("pe", lambda e: [e.matmul(pyd[:, hh * 64:(hh + 1) * 64], lhsT=MT[:, hh * 128:(hh + 1) * 128],
                                                   rhs=xdt[:, hh * 64:(hh + 1) * 64], start=True, stop=True) for hh in range(8)][-1],
                         reads=[MT_b, xdt_b], writes=[pyd_b])
                    pyo, pyo_b = b_yo
                    k.op("pe", lambda e: [e.matmul(pyo[:, g * 256:(g + 1) * 256], lhsT=CTg[g][:, jc],
                                                   rhs=Sbf[:, :], start=True, stop=True) for g in range(2)][-1],
                         reads=[CTg_b, Sbf_b], writes=[pyo_b])
                    pst, pst_b = b_st
                    k.op("pe", lambda e: [e.matmul(pst[:, g * 256:(g + 1) * 256], lhsT=B_tm[:, j, :],
                                                   rhs=xw[:, g * 256:(g + 1) * 256], start=True, stop=True) for g in range(2)][-1],
                         reads=[B_tm_b, xw_b], writes=[pst_b])
                    cp(9)
                    k.op("dve", lambda e: e.tensor_tensor(out=y1[:, :].rearrange("p (h c) -> p h c", h=8),
                                                          in0=pyo[:, :].rearrange("p (h c) -> p h c", h=8),
                                                          in1=ea.unsqueeze(2).to_broadcast([128, 8, 64]), op=ALU.mult),
                         reads=[pyo_b, sm_b], writes=[y1_b])
                    k.op("dve", lambda e: e.tensor_tensor(out=y1[:, :], in0=y1[:, :], in1=pyd[:, :], op=ALU.add),
                         reads=[y1_b, pyd_b], writes=[y1_b])
                    k.op("dve", lambda e: e.tensor_tensor(out=y2[:, :].rearrange("p (h c) -> p h c", h=8), in0=xs3,
                                                           in1=dsk[:, :].unsqueeze(2).to_broadcast([128, 8, 64]), op=ALU.mult),
                         reads=[xs_tm_b[j], pb], writes=[y2_b])
                    k.op("dve", lambda e: e.tensor_tensor(out=y1[:, :], in0=y1[:, :], in1=y2[:, :], op=ALU.add),
                         reads=[y1_b, y2_b], writes=[y1_b])
                    for g in range(2):
                        rs = slice(g * 64, (g + 1) * 64)
                        k.op("dve", lambda e: e.tensor_tensor(out=Stmp[rs, :].rearrange("p (h c) -> p h c", h=4),
                                                              in0=Sst[rs, :].rearrange("p (h c) -> p h c", h=4),
                                                              in1=cds_[rs, :].unsqueeze(2).to_broadcast([64, 4, 64]),
                                                              op=ALU.mult), reads=[Sst_b, sm_b], writes=[Stmp_b])
                        k.op("dve", lambda e: e.tensor_tensor(out=Sst[rs, :], in0=Stmp[rs, :], in1=pst[rs, g * 256:(g + 1) * 256],
                                                              op=ALU.add), reads=[Stmp_b, pst_b], writes=[Sst_b])
                    k.op("act", lambda e: e.activation(out=Sbf[:, :], in_=Sst[:, :], func=AF.Identity), reads=[Sst_b], writes=[Sbf_b])
                    cp(10)
                    k.op("act", lambda e: e.activation(out=sz[:, :], in_=pz[:, :], func=AF.Silu), reads=[pz_b], writes=[sz_b])
                    k.op("dve", lambda e: e.tensor_tensor(out=y1[:, :], in0=y1[:, :], in1=sz[:, :], op=ALU.mult),
                         reads=[y1_b, sz_b], writes=[y1_b])
                    for g in range(2):
                        k.op("dve", lambda e: e.bn_stats(out=junk[:, g * 6:(g + 1) * 6], in_=y1[:, g * 256:(g + 1) * 256]),
                             reads=[y1_b], writes=[junk_b])
                        k.op("dve", lambda e: e.bn_aggr(out=junk[:, 16 + g * 2:18 + g * 2], in_=junk[:, g * 6:(g + 1) * 6]),
                             reads=[junk_b], writes=[junk_b])
                        k.op("dve", lambda e: e.scalar_tensor_tensor(out=ss[:, g:g + 1], in0=junk[:, 16 + g * 2:17 + g * 2],
                                                                     scalar=junk[:, 16 + g * 2:17 + g * 2],
                                                                     in1=junk[:, 17 + g * 2:18 + g * 2], op0=ALU.mult, op1=ALU.add),
                             reads=[junk_b], writes=[ss_b])
                    k.op("dve", lambda e: e.tensor_scalar(out=ss[:, :], in0=ss[:, :], scalar1=EPS, scalar2=None,
                                                          op0=ALU.add), reads=[ss_b], writes=[ss_b])
                    k.op("pool", lambda e: e.tensor_tensor(out=ss[:, :], in0=ss[:, :], in1=negh[:, 0:2], op=ALU.pow),
                         reads=[ss_b, cb], writes=[ss_b])
                    yb, yb_b = yo[t % 2], yo_b[t % 2]
                    for g in range(2):
                        gs = slice(g * 256, (g + 1) * 256)
                        k.op("dve", lambda e: e.scalar_tensor_tensor(out=yb[:, gs], in0=y1[:, gs], scalar=ss[:, g:g + 1],
                                                                     in1=ngb[:, gs], op0=ALU.mult, op1=ALU.mult),
                             reads=[y1_b, ss_b, pb], writes=[yb_b])
                    cp(11)
                    r0 = PADR if t == 0 else 0
                    k.dma("sp", mixd[t, r0:128, 0:512], yb[r0:128, :], reads=[yb_b], writes=[mixd_b[t][0]])
        k.barrier()

    def phase_fox(l):
        win = W["w_in"][l].rearrange("(kc p) n -> p kc n", p=128)
        PJ = PsPool(banks[0:2])
        STP = PsPool(banks[2:5])
        OP = PsPool(banks[5:7])
        with ExitStack() as es:
            A = lambda nm, shp, dt: es.enter_context(nc.sbuf_tensor(un(nm), shp, dt))
            wf = A("wf", [128, KC, 772], BF16); wf_b = Buf()
            wqa = A("wqa", [128, 4, KC, 68], BF16); wqa_b = Buf()
            fbn = A("fbn", [128, 4], F32); fbn_b = Buf()
            KT = [A("KTf", [128, S], BF16) for _ in range(4)]
            KT_b = [[Buf() for _ in range(NT)] for _ in range(4)]
            V = A("Vf", [128, NT, 4, 66], BF16); V_b = [Buf() for _ in range(NT)]
            QT = [A("QTf", [128, 512], BF16) for _ in range(4)]; QT_b = [Buf() for _ in range(4)]
            ef = A("ef", [128, 512], F32); ef_b = Buf()
            c4 = A("c4", [128, 512], F32); c4_b = Buf()
            chi = A("chi", [128, 512], BF16); chi_b = Buf()
            clo = A("clo", [128, 512], F32); clo_b = Buf()
            ones4 = A("ones4", [128, 512], F32); ones4_b = Buf()
            cprev = A("cprev", [128, 4], F32); cprev_b = Buf()
            PT = [A("PT", [128, 512], BF16) for _ in range(3)]; PT_b = [Buf() for _ in range(3)]
            pti = [0]
            otile = [A("otile", [128, 4, 256], BF16) for _ in range(2)]; otile_b = [Buf(), Buf()]
            den = A("den", [128, 4], F32); den_b = Buf()

            k.dma("pool", wf[:, :, :], win[:, :, 1288:2060], writes=[wf_b])
            k.dma("sp", fbn[64:68, :], W["fox_f_b"][l].rearrange("(o d) -> o d", o=1).to_broadcast([4, 4]), writes=[fbn_b])
            k.op("dve", lambda e: e.tensor_scalar(out=fbn[64:68, :], in0=fbn[64:68, :], scalar1=-1.0, scalar2=None, op0=ALU.mult),
                 reads=[fbn_b], writes=[fbn_b])
            for h in range(4):
                k.op("dve", lambda e: e.tensor_copy(out=wqa[:, h, :, 0:64], in_=wf[:, :, h * 64:(h + 1) * 64]),
                     reads=[wf_b], writes=[wqa_b])
                k.op("dve", lambda e: e.tensor_copy(out=wqa[:, h, :, 64:68],
                                                    in_=wf[:, :, 768 + h:769 + h].to_broadcast([128, KC, 4])),
                     reads=[wf_b], writes=[wqa_b])
            k.op("dve", lambda e: e.memset(V[:, :, :, 64:66], 0.0), writes=V_b)
            k.op("dve", lambda e: e.memset(V[:, :, :, 64:65], 1.0), writes=V_b)
            k.op("dve", lambda e: e.memset(V[0:PADR, 0, :, 64:65], 0.0), writes=[V_b[0]])
            k.op("dve", lambda e: e.memset(ones4[64:68, :], 1.0), writes=[ones4_b])
            k.op("dve", lambda e: e.memset(cprev[64:68, :], 0.0), writes=[cprev_b])
            R4 = slice(64, 68)
            for ci_, tiles in enumerate(chunk_list(NT)):
                s0, nt = tiles[0] * 128, len(tiles)
                n = nt * 128
                hb = [hT_b[t] for t in tiles]
                for h in range(4):
                    pq, pq_b = PJ.get()
                    k.op("pe", lambda e: [e.matmul(pq[0:68, 0:n], lhsT=wqa[:, h, kc, :], rhs=hT[:, kc, s0:s0 + n],
                                                   start=(kc == 0), stop=(kc == KC - 1)) for kc in range(KC)][-1],
                         reads=[wqa_b] + hb, writes=[pq_b])
                    k.op("act", lambda e: e.activation(out=QT[h][0:64, 0:n], in_=pq[0:64, 0:n], func=AF.Identity),
                         reads=[pq_b], writes=[QT_b[h]])
                    k.op("act", lambda e: e.activation(out=ef[R4, 0:n], in_=pq[R4, 0:n], func=AF.Exp, scale=-1.0,
                                                       bias=fbn[R4, h:h + 1]), reads=[pq_b, fbn_b], writes=[ef_b])
                    k.op("act", lambda e: e.activation(out=ef[R4, 0:n], in_=ef[R4, 0:n], func=AF.Ln, bias=1.0),
                         reads=[ef_b], writes=[ef_b])
                    k.op("dve", lambda e: e.tensor_tensor_scan(out=c4[R4, 0:n], data0=ones4[R4, 0:n], data1=ef[R4, 0:n],
                                                               initial=cprev[R4, h:h + 1], op0=ALU.mult, op1=ALU.subtract),
                         reads=[ones4_b, ef_b, cprev_b], writes=[c4_b])
                    k.op("dve", lambda e: e.tensor_copy(out=cprev[R4, h:h + 1], in_=c4[R4, n - 1:n]), reads=[c4_b], writes=[cprev_b])
                    k.op("dve", lambda e: e.tensor_copy(out=chi[R4, 0:n], in_=c4[R4, 0:n]), reads=[c4_b], writes=[chi_b])
                    k.op("dve", lambda e: e.tensor_tensor(out=clo[R4, 0:n], in0=c4[R4, 0:n], in1=chi[R4, 0:n], op=ALU.subtract),
                         reads=[c4_b, chi_b], writes=[clo_b])
                    kts = [KT_b[h][t] for t in tiles]
                    k.op("dve", lambda e: e.tensor_scalar(out=c4[R4, 0:n], in0=chi[R4, 0:n], scalar1=aug[R4, 0:1], scalar2=aug[R4, 2:3],
                                                          op0=ALU.mult, op1=ALU.add), reads=[chi_b, cb, c4_b], writes=[c4_b])
                    k.op("dve", lambda e: e.scalar_tensor_tensor(out=KT[h][R4, s0:s0 + n], in0=clo[R4, 0:n], scalar=aug[R4, 1:2],
                                                                 in1=c4[R4, 0:n], op0=ALU.mult, op1=ALU.add),
                         reads=[clo_b, c4_b, cb], writes=kts)
                    k.op("dve", lambda e: e.tensor_scalar(out=c4[R4, 0:n], in0=chi[R4, 0:n], scalar1=aug[R4, 3:4], scalar2=aug[R4, 5:6],
                                                          op0=ALU.mult, op1=ALU.add), reads=[chi_b, cb, c4_b], writes=[c4_b])
                    k.op("dve", lambda e: e.scalar_tensor_tensor(out=QT[h][R4, 0:n], in0=clo[R4, 0:n], scalar=aug[R4, 4:5],
                                                                 in1=c4[R4, 0:n], op0=ALU.mult, op1=ALU.add),
                         reads=[clo_b, c4_b, cb], writes=[QT_b[h]])
                    pk, pk_b = PJ.get()
                    k.op("pe", lambda e: [e.matmul(pk[0:64, 0:n], lhsT=wf[:, kc, 256 + h * 64:256 + (h + 1) * 64],
                                                   rhs=hT[:, kc, s0:s0 + n], start=(kc == 0), stop=(kc == KC - 1))
                                          for kc in range(KC)][-1], reads=[wf_b] + hb, writes=[pk_b])
                    k.op("act", lambda e: e.activation(out=KT[h][0:64, s0:s0 + n], in_=pk[0:64, 0:n], func=AF.Identity),
                         reads=[pk_b], writes=kts)
                for j, t in enumerate(tiles):
                    pv, pv_b = PJ.get()
                    k.op("pe", lambda e: [e.matmul(pv[:, 0:256], lhsT=hT[:, kc, t * 128:(t + 1) * 128], rhs=wf[:, kc, 512:768],
                                                   start=(kc == 0), stop=(kc == KC - 1)) for kc in range(KC)][-1],
                         reads=[wf_b, hT_b[t]], writes=[pv_b])
                    k.op("act", lambda e: e.activation(out=V[:, t, :, 0:64], in_=pv[:, 0:256].rearrange("p (h c) -> p h c", h=4),
                                                       func=AF.Identity), reads=[pv_b], writes=[V_b[t]])
                ot, ot_b = otile[ci_ % 2], otile_b[ci_ % 2]
                for h in range(4):
                    po, po_b = attention(KT[h], KT_b[h], QT[h], QT_b[h], V, V_b, 68, 0.125, tiles, h, ot, ot_b, PT, PT_b, pti, STP, OP)
                    attn_norm(po, po_b, nt, h, ot, ot_b, den, den_b)
                for j, t in enumerate(tiles):
                    r0 = PADR if t == 0 else 0
                    k.dma("sp", mixd[t, r0:128, 512:768], ot[r0:128, j, :], reads=[ot_b], writes=[mixd_b[t][1]])
        k.barrier()

    def phase_mla(l):
        win = W["w_in"][l].rearrange("(kc p) n -> p kc n", p=128)
        PJ = PsPool(banks[0:2])
        STP = PsPool(banks[2:5])
        OP = PsPool(banks[5:7])
        b_x = banks[7]
        scale = float(96 ** -0.5)
        with ExitStack() as es:
            A = lambda nm, shp, dt: es.enter_context(nc.sbuf_tensor(un(nm), shp, dt))
            wm = A("wm", [128, KC, 416], BF16); wm_b = Buf()
            wuq = A("wuq", [128, 2, 384], BF16); wuqs = A("wuqs", [128, 2, 4, 96], BF16)
            wukv = A("wukv", [128, 512], BF16)
            wv = A("wv", [128, 256], BF16)
            wkr = A("wkr", [128, 2, KC, 96], BF16)
            qg = A("qg", [128, 2], F32); kvg = A("kvg", [128, 1], F32)
            wp_b = Buf()
            KT = [A("KTm", [128, S], BF16) for _ in range(4)]
            KT_b = [[Buf() for _ in range(NT)] for _ in range(4)]
            V = A("Vm", [128, NT, 4, 66], BF16); V_b = [Buf() for _ in range(NT)]
            QT = [A("QTm", [128, 512], BF16) for _ in range(4)]; QT_b = [Buf() for _ in range(4)]
            cqn = A("cqn", [128, 2, 512], BF16); cqn_b = Buf()
            ckvn = A("ckvn", [128, 512], BF16); ckvn_b = Buf()
            sqq = [A("sqq", [128, 512], BF16) for _ in range(2)]; sqq_b = [Buf(), Buf()]
            sqk = A("sqk", [128, 512], BF16); sqk_b = Buf()
            rq = A("rq", [128, 512], F32); rq_b = Buf()
            rkv = A("rkv", [128, 512], F32); rkv_b = Buf()
            rv = A("rv", [128, 4], F32); rv_b = Buf()
            cs = A("cs", [128, 2, 512], F32); cs_b = Buf()
            csr = A("csr", [128, 2, 512], F32); csr_b = Buf()
            t1 = A("t1", [128, 512], F32); t1_b = Buf()
            t2 = A("t2", [128, 512], F32); t2_b = Buf()
            PT = [A("PT", [128, 512], BF16) for _ in range(3)]; PT_b = [Buf() for _ in range(3)]
            pti = [0]
            otile = [A("otile", [128, 4, 256], BF16) for _ in range(2)]; otile_b = [Buf(), Buf()]
            den = A("den", [128, 4], F32); den_b = Buf()

            k.dma("pool", wm[:, :, :], win[:, :, 2060:2476], writes=[wm_b])
            k.dma("pool", wuq[:, :, :], W["mla_w_uq"][l].rearrange("(j p) n -> p j n", p=128), writes=[wp_b])
            k.dma("pool", wukv[:, :], W["mla_w_ukv"][l], writes=[wp_b])
            k.dma("sp", qg[:, :], W["mla_q_norm_g"][l].rearrange("(j p) -> p j", p=128), writes=[wp_b],
                  allow_slow_non_contiguous=True)
            k.dma("sp", kvg[:, :], W["mla_kv_norm_g"][l].rearrange("(p o) -> p o", o=1), writes=[wp_b])
            wuq4 = wuq[:, :, :].rearrange("p j (h c) -> p j h c", h=4)
            k.op("dve", lambda e: e.tensor_copy(out=wuqs[:, :, :, :], in_=wuq4), reads=[wp_b], writes=[wp_b])
            k.op("dve", lambda e: e.tensor_copy(out=wuqs[:, :, :, 64:80], in_=wuq4[:, :, :, 80:96]), reads=[wp_b], writes=[wp_b])
            k.op("dve", lambda e: e.tensor_copy(out=wuqs[:, :, :, 80:96], in_=wuq4[:, :, :, 64:80]), reads=[wp_b], writes=[wp_b])
            k.op("dve", lambda e: e.tensor_copy(out=wv[:, :].rearrange("p (h c) -> p h c", h=4),
                                                in_=wukv[:, :].rearrange("p (h c) -> p h c", h=4)[:, :, 64:128]),
                 reads=[wp_b], writes=[wp_b])
            k.op("dve", lambda e: e.memset(wkr[:, :, :, :], 0.0), writes=[wp_b])
            k.op("dve", lambda e: e.tensor_copy(out=wkr[:, 0, :, 64:96], in_=wm[:, :, 384:416]), reads=[wm_b, wp_b], writes=[wp_b])
            k.op("dve", lambda e: e.tensor_copy(out=wkr[:, 1, :, 64:80], in_=wm[:, :, 400:416]), reads=[wm_b, wp_b], writes=[wp_b])
            k.op("dve", lambda e: e.tensor_copy(out=wkr[:, 1, :, 80:96], in_=wm[:, :, 384:400]), reads=[wm_b, wp_b], writes=[wp_b])
            k.op("dve", lambda e: e.memset(V[:, :, :, 64:66], 0.0), writes=V_b)
            k.op("dve", lambda e: e.memset(V[:, :, :, 64:65], 1.0), writes=V_b)
            k.op("dve", lambda e: e.memset(V[0:PADR, 0, :, 64:65], 0.0), writes=[V_b[0]])
            RR = slice(64, 96)
            for ci_, tiles in enumerate(chunk_list(NT)):
                s0, nt = tiles[0] * 128, len(tiles)
                n = nt * 128
                hb = [hT_b[t] for t in tiles]
                k.dma("sp", cs[RR, 0, 0:n], c_cos[:, s0:s0 + n], writes=[cs_b])
                k.dma("sp", cs[RR, 1, 0:n], c_sin[:, s0:s0 + n], writes=[cs_b])
                for j2 in range(2):
                    pc, pc_b = PJ.get()
                    k.op("pe", lambda e: [e.matmul(pc[:, 0:n], lhsT=wm[:, kc, j2 * 128:(j2 + 1) * 128], rhs=hT[:, kc, s0:s0 + n],
                                                   start=(kc == 0), stop=(kc == KC - 1)) for kc in range(KC)][-1],
                         reads=[wm_b] + hb, writes=[pc_b])
                    k.op("act", lambda e: e.activation(out=cqn[:, j2, 0:n], in_=pc[:, 0:n], func=AF.Identity, scale=qg[:, j2:j2 + 1]),
                         reads=[pc_b, wp_b], writes=[cqn_b])
                    k.op("act", lambda e: e.activation(out=sqq[j2][:, 0:n], in_=pc[:, 0:n], func=AF.Square),
                         reads=[pc_b], writes=[sqq_b[j2]])
                px, px_b = b_x
                k.op("pe", lambda e: [e.matmul(px[:, 0:n], lhsT=ones_bf[:, :], rhs=sqq[0][:, 0:n], start=True, stop=False),
                                      e.matmul(px[:, 0:n], lhsT=ones_bf[:, :], rhs=sqq[1][:, 0:n], start=False, stop=True)][-1],
                     reads=sqq_b + [cb], writes=[px_b])
                k.op("dve", lambda e: e.tensor_scalar(out=rq[:, 0:n], in0=px[:, 0:n], scalar1=1.0 / 256, scalar2=EPS,
                                                      op0=ALU.mult, op1=ALU.add), reads=[px_b], writes=[rq_b])
                k.op("pool", lambda e: e.tensor_tensor(out=rq[:, 0:n], in0=rq[:, 0:n], in1=negh[:, 0:n], op=ALU.pow),
                     reads=[rq_b, cb], writes=[rq_b])
                pc, pc_b = PJ.get()
                k.op("pe", lambda e: [e.matmul(pc[:, 0:n], lhsT=wm[:, kc, 256:384], rhs=hT[:, kc, s0:s0 + n],
                                               start=(kc == 0), stop=(kc == KC - 1)) for kc in range(KC)][-1],
                     reads=[wm_b] + hb, writes=[pc_b])
                k.op("act", lambda e: e.activation(out=ckvn[:, 0:n], in_=pc[:, 0:n], func=AF.Identity, scale=kvg[:, 0:1]),
                     reads=[pc_b, wp_b], writes=[ckvn_b])
                k.op("act", lambda e: e.activation(out=sqk[:, 0:n], in_=pc[:, 0:n], func=AF.Square), reads=[pc_b], writes=[sqk_b])
                px, px_b = b_x
                k.op("pe", lambda e: e.matmul(px[:, 0:n], lhsT=ones_bf[:, :], rhs=sqk[:, 0:n], start=True, stop=True),
                     reads=[sqk_b, cb], writes=[px_b])
                k.op("dve", lambda e: e.tensor_scalar(out=rkv[:, 0:n], in0=px[:, 0:n], scalar1=1.0 / 128, scalar2=EPS,
                                                      op0=ALU.mult, op1=ALU.add), reads=[px_b], writes=[rkv_b])
                k.op("pool", lambda e: e.tensor_tensor(out=rkv[:, 0:n], in0=rkv[:, 0:n], in1=negh[:, 0:n], op=ALU.pow),
                     reads=[rkv_b, cb], writes=[rkv_b])
                px, px_b = b_x
                k.op("pe", lambda e: [e.matmul(px[:, 2 * j:2 * j + 2], lhsT=sqk[:, j * 128:(j + 1) * 128], rhs=ones_bf[:, 0:2],
                                               start=True, stop=True) for j in range(nt)][-1],
                     reads=[sqk_b, cb], writes=[px_b])
                k.op("dve", lambda e: e.tensor_scalar(out=rv[:, 0:nt], in0=px[:, 0:2 * nt].rearrange("p (j c) -> p j c", c=2)[:, :, 0],
                                                      scalar1=1.0 / 128, scalar2=EPS,
                                                      op0=ALU.mult, op1=ALU.add), reads=[px_b], writes=[rv_b])
                k.op("pool", lambda e: e.tensor_tensor(out=rv[:, 0:nt], in0=rv[:, 0:nt], in1=negh[:, 0:nt], op=ALU.pow),
                     reads=[rv_b, cb], writes=[rv_b])
                for i2 in range(2):
                    k.op("dve", lambda e: e.tensor_tensor(out=csr[RR, i2, 0:n], in0=cs[RR, i2, 0:n], in1=rq[RR, 0:n], op=ALU.mult),
                         reads=[cs_b, rq_b], writes=[csr_b])
                for h in range(4):
                    pq1, pq1_b = PJ.get()
                    pq2, pq2_b = PJ.get()
                    k.op("pe", lambda e: [e.matmul(pq1[0:96, 0:n], lhsT=wuq[:, j2, h * 96:(h + 1) * 96], rhs=cqn[:, j2, 0:n],
                                                   start=(j2 == 0), stop=(j2 == 1)) for j2 in range(2)][-1],
                         reads=[wp_b, cqn_b], writes=[pq1_b])
                    k.op("pe", lambda e: [e.matmul(pq2[0:96, 0:n], lhsT=wuqs[:, j2, h, :], rhs=cqn[:, j2, 0:n],
                                                   start=(j2 == 0), stop=(j2 == 1)) for j2 in range(2)][-1],
                         reads=[wp_b, cqn_b], writes=[pq2_b])
                    k.op("dve", lambda e: e.tensor_tensor(out=QT[h][0:64, 0:n], in0=pq1[0:64, 0:n], in1=rq[0:64, 0:n], op=ALU.mult),
                         reads=[pq1_b, rq_b], writes=[QT_b[h]])
                    k.op("dve", lambda e: e.tensor_tensor(out=t1[RR, 0:n], in0=pq1[RR, 0:n], in1=csr[RR, 0, 0:n], op=ALU.mult),
                         reads=[pq1_b, csr_b], writes=[t1_b])
                    k.op("dve", lambda e: e.tensor_tensor(out=t2[RR, 0:n], in0=pq2[RR, 0:n], in1=csr[RR, 1, 0:n], op=ALU.mult),
                         reads=[pq2_b, csr_b], writes=[t2_b])
                    k.op("dve", lambda e: e.tensor_tensor(out=QT[h][RR, 0:n], in0=t1[RR, 0:n], in1=t2[RR, 0:n], op=ALU.add),
                         reads=[t1_b, t2_b], writes=[QT_b[h]])
                pk1, pk1_b = PJ.get()
                pk2, pk2_b = PJ.get()
                for i2, (pk, pk_b) in enumerate([(pk1, pk1_b), (pk2, pk2_b)]):
                    k.op("pe", lambda e: [e.matmul(pk[0:96, 0:n], lhsT=wkr[:, i2, kc, :], rhs=hT[:, kc, s0:s0 + n],
                                                   start=(kc == 0), stop=(kc == KC - 1)) for kc in range(KC)][-1],
                         reads=[wp_b] + hb, writes=[pk_b])
                k.op("dve", lambda e: e.tensor_tensor(out=t1[RR, 0:n], in0=pk1[RR, 0:n], in1=cs[RR, 0, 0:n], op=ALU.mult),
                     reads=[pk1_b, cs_b], writes=[t1_b])
                k.op("dve", lambda e: e.tensor_tensor(out=t2[RR, 0:n], in0=pk2[RR, 0:n], in1=cs[RR, 1, 0:n], op=ALU.mult),
                     reads=[pk2_b, cs_b], writes=[t2_b])
                for h in range(4):
                    kts = [KT_b[h][t] for t in tiles]
                    k.op("dve", lambda e: e.tensor_tensor(out=KT[h][RR, s0:s0 + n], in0=t1[RR, 0:n], in1=t2[RR, 0:n], op=ALU.add),
                         reads=[t1_b, t2_b], writes=kts)
                    pkn, pkn_b = PJ.get()
                    k.op("pe", lambda e: e.matmul(pkn[0:64, 0:n], lhsT=wukv[:, h * 128:h * 128 + 64], rhs=ckvn[:, 0:n],
                                                  start=True, stop=True), reads=[wp_b, ckvn_b], writes=[pkn_b])
                    k.op("dve", lambda e: e.tensor_tensor(out=KT[h][0:64, s0:s0 + n], in0=pkn[0:64, 0:n], in1=rkv[0:64, 0:n], op=ALU.mult),
                         reads=[pkn_b, rkv_b], writes=kts)
                for j, t in enumerate(tiles):
                    pv, pv_b = PJ.get()
                    k.op("pe", lambda e: e.matmul(pv[:, 0:256], lhsT=ckvn[:, j * 128:(j + 1) * 128],
                                                  rhs=wv[:, :],
                                                  start=True, stop=True), reads=[wp_b, ckvn_b], writes=[pv_b])
                    k.op("act", lambda e: e.activation(out=V[:, t, :, 0:64], in_=pv[:, 0:256].rearrange("p (h c) -> p h c", h=4),
                                                       func=AF.Identity, scale=rv[:, j:j + 1]), reads=[pv_b, rv_b], writes=[V_b[t]])
                ot, ot_b = otile[ci_ % 2], otile_b[ci_ % 2]
                for h in range(4):
                    po, po_b = attention(KT[h], KT_b[h], QT[h], QT_b[h], V, V_b, 96, scale, tiles, h, ot, ot_b, PT, PT_b, pti, STP, OP)
                    attn_norm(po, po_b, nt, h, ot, ot_b, den, den_b)
                for j, t in enumerate(tiles):
                    r0 = PADR if t == 0 else 0
                    k.dma("sp", mixd[t, r0:128, 768:1024], ot[r0:128, j, :], reads=[ot_b], writes=[mixd_b[t][2]])
        k.barrier()

    def phase_out(l, ci):
        PG = PsPool(banks[0:6])
        with ExitStack() as es:
            A = lambda nm, shp, dt: es.enter_context(nc.sbuf_tensor(un(nm), shp, dt))
            wout = A("wout", [128, KC, D], BF16); wout_b = Buf()
            mt = [A("mt", [128, D], BF16) for _ in range(2)]; mt_b = [Buf(), Buf()]
            mixT = [A("mixT", [128, KC, 128], BF16) for _ in range(2)]; mixT_b = [Buf(), Buf()]
            gb = A("gb", [128, 2, D], F32); gb_b = Buf()
            lnb = make_ln_bufs(A)
            k.dma("pool", wout[:, :, :], W["w_out"][l].rearrange("(kc p) n -> p kc n", p=128), writes=[wout_b])
            k.dma("sp", gb[:, 0, :], W["ln2_g"][l].rearrange("(o d) -> o d", o=1).to_broadcast([128, D]), writes=[gb_b])
            k.dma("sp", gb[:, 1, :], W["ln2_b"][l].rearrange("(o d) -> o d", o=1).to_broadcast([128, D]), writes=[gb_b])

            def load_m(t):
                p = t % 2
                if t == 0:
                    k.op("dve", lambda e: e.memset(mt[p][:, :], 0.0), writes=[mt_b[p]])
                r0 = PADR if t == 0 else 0
                k.dma("sp", mt[p][r0:128, :], mixd[t, r0:128, :], reads=mixd_b[t], writes=[mt_b[p]])
                load_h(t, lnb[p])

            load_m(0)
            for t in range(NT):
                p = t % 2
                if t + 1 < NT:
                    load_m(t + 1)
                tb, tb_b = TP.get()
                tbv = tb[:, :].bitcast(BF16)
                k.op("pe", lambda e: [e.transpose(out=tbv[:, kc * 128:(kc + 1) * 128], in_=mt[p][:, kc * 128:(kc + 1) * 128],
                                                  identity=ident_bf[:, :]) for kc in range(KC)][-1],
                     reads=[mt_b[p], cb], writes=[tb_b])
                k.op("act", lambda e: e.activation(out=mixT[p][:, :, :], in_=tbv.rearrange("p (kc c) -> p kc c", kc=KC),
                                                   func=AF.Identity), reads=[tb_b], writes=[mixT_b[p]])
                ys = []
                for hf in range(2):
                    py, py_b = PG.get()
                    k.op("pe", lambda e: [e.matmul(py[:, :], lhsT=mixT[p][:, kc, :], rhs=wout[:, kc, hf * 512:(hf + 1) * 512],
                                                   start=(kc == 0), stop=(kc == KC - 1)) for kc in range(KC)][-1],
                         reads=[mixT_b[p], wout_b], writes=[py_b])
                    ys.append((py, py_b))
                ln_tile(t, ys, 1.0, gb, gb_b, ci, lnb[p], False)
        k.barrier()

    def run():
        phase_init()
        if only is not None:
            for ph in only:
                try:
                    dict(ssd=phase_ssd, fox=phase_fox, mla=phase_mla)[ph](0)
                except _StopPhase:
                    k.barrier()
            if not _cpstop:
                for ph in only:
                    dump_mix("mix_0", dict(ssd=(0, 512), fox=(512, 768), mla=(768, 1024))[ph])
            return
        for l in range(depth):
            last = (l == depth - 1)
            phase_ffn(l, 1, l * 6 + 0, False)
            dump_h("h1_%d" % l)
            if stop_after == "h1_%d" % l:
                return
            phase_ssd(l)
            phase_fox(l)
            phase_mla(l)
            dump_mix("mix_%d" % l)
            if stop_after == "mix_%d" % l:
                return
            phase_out(l, l * 6 + 2)
            dump_h("h2_%d" % l)
            if stop_after == "h2_%d" % l:
                return
            phase_ffn(l, 2, l * 6 + 4, last)
            dump_h("h3_%d" % l)
            if stop_after == "h3_%d" % l:
                return

    run()
    k.barrier()
    k.finish(out_b + dbg_b)
    build.stats = dict(nins=k.nins, nwait=k.nwait)
    return nc


def host_consts(NT):
    S = NT * 128
    bf = ml_dtypes.bfloat16
    idx = np.arange(128)
    c = {}
    c["c_ident_bf"] = np.eye(128, dtype=np.float32).astype(bf)
    c["c_ident_f"] = np.eye(128, dtype=np.float32)
    c["c_tri"] = (idx[:, None] <= idx[None, :]).astype(np.float32)
    mneg = np.where(idx[:, None] > idx[None, :], -30000.0, 0.0).astype(np.float32)
    c["c_maskneg"] = mneg.astype(bf)
    c["c_maskrep"] = np.tile(mneg, (1, 4)).astype(bf)
    pos = (np.arange(S) - PADR).astype(np.float32)
    inv_freq = (1.0 / (np.float32(10000.0) ** (np.arange(0, 32, 2, dtype=np.float32) / np.float32(32)))).astype(np.float32)
    ang = pos[None, :] * inv_freq[:, None]
    cos = np.cos(ang).astype(np.float32)
    sin = np.sin(ang).astype(np.float32)
    c["c_cos"] = np.concatenate([cos, cos], 0)
    c["c_sin"] = np.concatenate([-sin, sin], 0)
    aug = np.zeros((4, 6), np.float32)
    aug[0, 0] = -8.0
    aug[1, 1] = -8.0
    aug[2, 2] = 1.0
    aug[3, 2] = 1.0
    aug[2, 3] = 8.0
    aug[3, 4] = 8.0
    aug[0, 5] = 1.0
    aug[1, 5] = 1.0
    c["c_aug"] = aug
    return c


_CACHE = {}


def kernel(**inputs):
    x = np.ascontiguousarray(inputs["x"], dtype=np.float32)
    B, SEQ, _ = x.shape
    NT = SEQ // 128 + 1
    key = (NT,)
    if key not in _CACHE:
        _CACHE[key] = build(NT)
    nc = _CACHE[key]
    consts = host_consts(NT)
    shared = {name: np.ascontiguousarray(inputs[name], dtype=np.float32) for name, _ in PARAM_SHAPES}
    shared["meta"] = np.ascontiguousarray(inputs["meta"], dtype=np.float32)
    shared.update(consts)
    in_maps = []
    for b in range(B):
        m = dict(shared)
        m["x"] = x[b]
        in_maps.append(m)
    res = run_bass_kernel_spmd(nc, in_maps, core_ids=list(range(B)))
    out = np.stack([np.asarray(res.results[b]["out"], dtype=np.float32) for b in range(B)], 0)
    return out
```

```python
import numpy as np
import ml_dtypes
from contextlib import ExitStack
import concourse.bass as bass
import concourse.mybir as mybir
from concourse.bass_utils import run_bass_kernel_spmd

F32 = mybir.dt.float32
BF16 = mybir.dt.bfloat16
AF = mybir.ActivationFunctionType
ALU = mybir.AluOpType

D = 1024
F = 2816
NFC = 22
KC = 8
N_IN = 2476
DEPTH = 2
ALPHA = float((2 * DEPTH) ** 0.25)
EPS = 1e-5
PADR = 112


class Buf:
    __slots__ = ("w", "r", "name")

    def __init__(self, name=""):
        self.w = {}
        self.r = {}
        self.name = name


class KB:
    NDMA = 40

    def __init__(self, nc):
        self.nc = nc
        self.eng = dict(pe=nc.tensor, act=nc.scalar, dve=nc.vector, pool=nc.gpsimd, sp=nc.sync)
        self.sem = {k: nc.alloc_semaphore("s_" + k) for k in self.eng}
        self.cnt = {k: 0 for k in self.eng}
        self.seen = {k: {} for k in self.eng}
        self.dsem = [nc.alloc_semaphore("d%d" % i) for i in range(self.NDMA)]
        self.dval = [0] * self.NDMA
        self.dq = dict(sp=list(range(0, 24)), pool=list(range(24, self.NDMA)))
        self.dnext = dict(sp=0, pool=0)
        self.nwait = 0
        self.nins = 0

    def _wait(self, e, s, v):
        sid = id(s)
        if self.seen[e].get(sid, 0) < v:
            self.eng[e].wait_ge(s, v)
            self.seen[e][sid] = v
            self.nwait += 1

    def _deps(self, e, reads, writes):
        need = {}
        for b in reads:
            for sid, (s, v) in b.w.items():
                if need.get(sid, (None, 0))[1] < v:
                    need[sid] = (s, v)
        for b in writes:
            for d in (b.w, b.r):
                for sid, (s, v) in d.items():
                    if need.get(sid, (None, 0))[1] < v:
                        need[sid] = (s, v)
        if e == "pe":
            need.pop(id(self.sem["pe"]), None)
        for sid, (s, v) in need.items():
            self._wait(e, s, v)

    def _mark(self, s, v, reads, writes):
        sid = id(s)
        for b in reads:
            b.r[sid] = (s, v)
        for b in writes:
            b.w[sid] = (s, v)

    def op(self, e, fn, reads=(), writes=()):
        self._deps(e, reads, writes)
        ins = fn(self.eng[e])
        self.cnt[e] += 1
        ins.then_inc(self.sem[e], 1)
        self._mark(self.sem[e], self.cnt[e], reads, writes)
        self.nins += 1
        return ins

    def dma(self, q, out, in_, reads=(), writes=(), **kw):
        self._deps(q, reads, writes)
        lst = self.dq[q]
        i = lst[self.dnext[q] % len(lst)]
        self.dnext[q] += 1
        s = self.dsem[i]
        self._wait(q, s, self.dval[i])
        ins = self.eng[q].dma_start(out=out, in_=in_, **kw)
        self.dval[i] += 16
        ins.then_inc(s, 16)
        self._mark(s, self.dval[i], reads, writes)
        self.nins += 1
        return ins

    def barrier(self):
        for e in self.eng:
            for kk in self.eng:
                if kk != e and self.cnt[kk] > 0:
                    self._wait(e, self.sem[kk], self.cnt[kk])
            for i in range(self.NDMA):
                if self.dval[i] > 0:
                    self._wait(e, self.dsem[i], self.dval[i])

    def finish(self, bufs):
        for b in bufs:
            for sid, (s, v) in b.w.items():
                self._wait("sp", s, v)


class PsPool:
    def __init__(self, banks):
        self.banks = banks
        self.i = 0

    def get(self):
        b = self.banks[self.i % len(self.banks)]
        self.i += 1
        return b


def chunk_list(NT):
    out = [[0]]
    t = 1
    while t < NT:
        out.append(list(range(t, min(t + 4, NT))))
        t += 4
    return out


PARAM_SHAPES = [
    ("ffn1_w_gate", [D, F]), ("ffn1_w_up", [D, F]), ("ffn1_w_down", [F, D]),
    ("ln1_g", [D]), ("ln1_b", [D]), ("w_in", [D, N_IN]), ("conv_w", [4, 768]), ("conv_b", [768]),
    ("dt_bias", [8]), ("a_log", [8]), ("d_skip", [8]), ("ssd_norm_g", [512]), ("fox_f_b", [4]),
    ("mla_q_norm_g", [256]), ("mla_w_uq", [256, 384]), ("mla_kv_norm_g", [128]), ("mla_w_ukv", [128, 512]),
    ("w_out", [D, D]), ("ln2_g", [D]), ("ln2_b", [D]),
    ("ffn2_w_gate", [D, F]), ("ffn2_w_up", [D, F]), ("ffn2_w_down", [F, D]),
    ("ln3_g", [D]), ("ln3_b", [D]),
]


def build(NT, depth=DEPTH, dbg=None, stop_after=None, only=None):
    S = NT * 128
    nc = bass.Bass("TRN2", target_bir_lowering=False)
    k = KB(nc)
    uid = [0]

    def un(name):
        uid[0] += 1
        return "%s_%d" % (name, uid[0])

    import os as _os
    _cpstop = int(_os.environ.get("SSD_STOP", "0"))

    class _StopPhase(Exception):
        pass

    def cp(n):
        if _cpstop and n == _cpstop:
            raise _StopPhase()

    def din(name, shape, dt=F32):
        return nc.dram_tensor(name, shape, dt, kind="ExternalInput").ap()

    x_in = din("x", [(NT - 1) * 128, D])
    meta_in = din("meta", [16, D])
    W = {name: din(name, [DEPTH] + shp) for name, shp in PARAM_SHAPES}
    c_ident_bf = din("c_ident_bf", [128, 128], BF16)
    c_ident_f = din("c_ident_f", [128, 128])
    c_tri = din("c_tri", [128, 128])
    c_maskneg = din("c_maskneg", [128, 128], BF16)
    c_maskrep = din("c_maskrep", [128, 512], BF16)
    c_cos = din("c_cos", [32, S])
    c_sin = din("c_sin", [32, S])
    c_aug = din("c_aug", [4, 6])
    out_d = nc.dram_tensor("out", [(NT - 1) * 128, D], F32, kind="ExternalOutput").ap()
    hres = nc.dram_tensor("hres", [NT, 128, D], F32).ap()
    mixd = nc.dram_tensor("mixd", [NT, 128, D], BF16).ap()
    hres_b = [Buf("hres%d" % t) for t in range(NT)]
    mixd_b = [[Buf() for _ in range(3)] for t in range(NT)]
    out_b = [Buf() for t in range(NT)]
    dbg_outs = {}
    if dbg:
        for name in dbg:
            if name.startswith("h"):
                dbg_outs[name] = nc.dram_tensor("dbg_" + name, [NT, 128, D], F32, kind="ExternalOutput").ap()
            else:
                dbg_outs[name] = nc.dram_tensor("dbg_" + name, [NT, 128, D], BF16, kind="ExternalOutput").ap()
    dbg_b = []

    PA = nc.alloc_sbuf_tensor
    hT = PA("hT", [128, KC, S], BF16)
    hT_b = [Buf("hT%d" % t) for t in range(NT)]
    ident_bf = PA("ident_bf", [128, 128], BF16)
    ident_f = PA("ident_f", [128, 128], F32)
    tri = PA("tri", [128, 128], F32)
    ones_f = PA("ones_f", [128, 128], F32)
    maskneg = PA("maskneg", [128, 128], BF16)
    maskrep = PA("maskrep", [128, 512], BF16)
    tri_bf = PA("tri_bf", [128, 128], BF16)
    ones_bf = PA("ones_bf", [128, 128], BF16)
    negh = PA("negh", [128, 512], F32)
    aug = PA("aug", [128, 6], F32)
    lncol = PA("lncol", [128, DEPTH * 6, KC], F32)
    cb = Buf("consts")

    banks = []
    for i in range(8):
        banks.append((nc.alloc_psum_tensor("bank%d" % i, [128, 512], F32), Buf("bank%d" % i)))

    k.dma("sp", ident_bf[:, :], c_ident_bf, writes=[cb])
    k.dma("sp", ident_f[:, :], c_ident_f, writes=[cb])
    k.dma("sp", tri[:, :], c_tri, writes=[cb])
    k.dma("sp", maskneg[:, :], c_maskneg, writes=[cb])
    k.dma("sp", maskrep[:, :], c_maskrep, writes=[cb])
    k.dma("sp", aug[64:68, :], c_aug, writes=[cb])
    k.op("dve", lambda e: e.memset(ones_f[:, :], 1.0), writes=[cb])
    k.op("dve", lambda e: e.memset(ones_bf[:, :], 1.0), writes=[cb])
    k.op("dve", lambda e: e.tensor_copy(out=tri_bf[:, :], in_=tri[:, :]), reads=[cb], writes=[cb])
    k.op("dve", lambda e: e.memset(negh[:, :], -0.5), writes=[cb])
    for l in range(depth):
        for i, nm in enumerate(["ln1_g", "ln1_b", "ln2_g", "ln2_b", "ln3_g", "ln3_b"]):
            k.dma("sp", lncol[:, l * 6 + i, :], W[nm][l].rearrange("(kc p) -> p kc", p=128), writes=[cb],
                  allow_slow_non_contiguous=True)
    k.op("dve", lambda e: e.memset(hT[:, :, 0:PADR], 0.0), writes=[hT_b[0]])

    tile_cols = lambda t: (t * 128, (t + 1) * 128)

    def make_ln_bufs(A):
        bufs = []
        for p in range(2):
            d = dict(hin=A("hin", [128, D], F32), yh=A("yh", [128, D], F32), xnb=A("xnb", [128, D], BF16),
                     st=A("st", [128, 12], F32), mv=A("mv", [128, 4], F32))
            d.update(hin_b=Buf(), yh_b=Buf(), xnb_b=Buf(), st_b=Buf(), mv_b=Buf())
            bufs.append(d)
        return bufs

    def load_h(t, lb):
        k.dma("sp", lb["hin"][:, :], hres[t], reads=[hres_b[t]], writes=[lb["hin_b"]])

    def ln_tile(t, ys, coef, gb, gb_b, ci, lb, final):
        hin, yh, xnb, st, mv = lb["hin"], lb["yh"], lb["xnb"], lb["st"], lb["mv"]
        hin_b, yh_b, xnb_b, st_b, mv_b = lb["hin_b"], lb["yh_b"], lb["xnb_b"], lb["st_b"], lb["mv_b"]
        for hf in range(2):
            k.op("act", lambda e: e.activation(out=yh[:, hf * 512:(hf + 1) * 512], in_=ys[hf][0][:, :],
                                               func=AF.Identity, scale=float(coef)),
                 reads=[ys[hf][1]], writes=[yh_b])
        k.op("dve", lambda e: e.scalar_tensor_tensor(out=yh[:, :], in0=hin[:, :], scalar=ALPHA, in1=yh[:, :],
                                                     op0=ALU.mult, op1=ALU.add), reads=[hin_b, yh_b], writes=[yh_b])
        for hf in range(2):
            k.op("dve", lambda e: e.bn_stats(out=st[:, hf * 6:(hf + 1) * 6], in_=yh[:, hf * 512:(hf + 1) * 512]),
                 reads=[yh_b], writes=[st_b])
        k.op("dve", lambda e: e.bn_aggr(out=mv[:, 0:2], in_=st[:, :]), reads=[st_b], writes=[mv_b])
        k.op("dve", lambda e: e.tensor_scalar(out=mv[:, 2:3], in0=mv[:, 1:2], scalar1=EPS, scalar2=None, op0=ALU.add),
             reads=[mv_b], writes=[mv_b])
        k.op("pool", lambda e: e.tensor_tensor(out=mv[:, 2:3], in0=mv[:, 2:3], in1=negh[:, 0:1], op=ALU.pow),
             reads=[mv_b, cb], writes=[mv_b])
        k.op("dve", lambda e: e.scalar_tensor_tensor(out=mv[:, 3:4], in0=mv[:, 0:1], scalar=-1.0, in1=mv[:, 2:3],
                                                     op0=ALU.mult, op1=ALU.mult), reads=[mv_b], writes=[mv_b])
        k.op("dve", lambda e: e.tensor_scalar(out=hin[:, :], in0=yh[:, :], scalar1=mv[:, 0:1], scalar2=mv[:, 2:3],
                                              op0=ALU.subtract, op1=ALU.mult), reads=[yh_b, mv_b], writes=[hin_b])
        k.op("act", lambda e: e.activation(out=xnb[:, :], in_=yh[:, :], func=AF.Identity, scale=mv[:, 2:3],
                                           bias=mv[:, 3:4]), reads=[yh_b, mv_b], writes=[xnb_b])
        k.op("dve", lambda e: e.tensor_tensor(out=yh[:, :], in0=hin[:, :], in1=gb[:, 0, :], op=ALU.mult),
             reads=[hin_b, gb_b], writes=[yh_b])
        k.op("pool", lambda e: e.tensor_tensor(out=yh[:, :], in0=yh[:, :], in1=gb[:, 1, :], op=ALU.add),
             reads=[yh_b, gb_b], writes=[yh_b])
        r0 = PADR if t == 0 else 0
        k.dma("sp", hres[t, r0:128, :], yh[r0:128, :], reads=[yh_b], writes=[hres_b[t]])
        if final and t > 0:
            k.dma("sp", out_d[(t - 1) * 128:t * 128, :], yh[:, :], reads=[yh_b], writes=[out_b[t]])
        tb, tb_b = TP.get()
        tbv = tb[:, :].bitcast(BF16)
        k.op("pe", lambda e: [e.transpose(out=tbv[:, kc * 128:(kc + 1) * 128], in_=xnb[:, kc * 128:(kc + 1) * 128],
                                          identity=ident_bf[:, :]) for kc in range(KC)][-1],
             reads=[xnb_b, cb], writes=[tb_b])
        c0 = PADR if t == 0 else 0
        for kc in range(KC):
            k.op("act", lambda e: e.activation(out=hT[:, kc, t * 128 + c0:(t + 1) * 128],
                                               in_=tbv[:, kc * 128 + c0:(kc + 1) * 128], func=AF.Identity,
                                               scale=lncol[:, ci, kc:kc + 1], bias=lncol[:, ci + 1, kc:kc + 1]),
                 reads=[tb_b, cb], writes=[hT_b[t]])

    def dump_h(name):
        if dbg and name in dbg_outs:
            k.barrier()
            b = Buf()
            k.dma("sp", dbg_outs[name], hres, reads=hres_b, writes=[b])
            dbg_b.append(b)
            k.barrier()

    def dump_mix(name, cols=(0, D)):
        if dbg and name in dbg_outs:
            k.barrier()
            b = Buf()
            for t in range(NT):
                r0 = PADR if t == 0 else 0
                k.dma("sp", dbg_outs[name][t, r0:128, cols[0]:cols[1]], mixd[t, r0:128, cols[0]:cols[1]], reads=mixd_b[t], writes=[b])
            dbg_b.append(b)
            k.barrier()

    TP = PsPool(banks[6:8])

    def phase_init():
        with ExitStack() as es:
            A = lambda nm, shp, dt: es.enter_context(nc.sbuf_tensor(un(nm), shp, dt))
            zt = A("zt", [128, D], F32)
            zt_b = Buf()
            hin = [A("hin0", [128, D], F32) for _ in range(2)]
            hb = [A("hb0", [128, D], BF16) for _ in range(2)]
            hin_b = [Buf(), Buf()]
            hb_b = [Buf(), Buf()]
            k.op("dve", lambda e: e.memset(zt[:, :], 0.0), writes=[zt_b])
            k.dma("sp", hres[0], zt[:, :], reads=[zt_b], writes=[hres_b[0]])
            k.dma("sp", hres[0, PADR:128, :], meta_in, writes=[hres_b[0]])
            for t in range(1, NT):
                k.dma("sp", hres[t], x_in[(t - 1) * 128:t * 128, :], writes=[hres_b[t]])
            for t in range(NT):
                p = t % 2
                k.dma("sp", hin[p][:, :], hres[t], reads=[hres_b[t]], writes=[hin_b[p]])
                k.op("act", lambda e: e.activation(out=hb[p][:, :], in_=hin[p][:, :], func=AF.Identity),
                     reads=[hin_b[p]], writes=[hb_b[p]])
                tb, tb_b = TP.get()
                tbv = tb[:, :].bitcast(BF16)
                k.op("pe", lambda e: [e.transpose(out=tbv[:, kc * 128:(kc + 1) * 128],
                                                  in_=hb[p][:, kc * 128:(kc + 1) * 128],
                                                  identity=ident_bf[:, :]) for kc in range(KC)][-1],
                     reads=[hb_b[p], cb], writes=[tb_b])
                c0 = PADR if t == 0 else 0
                k.op("dve", lambda e: e.tensor_copy(
                    out=hT[:, :, t * 128 + c0:(t + 1) * 128],
                    in_=tbv.rearrange("p (kc c) -> p kc c", kc=KC)[:, :, c0:128]),
                     reads=[tb_b], writes=[hT_b[t]])
        k.barrier()

    def phase_ffn(l, which, ci, final):
        wg = W["ffn%d_w_gate" % which][l].rearrange("(kc p) f -> p kc f", p=128)
        wu = W["ffn%d_w_up" % which][l].rearrange("(kc p) f -> p kc f", p=128)
        wdn = W["ffn%d_w_down" % which][l]
        lg = W["ln%d_g" % (1 if which == 1 else 3)][l]
        lb_ = W["ln%d_b" % (1 if which == 1 else 3)][l]
        PMAXT = 9
        passes = []
        t = 0
        while t < NT:
            passes.append(list(range(t, min(t + PMAXT, NT))))
            t += PMAXT
        if len(passes) > 1 and len(passes[-1]) < 4:
            allt = list(range(NT))
            h = (NT + 1) // 2
            passes = [allt[:h], allt[h:]]
        PG = PsPool(banks[0:6])
        with ExitStack() as es:
            A = lambda nm, shp, dt: es.enter_context(nc.sbuf_tensor(un(nm), shp, dt))
            actT = A("actT", [128, NFC, PMAXT * 128], BF16)
            actT_b = [Buf() for _ in range(PMAXT)]
            wd = A("wd", [128, NFC, D], BF16)
            wd_b = [Buf() for _ in range(NFC)]
            wgu = [A("wgu", [128, 2, KC, 256], BF16) for _ in range(2)]
            wgu_b = [Buf(), Buf()]
            stmp = [A("stmp", [128, 512], F32) for _ in range(2)]
            stmp_b = [Buf(), Buf()]
            gb = A("gb", [128, 2, D], F32)
            gb_b = Buf()
            lnb = make_ln_bufs(A)
            k.dma("sp", gb[:, 0, :], lg.rearrange("(o d) -> o d", o=1).to_broadcast([128, D]), writes=[gb_b])
            k.dma("sp", gb[:, 1, :], lb_.rearrange("(o d) -> o d", o=1).to_broadcast([128, D]), writes=[gb_b])
            si = 0
            for ptiles in passes:
                p0 = ptiles[0] * 128
                chunks = []
                tl = list(ptiles)
                if tl[0] == 0:
                    chunks.append((0, 128, [0]))
                    tl = tl[1:]
                while tl:
                    grp = tl[:4]
                    tl = tl[4:]
                    chunks.append((grp[0] * 128, len(grp) * 128, grp))
                for fcp in range(NFC // 2):
                    wb, wb_b = wgu[fcp % 2], wgu_b[fcp % 2]
                    k.dma("pool", wb[:, 0], wg[:, :, fcp * 256:(fcp + 1) * 256], writes=[wb_b])
                    k.dma("pool", wb[:, 1], wu[:, :, fcp * 256:(fcp + 1) * 256], writes=[wb_b])
                    for j in range(2):
                        fc = fcp * 2 + j
                        k.dma("pool", wd[:, fc, :], wdn[fc * 128:(fc + 1) * 128, :], writes=[wd_b[fc]])
                    for j in range(2):
                        fc = fcp * 2 + j
                        for (s0, n, tiles) in chunks:
                            pg, pg_b = PG.get()
                            pu, pu_b = PG.get()
                            hb = [hT_b[t] for t in tiles]
                            k.op("pe", lambda e: [e.matmul(pg[:, 0:n], lhsT=wb[:, 0, kc, j * 128:(j + 1) * 128],
                                                           rhs=hT[:, kc, s0:s0 + n], start=(kc == 0),
                                                           stop=(kc == KC - 1)) for kc in range(KC)][-1],
                                 reads=[wb_b] + hb, writes=[pg_b])
                            k.op("pe", lambda e: [e.matmul(pu[:, 0:n], lhsT=wb[:, 1, kc, j * 128:(j + 1) * 128],
                                                           rhs=hT[:, kc, s0:s0 + n], start=(kc == 0),
                                                           stop=(kc == KC - 1)) for kc in range(KC)][-1],
                                 reads=[wb_b] + hb, writes=[pu_b])
                            sp_, sp_b = stmp[si % 2], stmp_b[si % 2]
                            si += 1
                            k.op("act", lambda e: e.activation(out=sp_[:, 0:n], in_=pg[:, 0:n], func=AF.Silu),
                                 reads=[pg_b], writes=[sp_b])
                            k.op("dve", lambda e: e.tensor_tensor(out=actT[:, fc, s0 - p0:s0 - p0 + n], in0=sp_[:, 0:n],
                                                                  in1=pu[:, 0:n], op=ALU.mult),
                                 reads=[sp_b, pu_b], writes=[actT_b[t - ptiles[0]] for t in tiles])
                load_h(ptiles[0], lnb[ptiles[0] % 2])
                for ti, t in enumerate(ptiles):
                    if ti + 1 < len(ptiles):
                        load_h(ptiles[ti + 1], lnb[ptiles[ti + 1] % 2])
                    ys = []
                    for hf in range(2):
                        py, py_b = PG.get()
                        k.op("pe", lambda e: [e.matmul(py[:, :], lhsT=actT[:, fc, ti * 128:(ti + 1) * 128],
                                                       rhs=wd[:, fc, hf * 512:(hf + 1) * 512], start=(fc == 0),
                                                       stop=(fc == NFC - 1)) for fc in range(NFC)][-1],
                             reads=[actT_b[ti]] + wd_b, writes=[py_b])
                        ys.append((py, py_b))
                    ln_tile(t, ys, 0.5, gb, gb_b, ci, lnb[t % 2], final)
        k.barrier()

    def attention(KT, KT_b, QT, QT_b, V, V_b, Kd, scale, tiles, h, otile, otile_b, PT, PT_b, pti, STP, OP):
        first, nt, last = tiles[0], len(tiles), tiles[-1]
        n = nt * 128
        po, po_b = OP.get()
        for kt in range(last + 1):
            jk = kt - first
            q0 = 0 if kt < first else jk * 128
            ps_, ps_b = STP.get()
            kcols = slice(kt * 128, (kt + 1) * 128)
            if kt < first:
                k.op("pe", lambda e: e.matmul(ps_[:, 0:n], lhsT=KT[0:Kd, kcols], rhs=QT[0:Kd, 0:n], start=True, stop=True),
                     reads=[KT_b[kt], QT_b], writes=[ps_b])
            else:
                def f(e):
                    e.matmul(ps_[:, q0:q0 + 128], lhsT=KT[0:Kd, kcols], rhs=QT[0:Kd, q0:q0 + 128], start=True, stop=False)
                    r = e.matmul(ps_[:, q0:q0 + 128], lhsT=ident_bf[:, :], rhs=maskneg[:, :], start=False, stop=True)
                    if q0 + 128 < n:
                        r = e.matmul(ps_[:, q0 + 128:n], lhsT=KT[0:Kd, kcols], rhs=QT[0:Kd, q0 + 128:n], start=True, stop=True)
                    return r
                k.op("pe", f, reads=[KT_b[kt], QT_b, cb], writes=[ps_b])
            pt, pt_b = PT[pti[0] % len(PT)], PT_b[pti[0] % len(PT)]
            pti[0] += 1
            k.op("act", lambda e: e.activation(out=pt[:, q0:n], in_=ps_[:, q0:n], func=AF.Exp, scale=float(scale)),
                 reads=[ps_b], writes=[pt_b])
            j0 = max(0, jk)
            k.op("pe", lambda e: [e.matmul(po[:, j * 128:j * 128 + 66], lhsT=pt[:, j * 128:(j + 1) * 128],
                                           rhs=V[:, kt, h, 0:66], start=(kt == 0 and j == j0), stop=(kt == first + j),
                                           skip_group_check=True)
                                  for j in range(j0, nt)][-1],
                 reads=[pt_b, V_b[kt]], writes=[po_b])
        return po, po_b

    def attn_norm(po, po_b, nt, h, otile, otile_b, den, den_b):
        pov = po[:, :].rearrange("p (j c) -> p j c", c=128)
        k.op("dve", lambda e: e.tensor_scalar(out=den[:, 0:nt], in0=pov[:, 0:nt, 64], scalar1=1e-30, scalar2=None,
                                              op0=ALU.add), reads=[po_b], writes=[den_b])
        k.op("dve", lambda e: e.reciprocal(out=den[:, 0:nt], in_=den[:, 0:nt]), reads=[den_b], writes=[den_b])
        k.op("dve", lambda e: e.tensor_tensor(out=otile[:, 0:nt, h * 64:(h + 1) * 64], in0=pov[:, 0:nt, 0:64],
                                              in1=den[:, 0:nt].unsqueeze(2).to_broadcast([128, nt, 64]), op=ALU.mult),
             reads=[po_b, den_b], writes=[otile_b])

    def phase_ssd(l):
        win = W["w_in"][l].rearrange("(kc p) n -> p kc n", p=128)
        PJ = PsPool(banks[0:2])
        PD = PsPool(banks[2:4])
        b_yd, b_yo, b_st, b_sm = banks[4], banks[5], banks[6], banks[7]
        with ExitStack() as es:
            A = lambda nm, shp, dt: es.enter_context(nc.sbuf_tensor(un(nm), shp, dt))
            wss = A("wss", [128, KC, 1288], BF16); wss_b = Buf()
            cw = A("cw", [128, 6, 4], F32); cbias = A("cbias", [128, 6], F32)
            dtb = A("dtb", [128, 8], F32); Ab = A("Ab", [128, 8], F32); dsk = A("dsk", [128, 8], F32)
            ngb = A("ngb", [128, 512], F32)
            pb = Buf("ssd_params")
            xraw = A("xraw", [128, 6, 515], F32); xraw_b = [Buf() for _ in range(6)]
            acc = [A("acc", [128, 512], F32) for _ in range(2)]; acc_b = [Buf(), Buf()]
            xsT = [A("xsT", [128, 512], F32) for _ in range(2)]; xsT_b = [Buf(), Buf()]
            BT = A("BT", [128, 512], BF16); BT_b = Buf()
            CT = A("CT", [128, 512], BF16); CT_b = Buf()
            BTg = [A("BTg", [128, 512], BF16) for _ in range(2)]; BTg_b = Buf()
            CTg = [A("CTg", [128, 512], BF16) for _ in range(2)]; CTg_b = Buf()
            xs_tm = A("xs_tm", [128, 4, 512], F32); xs_tm_b = [Buf() for _ in range(4)]
            B_tm = A("B_tm", [128, 4, 128], BF16); B_tm_b = Buf()
            sm = A("sm", [128, 64], F32); sm_b = Buf()
            R = [A("R", [128, 1024], BF16) for _ in range(2)]; R_b = Buf()
            negA = [A("negA", [128, 1024], BF16) for _ in range(2)]; negA_b = Buf()
            smb = A("smb", [128, 16], BF16); smb_b = Buf()
            alo = A("alo", [128, 8], F32)
            E = A("E", [128, 1024], F32); E_b = Buf()
            MT = A("MT", [128, 1024], BF16); MT_b = Buf()
            xdt = A("xdt", [128, 512], BF16); xdt_b = Buf()
            xw = A("xw", [128, 512], BF16); xw_b = Buf()
            y1 = A("y1", [128, 512], F32); y1_b = Buf()
            y2 = A("y2", [128, 512], F32); y2_b = Buf()
            Sst = A("Sst", [128, 256], F32); Sst_b = Buf()
            Stmp = A("Stmp", [128, 256], F32); Stmp_b = Buf()
            Sbf = A("Sbf", [128, 256], BF16); Sbf_b = Buf()
            sz = A("sz", [128, 512], F32); sz_b = Buf()
            junk = A("junk", [128, 256], F32); junk_b = Buf()
            ss = A("ss", [128, 2], F32); ss_b = Buf()
            yo = [A("yo", [128, 512], BF16) for _ in range(2)]; yo_b = [Buf(), Buf()]

            k.dma("pool", wss[:, :, :], win[:, :, 0:1288], writes=[wss_b])
            for j in range(4):
                k.dma("sp", cw[:, :, j], W["conv_w"][l, j].rearrange("(cc p) -> p cc", p=128), writes=[pb],
                      allow_slow_non_contiguous=True)
            k.dma("sp", cbias[:, :], W["conv_b"][l].rearrange("(cc p) -> p cc", p=128), writes=[pb],
                  allow_slow_non_contiguous=True)
            bc = lambda ap, n_: ap.rearrange("(o d) -> o d", o=1).to_broadcast([128, n_])
            k.dma("sp", dtb[:, :], bc(W["dt_bias"][l], 8), writes=[pb])
            k.dma("sp", Ab[:, :], bc(W["a_log"][l], 8), writes=[pb])
            k.dma("sp", dsk[:, :], bc(W["d_skip"][l], 8), writes=[pb])
            k.dma("sp", ngb[:, :], bc(W["ssd_norm_g"][l], 512), writes=[pb])
            k.op("act", lambda e: e.activation(out=Ab[:, :], in_=Ab[:, :], func=AF.Exp), reads=[pb], writes=[pb])
            k.op("dve", lambda e: e.tensor_scalar(out=Ab[:, :], in0=Ab[:, :], scalar1=-1.0, scalar2=None, op0=ALU.mult),
                 reads=[pb], writes=[pb])
            k.op("dve", lambda e: e.memset(xraw[:, :, 0:3], 0.0), writes=xraw_b)
            for g in range(2):
                k.op("dve", lambda e: e.memset(BTg[g][:, :], 0.0), writes=[BTg_b])
                k.op("dve", lambda e: e.memset(CTg[g][:, :], 0.0), writes=[CTg_b])
            k.op("dve", lambda e: e.memset(Sst[:, :], 0.0), writes=[Sst_b])
            k.op("dve", lambda e: e.memset(Sbf[:, :], 0.0), writes=[Sbf_b])
            cp(1)

            xi = 0
            for tiles in chunk_list(NT):
                s0, nt = tiles[0] * 128, len(tiles)
                n = nt * 128
                hb = [hT_b[t] for t in tiles]
                for cc in range(6):
                    pj, pj_b = PJ.get()
                    k.op("pe", lambda e: [e.matmul(pj[:, 0:n], lhsT=wss[:, kc, 512 + cc * 128:512 + (cc + 1) * 128],
                                                   rhs=hT[:, kc, s0:s0 + n], start=(kc == 0), stop=(kc == KC - 1))
                                          for kc in range(KC)][-1], reads=[wss_b] + hb, writes=[pj_b])
                    k.op("act", lambda e: e.activation(out=xraw[:, cc, 3:3 + n], in_=pj[:, 0:n], func=AF.Identity),
                         reads=[pj_b], writes=[xraw_b[cc]])
                    ac, ac_b = acc[cc % 2], acc_b[cc % 2]
                    k.op("dve", lambda e: e.tensor_scalar(out=ac[:, 0:n], in0=xraw[:, cc, 3:3 + n], scalar1=cw[:, cc, 3:4],
                                                          scalar2=None, op0=ALU.mult), reads=[xraw_b[cc], pb], writes=[ac_b])
                    for j in (2, 1, 0):
                        k.op("dve", lambda e: e.scalar_tensor_tensor(out=ac[:, 0:n], in0=xraw[:, cc, j:j + n],
                                                                     scalar=cw[:, cc, j:j + 1], in1=ac[:, 0:n],
                                                                     op0=ALU.mult, op1=ALU.add),
                             reads=[xraw_b[cc], pb, ac_b], writes=[ac_b])
                    k.op("dve", lambda e: e.tensor_copy(out=xraw[:, cc, 0:3], in_=xraw[:, cc, n:n + 3]),
                         reads=[xraw_b[cc]], writes=[xraw_b[cc]])
                    if cc < 4:
                        xo, xo_b = xsT[xi % 2], xsT_b[xi % 2]
                        xi += 1
                    elif cc == 4:
                        xo, xo_b = BT, BT_b
                    else:
                        xo, xo_b = CT, CT_b
                    k.op("act", lambda e: e.activation(out=xo[:, 0:n], in_=ac[:, 0:n], func=AF.Silu, bias=cbias[:, cc:cc + 1]),
                         reads=[ac_b, pb], writes=[xo_b])
                    if tiles[0] == 0:
                        k.op("dve", lambda e: e.memset(xo[:, 0:PADR], 0.0), writes=[xo_b])
                    if cc >= 4:
                        tg, tg_b = (BTg, BTg_b) if cc == 4 else (CTg, CTg_b)
                        for g in range(2):
                            k.op("act", lambda e: e.activation(out=tg[g][g * 64:(g + 1) * 64, 0:n], in_=xo[g * 64:(g + 1) * 64, 0:n],
                                                               func=AF.Identity), reads=[xo_b], writes=[tg_b])
                    cp(2)
                    if cc < 4:
                        pt_, pt_b = PJ.get()
                        k.op("pe", lambda e: [e.transpose(out=pt_[:, j * 128:(j + 1) * 128], in_=xo[:, j * 128:(j + 1) * 128],
                                                          identity=ident_f[:, :]) for j in range(nt)][-1],
                             reads=[xo_b, cb], writes=[pt_b])
                        k.op("act", lambda e: e.activation(
                            out=xs_tm[:, 0:nt, cc * 128:(cc + 1) * 128],
                            in_=pt_[:, 0:n].rearrange("p (j c) -> p j c", c=128), func=AF.Identity),
                             reads=[pt_b], writes=xs_tm_b[0:nt])
                        cp(3)
                    elif cc == 4:
                        pt_, pt_b = PJ.get()
                        ptv = pt_[:, :].bitcast(BF16)
                        k.op("pe", lambda e: [e.transpose(out=ptv[:, j * 128:(j + 1) * 128], in_=xo[:, j * 128:(j + 1) * 128],
                                                          identity=ident_bf[:, :]) for j in range(nt)][-1],
                             reads=[xo_b, cb], writes=[pt_b])
                        k.op("act", lambda e: e.activation(out=B_tm[:, 0:nt, :],
                                                           in_=ptv[:, 0:n].rearrange("p (j c) -> p j c", c=128),
                                                           func=AF.Identity), reads=[pt_b], writes=[B_tm_b])
                cp(4)
                for j, t in enumerate(tiles):
                    c0, c1 = t * 128, (t + 1) * 128
                    jc = slice(j * 128, (j + 1) * 128)
                    pz, pz_b = PJ.get()
                    k.op("pe", lambda e: [e.matmul(pz[:, :], lhsT=hT[:, kc, c0:c1], rhs=wss[:, kc, 0:512], start=(kc == 0),
                                                   stop=(kc == KC - 1)) for kc in range(KC)][-1],
                         reads=[wss_b, hT_b[t]], writes=[pz_b])
                    psm, psm_b = b_sm
                    k.op("pe", lambda e: [e.matmul(psm[:, 0:8], lhsT=hT[:, kc, c0:c1], rhs=wss[:, kc, 1280:1288],
                                                   start=(kc == 0), stop=(kc == KC - 1)) for kc in range(KC)][-1],
                         reads=[wss_b, hT_b[t]], writes=[psm_b])
                    dtr, e1, dt_, a_, ct, ea, dd, dec, cds = (sm[:, 0:8], sm[:, 8:16], sm[:, 16:24], sm[:, 24:32],
                                                              sm[:, 32:48], sm[:, 48:56], sm[:, 56:64], None, None)
                    k.op("dve", lambda e: e.tensor_tensor(out=dtr, in0=psm[:, 0:8], in1=dtb[:, :], op=ALU.add),
                         reads=[psm_b, pb], writes=[sm_b])
                    k.op("act", lambda e: e.activation(out=e1, in_=dtr, func=AF.Exp), reads=[sm_b], writes=[sm_b])
                    k.op("act", lambda e: e.activation(out=dt_, in_=e1, func=AF.Ln, bias=1.0), reads=[sm_b], writes=[sm_b])
                    if t == 0:
                        k.op("dve", lambda e: e.memset(sm[0:PADR, 16:24], 0.0), reads=[sm_b], writes=[sm_b])
                    k.op("dve", lambda e: e.tensor_tensor(out=a_, in0=dt_, in1=Ab[:, :], op=ALU.mult),
                         reads=[sm_b, pb], writes=[sm_b])
                    cp(5)
                    k.op("dve", lambda e: e.tensor_copy(out=smb[:, 0:8], in_=a_), reads=[sm_b], writes=[smb_b])
                    k.op("dve", lambda e: e.tensor_tensor(out=alo[:, :], in0=a_, in1=smb[:, 0:8], op=ALU.subtract),
                         reads=[sm_b, smb_b], writes=[smb_b])
                    k.op("dve", lambda e: e.tensor_copy(out=smb[:, 8:16], in_=alo[:, :]), reads=[smb_b], writes=[smb_b])
                    k.op("pe", lambda e: [e.matmul(psm[:, 8:16], lhsT=tri_bf[:, :], rhs=smb[:, 0:8], start=True, stop=False),
                                          e.matmul(psm[:, 8:16], lhsT=tri_bf[:, :], rhs=smb[:, 8:16], start=False, stop=True),
                                          e.matmul(psm[:, 16:24], lhsT=ones_bf[:, :], rhs=smb[:, 0:8], start=True, stop=False),
                                          e.matmul(psm[:, 16:24], lhsT=ones_bf[:, :], rhs=smb[:, 8:16], start=False, stop=True)][-1],
                         reads=[smb_b, cb], writes=[psm_b])
                    k.op("act", lambda e: e.activation(out=ct, in_=psm[:, 8:24], func=AF.Identity), reads=[psm_b], writes=[sm_b])
                    cum, tot = sm[:, 32:40], sm[:, 40:48]
                    k.op("act", lambda e: e.activation(out=ea, in_=cum, func=AF.Exp), reads=[sm_b], writes=[sm_b])
                    k.op("dve", lambda e: e.tensor_tensor(out=dd, in0=tot, in1=cum, op=ALU.subtract), reads=[sm_b], writes=[sm_b])
                    k.op("act", lambda e: e.activation(out=dd, in_=dd, func=AF.Exp), reads=[sm_b], writes=[sm_b])
                    k.op("act", lambda e: e.activation(out=sm[0:64, 8:12], in_=sm[0:64, 40:44], func=AF.Exp), reads=[sm_b], writes=[sm_b])
                    k.op("act", lambda e: e.activation(out=sm[64:128, 8:12], in_=sm[64:128, 44:48], func=AF.Exp), reads=[sm_b], writes=[sm_b])
                    cds_ = sm[:, 8:12]
                    cp(6)
                    for i2 in range(2):
                        a_bc = smb[:, i2 * 8:(i2 + 1) * 8].unsqueeze(2).to_broadcast([128, 8, 128])
                        k.op("dve", lambda e: e.tensor_tensor(out=R[i2][:, :].rearrange("p (h c) -> p h c", h=8),
                                                              in0=tri_bf[:, :].unsqueeze(1).to_broadcast([128, 8, 128]),
                                                              in1=a_bc, op=ALU.mult), reads=[smb_b, cb], writes=[R_b])
                        k.op("dve", lambda e: e.tensor_scalar(out=negA[i2][:, :].rearrange("p (h c) -> p h c", h=8), in0=a_bc,
                                                              scalar1=-1.0, scalar2=None, op0=ALU.mult),
                             reads=[smb_b], writes=[negA_b])
                    cp(7)
                    k.op("pe", lambda e: [e.matmul(psm[:, 256 + g * 128:256 + (g + 1) * 128], lhsT=BTg[g][:, jc],
                                                   rhs=CT[:, jc], start=True, stop=True) for g in range(2)][-1],
                         reads=[BTg_b, CT_b], writes=[psm_b])
                    for g in range(2):
                        pD, pD_b = PD.get()
                        hs = slice(g * 512, (g + 1) * 512)
                        k.op("pe", lambda e: [e.matmul(pD[:, :], lhsT=ones_bf[:, :], rhs=R[0][:, hs], start=True, stop=False),
                                              e.matmul(pD[:, :], lhsT=ones_bf[:, :], rhs=R[1][:, hs], start=False, stop=False),
                                              e.matmul(pD[:, :], lhsT=tri_bf[:, :], rhs=negA[0][:, hs], start=False, stop=False),
                                              e.matmul(pD[:, :], lhsT=tri_bf[:, :], rhs=negA[1][:, hs], start=False, stop=False),
                                              e.matmul(pD[:, :], lhsT=ident_bf[:, :], rhs=maskrep[:, :], start=False, stop=True)][-1],
                             reads=[R_b, negA_b, cb], writes=[pD_b])
                        k.op("act", lambda e: e.activation(out=E[:, hs], in_=pD[:, :], func=AF.Exp), reads=[pD_b], writes=[E_b])
                        for h4 in range(4):
                            hc = slice(g * 512 + h4 * 128, g * 512 + (h4 + 1) * 128)
                            k.op("dve", lambda e: e.tensor_tensor(out=MT[:, hc], in0=E[:, hc],
                                                                  in1=psm[:, 256 + g * 128:256 + (g + 1) * 128],
                                                                  op=ALU.mult), reads=[E_b, psm_b], writes=[MT_b])
                    cp(8)
                    xs3 = xs_tm[:, j, :].rearrange("p (h c) -> p h c", h=8)
                    k.op("dve", lambda e: e.tensor_tensor(out=xdt[:, :].rearrange("p (h c) -> p h c", h=8), in0=xs3,
                                                           in1=dt_.unsqueeze(2).to_broadcast([128, 8, 64]), op=ALU.mult),
                         reads=[xs_tm_b[j], sm_b], writes=[xdt_b])
                    k.op("dve", lambda e: e.tensor_tensor(out=xw[:, :].rearrange("p (h c) -> p h c", h=8),
                                                           in0=xdt[:, :].rearrange("p (h c) -> p h c", h=8),
                                                           in1=dd.unsqueeze(2).to_broadcast([128, 8, 64]), op=ALU.mult),
                         reads=[xdt_b, sm_b], writes=[xw_b])
                    pyd, pyd_b = b_yd
                    k.op("pe", lambda e: [e.matmul(pyd[:, hh * 64:(hh + 1) * 64], lhsT=MT[:, hh * 128:(hh + 1) * 128],
                                                   rhs=xdt[:, hh * 64:(hh + 1) * 64], start=True, stop=True) for hh in range(8)][-1],
                         reads=[MT_b, xdt_b], writes=[pyd_b])
                    pyo, pyo_b = b_yo
                    k.op("pe", lambda e: [e.matmul(pyo[:, g * 256:(g + 1) * 256], lhsT=CTg[g][:, jc],
                                                   rhs=Sbf[:, :], start=True, stop=True) for g in range(2)][-1],
                         reads=[CTg_b, Sbf_b], writes=[pyo_b])
                    pst, pst_b = b_st
                    k.op("pe", lambda e: [e.matmul(pst[:, g * 256:(g + 1) * 256], lhsT=B_tm[:, j, :],
                                                   rhs=xw[:, g * 256:(g + 1) * 256], start=True, stop=True) for g in range(2)][-1],
                         reads=[B_tm_b, xw_b], writes=[pst_b])
                    cp(9)
                    k.op("dve", lambda e: e.tensor_tensor(out=y1[:, :].rearrange("p (h c) -> p h c", h=8),
                                                          in0=pyo[:, :].rearrange("p (h c) -> p h c", h=8),
                                                          in1=ea.unsqueeze(2).to_broadcast([128, 8, 64]), op=ALU.mult),
                         reads=[pyo_b, sm_b], writes=[y1_b])
                    k.op("dve", lambda e: e.tensor_tensor(out=y1[:, :], in0=y1[:, :], in1=pyd[:, :], op=ALU.add),
                         reads=[y1_b, pyd_b], writes=[y1_b])
                    k.op("dve", lambda e: e.tensor_tensor(out=y2[:, :].rearrange("p (h c) -> p h c", h=8), in0=xs3,
                                                           in1=dsk[:, :].unsqueeze(2).to_broadcast([128, 8, 64]), op=ALU.mult),
                         reads=[xs_tm_b[j], pb], writes=[y2_b])
                    k.op("dve", lambda e: e.tensor_tensor(out=y1[:, :], in0=y1[:, :], in1=y2[:, :], op=ALU.add),
                         reads=[y1_b, y2_b], writes=[y1_b])
                    for g in range(2):
                        rs = slice(g * 64, (g + 1) * 64)
                        k.op("dve", lambda e: e.tensor_tensor(out=Stmp[rs, :].rearrange("p (h c) -> p h c", h=4),
                                                              in0=Sst[rs, :].rearrange("p (h c) -> p h c", h=4),
                                                              in1=cds_[rs, :].unsqueeze(2).to_broadcast([64, 4, 64]),
                                                              op=ALU.mult), reads=[Sst_b, sm_b], writes=[Stmp_b])
                        k.op("dve", lambda e: e.tensor_tensor(out=Sst[rs, :], in0=Stmp[rs, :], in1=pst[rs, g * 256:(g + 1) * 256],
                                                              op=ALU.add), reads=[Stmp_b, pst_b], writes=[Sst_b])
                    k.op("act", lambda e: e.activation(out=Sbf[:, :], in_=Sst[:, :], func=AF.Identity), reads=[Sst_b], writes=[Sbf_b])
                    cp(10)
                    k.op("act", lambda e: e.activation(out=sz[:, :], in_=pz[:, :], func=AF.Silu), reads=[pz_b], writes=[sz_b])
                    k.op("dve", lambda e: e.tensor_tensor(out=y1[:, :], in0=y1[:, :], in1=sz[:, :], op=ALU.mult),
                         reads=[y1_b, sz_b], writes=[y1_b])
                    for g in range(2):
                        k.op("dve", lambda e: e.bn_stats(out=junk[:, g * 6:(g + 1) * 6], in_=y1[:, g * 256:(g + 1) * 256]),
                             reads=[y1_b], writes=[junk_b])
                        k.op("dve", lambda e: e.bn_aggr(out=junk[:, 16 + g * 2:18 + g * 2], in_=junk[:, g * 6:(g + 1) * 6]),
                             reads=[junk_b], writes=[junk_b])
                        k.op("dve", lambda e: e.scalar_tensor_tensor(out=ss[:, g:g + 1], in0=junk[:, 16 + g * 2:17 + g * 2],
                                                                     scalar=junk[:, 16 + g * 2:17 + g * 2],
                                                                     in1=junk[:, 17 + g * 2:18 + g * 2], op0=ALU.mult, op1=ALU.add),
                             reads=[junk_b], writes=[ss_b])
                    k.op("dve", lambda e: e.tensor_scalar(out=ss[:, :], in0=ss[:, :], scalar1=EPS, scalar2=None,
                                                          op0=ALU.add), reads=[ss_b], writes=[ss_b])
                    k.op("pool", lambda e: e.tensor_tensor(out=ss[:, :], in0=ss[:, :], in1=negh[:, 0:2], op=ALU.pow),
                         reads=[ss_b, cb], writes=[ss_b])
                    yb, yb_b = yo[t % 2], yo_b[t % 2]
                    for g in range(2):
                        gs = slice(g * 256, (g + 1) * 256)
                        k.op("dve", lambda e: e.scalar_tensor_tensor(out=yb[:, gs], in0=y1[:, gs], scalar=ss[:, g:g + 1],
                                                                     in1=ngb[:, gs], op0=ALU.mult, op1=ALU.mult),
                             reads=[y1_b, ss_b, pb], writes=[yb_b])
                    cp(11)
                    r0 = PADR if t == 0 else 0
                    k.dma("sp", mixd[t, r0:128, 0:512], yb[r0:128, :], reads=[yb_b], writes=[mixd_b[t][0]])
        k.barrier()

    def phase_fox(l):
        win = W["w_in"][l].rearrange("(kc p) n -> p kc n", p=128)
        PJ = PsPool(banks[0:2])
        STP = PsPool(banks[2:5])
        OP = PsPool(banks[5:7])
        with ExitStack() as es:
            A = lambda nm, shp, dt: es.enter_context(nc.sbuf_tensor(un(nm), shp, dt))
            wf = A("wf", [128, KC, 772], BF16); wf_b = Buf()
            wqa = A("wqa", [128, 4, KC, 68], BF16); wqa_b = Buf()
            fbn = A("fbn", [128, 4], F32); fbn_b = Buf()
            KT = [A("KTf", [128, S], BF16) for _ in range(4)]
            KT_b = [[Buf() for _ in range(NT)] for _ in range(4)]
            V = A("Vf", [128, NT, 4, 66], BF16); V_b = [Buf() for _ in range(NT)]
            QT = [A("QTf", [128, 512], BF16) for _ in range(4)]; QT_b = [Buf() for _ in range(4)]
            ef = A("ef", [128, 512], F32); ef_b = Buf()
            c4 = A("c4", [128, 512], F32); c4_b = Buf()
            chi = A("chi", [128, 512], BF16); chi_b = Buf()
            clo = A("clo", [128, 512], F32); clo_b = Buf()
            ones4 = A("ones4", [128, 512], F32); ones4_b = Buf()
            cprev = A("cprev", [128, 4], F32); cprev_b = Buf()
            PT = [A("PT", [128, 512], BF16) for _ in range(3)]; PT_b = [Buf() for _ in range(3)]
            pti = [0]
            otile = [A("otile", [128, 4, 256], BF16) for _ in range(2)]; otile_b = [Buf(), Buf()]
            den = A("den", [128, 4], F32); den_b = Buf()

            k.dma("pool", wf[:, :, :], win[:, :, 1288:2060], writes=[wf_b])
            k.dma("sp", fbn[64:68, :], W["fox_f_b"][l].rearrange("(o d) -> o d", o=1).to_broadcast([4, 4]), writes=[fbn_b])
            k.op("dve", lambda e: e.tensor_scalar(out=fbn[64:68, :], in0=fbn[64:68, :], scalar1=-1.0, scalar2=None, op0=ALU.mult),
                 reads=[fbn_b], writes=[fbn_b])
            for h in range(4):
                k.op("dve", lambda e: e.tensor_copy(out=wqa[:, h, :, 0:64], in_=wf[:, :, h * 64:(h + 1) * 64]),
                     reads=[wf_b], writes=[wqa_b])
                k.op("dve", lambda e: e.tensor_copy(out=wqa[:, h, :, 64:68],
                                                    in_=wf[:, :, 768 + h:769 + h].to_broadcast([128, KC, 4])),
                     reads=[wf_b], writes=[wqa_b])
            k.op("dve", lambda e: e.memset(V[:, :, :, 64:66], 0.0), writes=V_b)
            k.op("dve", lambda e: e.memset(V[:, :, :, 64:65], 1.0), writes=V_b)
            k.op("dve", lambda e: e.memset(V[0:PADR, 0, :, 64:65], 0.0), writes=[V_b[0]])
            k.op("dve", lambda e: e.memset(ones4[64:68, :], 1.0), writes=[ones4_b])
            k.op("dve", lambda e: e.memset(cprev[64:68, :], 0.0), writes=[cprev_b])
            R4 = slice(64, 68)
            for ci_, tiles in enumerate(chunk_list(NT)):
                s0, nt = tiles[0] * 128, len(tiles)
                n = nt * 128
                hb = [hT_b[t] for t in tiles]
                for h in range(4):
                    pq, pq_b = PJ.get()
                    k.op("pe", lambda e: [e.matmul(pq[0:68, 0:n], lhsT=wqa[:, h, kc, :], rhs=hT[:, kc, s0:s0 + n],
                                                   start=(kc == 0), stop=(kc == KC - 1)) for kc in range(KC)][-1],
                         reads=[wqa_b] + hb, writes=[pq_b])
                    k.op("act", lambda e: e.activation(out=QT[h][0:64, 0:n], in_=pq[0:64, 0:n], func=AF.Identity),
                         reads=[pq_b], writes=[QT_b[h]])
                    k.op("act", lambda e: e.activation(out=ef[R4, 0:n], in_=pq[R4, 0:n], func=AF.Exp, scale=-1.0,
                                                       bias=fbn[R4, h:h + 1]), reads=[pq_b, fbn_b], writes=[ef_b])
                    k.op("act", lambda e: e.activation(out=ef[R4, 0:n], in_=ef[R4, 0:n], func=AF.Ln, bias=1.0),
                         reads=[ef_b], writes=[ef_b])
                    k.op("dve", lambda e: e.tensor_tensor_scan(out=c4[R4, 0:n], data0=ones4[R4, 0:n], data1=ef[R4, 0:n],
                                                               initial=cprev[R4, h:h + 1], op0=ALU.mult, op1=ALU.subtract),
                         reads=[ones4_b, ef_b, cprev_b], writes=[c4_b])
                    k.op("dve", lambda e: e.tensor_copy(out=cprev[R4, h:h + 1], in_=c4[R4, n - 1:n]), reads=[c4_b], writes=[cprev_b])
                    k.op("dve", lambda e: e.tensor_copy(out=chi[R4, 0:n], in_=c4[R4, 0:n]), reads=[c4_b], writes=[chi_b])
                    k.op("dve", lambda e: e.tensor_tensor(out=clo[R4, 0:n], in0=c4[R4, 0:n], in1=chi[R4, 0:n], op=ALU.subtract),
                         reads=[c4_b, chi_b], writes=[clo_b])
                    kts = [KT_b[h][t] for t in tiles]
                    k.op("dve", lambda e: e.tensor_scalar(out=c4[R4, 0:n], in0=chi[R4, 0:n], scalar1=aug[R4, 0:1], scalar2=aug[R4, 2:3],
                                                          op0=ALU.mult, op1=ALU.add), reads=[chi_b, cb, c4_b], writes=[c4_b])
                    k.op("dve", lambda e: e.scalar_tensor_tensor(out=KT[h][R4, s0:s0 + n], in0=clo[R4, 0:n], scalar=aug[R4, 1:2],
                                                                 in1=c4[R4, 0:n], op0=ALU.mult, op1=ALU.add),
                         reads=[clo_b, c4_b, cb], writes=kts)
                    k.op("dve", lambda e: e.tensor_scalar(out=c4[R4, 0:n], in0=chi[R4, 0:n], scalar1=aug[R4, 3:4], scalar2=aug[R4, 5:6],
                                                          op0=ALU.mult, op1=ALU.add), reads=[chi_b, cb, c4_b], writes=[c4_b])
                    k.op("dve", lambda e: e.scalar_tensor_tensor(out=QT[h][R4, 0:n], in0=clo[R4, 0:n], scalar=aug[R4, 4:5],
                                                                 in1=c4[R4, 0:n], op0=ALU.mult, op1=ALU.add),
                         reads=[clo_b, c4_b, cb], writes=[QT_b[h]])
                    pk, pk_b = PJ.get()
                    k.op("pe", lambda e: [e.matmul(pk[0:64, 0:n], lhsT=wf[:, kc, 256 + h * 64:256 + (h + 1) * 64],
                                                   rhs=hT[:, kc, s0:s0 + n], start=(kc == 0), stop=(kc == KC - 1))
                                          for kc in range(KC)][-1], reads=[wf_b] + hb, writes=[pk_b])
                    k.op("act", lambda e: e.activation(out=KT[h][0:64, s0:s0 + n], in_=pk[0:64, 0:n], func=AF.Identity),
                         reads=[pk_b], writes=kts)
                for j, t in enumerate(tiles):
                    pv, pv_b = PJ.get()
                    k.op("pe", lambda e: [e.matmul(pv[:, 0:256], lhsT=hT[:, kc, t * 128:(t + 1) * 128], rhs=wf[:, kc, 512:768],
                                                   start=(kc == 0), stop=(kc == KC - 1)) for kc in range(KC)][-1],
                         reads=[wf_b, hT_b[t]], writes=[pv_b])
                    k.op("act", lambda e: e.activation(out=V[:, t, :, 0:64], in_=pv[:, 0:256].rearrange("p (h c) -> p h c", h=4),
                                                       func=AF.Identity), reads=[pv_b], writes=[V_b[t]])
                ot, ot_b = otile[ci_ % 2], otile_b[ci_ % 2]
                for h in range(4):
                    po, po_b = attention(KT[h], KT_b[h], QT[h], QT_b[h], V, V_b, 68, 0.125, tiles, h, ot, ot_b, PT, PT_b, pti, STP, OP)
                    attn_norm(po, po_b, nt, h, ot, ot_b, den, den_b)
                for j, t in enumerate(tiles):
                    r0 = PADR if t == 0 else 0
                    k.dma("sp", mixd[t, r0:128, 512:768], ot[r0:128, j, :], reads=[ot_b], writes=[mixd_b[t][1]])
        k.barrier()

    def phase_mla(l):
        win = W["w_in"][l].rearrange("(kc p) n -> p kc n", p=128)
        PJ = PsPool(banks[0:2])
        STP = PsPool(banks[2:5])
        OP = PsPool(banks[5:7])
        b_x = banks[7]
        scale = float(96 ** -0.5)
        with ExitStack() as es:
            A = lambda nm, shp, dt: es.enter_context(nc.sbuf_tensor(un(nm), shp, dt))
            wm = A("wm", [128, KC, 416], BF16); wm_b = Buf()
            wuq = A("wuq", [128, 2, 384], BF16); wuqs = A("wuqs", [128, 2, 4, 96], BF16)
            wukv = A("wukv", [128, 512], BF16)
            wv = A("wv", [128, 256], BF16)
            wkr = A("wkr", [128, 2, KC, 96], BF16)
            qg = A("qg", [128, 2], F32); kvg = A("kvg", [128, 1], F32)
            wp_b = Buf()
            KT = [A("KTm", [128, S], BF16) for _ in range(4)]
            KT_b = [[Buf() for _ in range(NT)] for _ in range(4)]
            V = A("Vm", [128, NT, 4, 66], BF16); V_b = [Buf() for _ in range(NT)]
            QT = [A("QTm", [128, 512], BF16) for _ in range(4)]; QT_b = [Buf() for _ in range(4)]
            cqn = A("cqn", [128, 2, 512], BF16); cqn_b = Buf()
            ckvn = A("ckvn", [128, 512], BF16); ckvn_b = Buf()
            sqq = [A("sqq", [128, 512], BF16) for _ in range(2)]; sqq_b = [Buf(), Buf()]
            sqk = A("sqk", [128, 512], BF16); sqk_b = Buf()
            rq = A("rq", [128, 512], F32); rq_b = Buf()
            rkv = A("rkv", [128, 512], F32); rkv_b = Buf()
            rv = A("rv", [128, 4], F32); rv_b = Buf()
            cs = A("cs", [128, 2, 512], F32); cs_b = Buf()
            csr = A("csr", [128, 2, 512], F32); csr_b = Buf()
            t1 = A("t1", [128, 512], F32); t1_b = Buf()
            t2 = A("t2", [128, 512], F32); t2_b = Buf()
            PT = [A("PT", [128, 512], BF16) for _ in range(3)]; PT_b = [Buf() for _ in range(3)]
            pti = [0]
            otile = [A("otile", [128, 4, 256], BF16) for _ in range(2)]; otile_b = [Buf(), Buf()]
            den = A("den", [128, 4], F32); den_b = Buf()

            k.dma("pool", wm[:, :, :], win[:, :, 2060:2476], writes=[wm_b])
            k.dma("pool", wuq[:, :, :], W["mla_w_uq"][l].rearrange("(j p) n -> p j n", p=128), writes=[wp_b])
            k.dma("pool", wukv[:, :], W["mla_w_ukv"][l], writes=[wp_b])
            k.dma("sp", qg[:, :], W["mla_q_norm_g"][l].rearrange("(j p) -> p j", p=128), writes=[wp_b],
                  allow_slow_non_contiguous=True)
            k.dma("sp", kvg[:, :], W["mla_kv_norm_g"][l].rearrange("(p o) -> p o", o=1), writes=[wp_b])
            wuq4 = wuq[:, :, :].rearrange("p j (h c) -> p j h c", h=4)
            k.op("dve", lambda e: e.tensor_copy(out=wuqs[:, :, :, :], in_=wuq4), reads=[wp_b], writes=[wp_b])
            k.op("dve", lambda e: e.tensor_copy(out=wuqs[:, :, :, 64:80], in_=wuq4[:, :, :, 80:96]), reads=[wp_b], writes=[wp_b])
            k.op("dve", lambda e: e.tensor_copy(out=wuqs[:, :, :, 80:96], in_=wuq4[:, :, :, 64:80]), reads=[wp_b], writes=[wp_b])
            k.op("dve", lambda e: e.tensor_copy(out=wv[:, :].rearrange("p (h c) -> p h c", h=4),
                                                in_=wukv[:, :].rearrange("p (h c) -> p h c", h=4)[:, :, 64:128]),
                 reads=[wp_b], writes=[wp_b])
            k.op("dve", lambda e: e.memset(wkr[:, :, :, :], 0.0), writes=[wp_b])
            k.op("dve", lambda e: e.tensor_copy(out=wkr[:, 0, :, 64:96], in_=wm[:, :, 384:416]), reads=[wm_b, wp_b], writes=[wp_b])
            k.op("dve", lambda e: e.tensor_copy(out=wkr[:, 1, :, 64:80], in_=wm[:, :, 400:416]), reads=[wm_b, wp_b], writes=[wp_b])
            k.op("dve", lambda e: e.tensor_copy(out=wkr[:, 1, :, 80:96], in_=wm[:, :, 384:400]), reads=[wm_b, wp_b], writes=[wp_b])
            k.op("dve", lambda e: e.memset(V[:, :, :, 64:66], 0.0), writes=V_b)
            k.op("dve", lambda e: e.memset(V[:, :, :, 64:65], 1.0), writes=V_b)
            k.op("dve", lambda e: e.memset(V[0:PADR, 0, :, 64:65], 0.0), writes=[V_b[0]])
            RR = slice(64, 96)
            for ci_, tiles in enumerate(chunk_list(NT)):
                s0, nt = tiles[0] * 128, len(tiles)
                n = nt * 128
                hb = [hT_b[t] for t in tiles]
                k.dma("sp", cs[RR, 0, 0:n], c_cos[:, s0:s0 + n], writes=[cs_b])
                k.dma("sp", cs[RR, 1, 0:n], c_sin[:, s0:s0 + n], writes=[cs_b])
                for j2 in range(2):
                    pc, pc_b = PJ.get()
                    k.op("pe", lambda e: [e.matmul(pc[:, 0:n], lhsT=wm[:, kc, j2 * 128:(j2 + 1) * 128], rhs=hT[:, kc, s0:s0 + n],
                                                   start=(kc == 0), stop=(kc == KC - 1)) for kc in range(KC)][-1],
                         reads=[wm_b] + hb, writes=[pc_b])
                    k.op("act", lambda e: e.activation(out=cqn[:, j2, 0:n], in_=pc[:, 0:n], func=AF.Identity, scale=qg[:, j2:j2 + 1]),
                         reads=[pc_b, wp_b], writes=[cqn_b])
                    k.op("act", lambda e: e.activation(out=sqq[j2][:, 0:n], in_=pc[:, 0:n], func=AF.Square),
                         reads=[pc_b], writes=[sqq_b[j2]])
                px, px_b = b_x
                k.op("pe", lambda e: [e.matmul(px[:, 0:n], lhsT=ones_bf[:, :], rhs=sqq[0][:, 0:n], start=True, stop=False),
                                      e.matmul(px[:, 0:n], lhsT=ones_bf[:, :], rhs=sqq[1][:, 0:n], start=False, stop=True)][-1],
                     reads=sqq_b + [cb], writes=[px_b])
                k.op("dve", lambda e: e.tensor_scalar(out=rq[:, 0:n], in0=px[:, 0:n], scalar1=1.0 / 256, scalar2=EPS,
                                                      op0=ALU.mult, op1=ALU.add), reads=[px_b], writes=[rq_b])
                k.op("act", lambda e: e.activation(out=rq[:, 0:n], in_=rq[:, 0:n], func=AF.Ln), reads=[rq_b], writes=[rq_b])
                k.op("act", lambda e: e.activation(out=rq[:, 0:n], in_=rq[:, 0:n], func=AF.Exp, scale=-0.5), reads=[rq_b], writes=[rq_b])
                pc, pc_b = PJ.get()
                k.op("pe", lambda e: [e.matmul(pc[:, 0:n], lhsT=wm[:, kc, 256:384], rhs=hT[:, kc, s0:s0 + n],
                                               start=(kc == 0), stop=(kc == KC - 1)) for kc in range(KC)][-1],
                     reads=[wm_b] + hb, writes=[pc_b])
                k.op("act", lambda e: e.activation(out=ckvn[:, 0:n], in_=pc[:, 0:n], func=AF.Identity, scale=kvg[:, 0:1]),
                     reads=[pc_b, wp_b], writes=[ckvn_b])
                k.op("act", lambda e: e.activation(out=sqk[:, 0:n], in_=pc[:, 0:n], func=AF.Square), reads=[pc_b], writes=[sqk_b])
                px, px_b = b_x
                k.op("pe", lambda e: e.matmul(px[:, 0:n], lhsT=ones_bf[:, :], rhs=sqk[:, 0:n], start=True, stop=True),
                     reads=[sqk_b, cb], writes=[px_b])
                k.op("dve", lambda e: e.tensor_scalar(out=rkv[:, 0:n], in0=px[:, 0:n], scalar1=1.0 / 128, scalar2=EPS,
                                                      op0=ALU.mult, op1=ALU.add), reads=[px_b], writes=[rkv_b])
                k.op("act", lambda e: e.activation(out=rkv[:, 0:n], in_=rkv[:, 0:n], func=AF.Ln), reads=[rkv_b], writes=[rkv_b])
                k.op("act", lambda e: e.activation(out=rkv[:, 0:n], in_=rkv[:, 0:n], func=AF.Exp, scale=-0.5), reads=[rkv_b], writes=[rkv_b])
                px, px_b = b_x
                k.op("pe", lambda e: [e.matmul(px[:, 2 * j:2 * j + 2], lhsT=sqk[:, j * 128:(j + 1) * 128], rhs=ones_bf[:, 0:2],
                                               start=True, stop=True) for j in range(nt)][-1],
                     reads=[sqk_b, cb], writes=[px_b])
                k.op("dve", lambda e: e.tensor_scalar(out=rv[:, 0:nt], in0=px[:, 0:2 * nt].rearrange("p (j c) -> p j c", c=2)[:, :, 0],
                                                      scalar1=1.0 / 128, scalar2=EPS,
                                                      op0=ALU.mult, op1=ALU.add), reads=[px_b], writes=[rv_b])
                k.op("pool", lambda e: e.tensor_tensor(out=rv[:, 0:nt], in0=rv[:, 0:nt], in1=negh[:, 0:nt], op=ALU.pow),
                     reads=[rv_b, cb], writes=[rv_b])
                for i2 in range(2):
                    k.op("dve", lambda e: e.tensor_tensor(out=csr[RR, i2, 0:n], in0=cs[RR, i2, 0:n], in1=rq[RR, 0:n], op=ALU.mult),
                         reads=[cs_b, rq_b], writes=[csr_b])
                for h in range(4):
                    pq1, pq1_b = PJ.get()
                    pq2, pq2_b = PJ.get()
                    k.op("pe", lambda e: [e.matmul(pq1[0:96, 0:n], lhsT=wuq[:, j2, h * 96:(h + 1) * 96], rhs=cqn[:, j2, 0:n],
                                                   start=(j2 == 0), stop=(j2 == 1)) for j2 in range(2)][-1],
                         reads=[wp_b, cqn_b], writes=[pq1_b])
                    k.op("pe", lambda e: [e.matmul(pq2[0:96, 0:n], lhsT=wuqs[:, j2, h, :], rhs=cqn[:, j2, 0:n],
                                                   start=(j2 == 0), stop=(j2 == 1)) for j2 in range(2)][-1],
                         reads=[wp_b, cqn_b], writes=[pq2_b])
                    k.op("dve", lambda e: e.tensor_tensor(out=QT[h][0:64, 0:n], in0=pq1[0:64, 0:n], in1=rq[0:64, 0:n], op=ALU.mult),
                         reads=[pq1_b, rq_b], writes=[QT_b[h]])
                    k.op("dve", lambda e: e.tensor_tensor(out=t1[RR, 0:n], in0=pq1[RR, 0:n], in1=csr[RR, 0, 0:n], op=ALU.mult),
                         reads=[pq1_b, csr_b], writes=[t1_b])
                    k.op("dve", lambda e: e.tensor_tensor(out=t2[RR, 0:n], in0=pq2[RR, 0:n], in1=csr[RR, 1, 0:n], op=ALU.mult),
                         reads=[pq2_b, csr_b], writes=[t2_b])
                    k.op("dve", lambda e: e.tensor_tensor(out=QT[h][RR, 0:n], in0=t1[RR, 0:n], in1=t2[RR, 0:n], op=ALU.add),
                         reads=[t1_b, t2_b], writes=[QT_b[h]])
                pk1, pk1_b = PJ.get()
                pk2, pk2_b = PJ.get()
                for i2, (pk, pk_b) in enumerate([(pk1, pk1_b), (pk2, pk2_b)]):
                    k.op("pe", lambda e: [e.matmul(pk[0:96, 0:n], lhsT=wkr[:, i2, kc, :], rhs=hT[:, kc, s0:s0 + n],
                                                   start=(kc == 0), stop=(kc == KC - 1)) for kc in range(KC)][-1],
                         reads=[wp_b] + hb, writes=[pk_b])
                k.op("dve", lambda e: e.tensor_tensor(out=t1[RR, 0:n], in0=pk1[RR, 0:n], in1=cs[RR, 0, 0:n], op=ALU.mult),
                     reads=[pk1_b, cs_b], writes=[t1_b])
                k.op("dve", lambda e: e.tensor_tensor(out=t2[RR, 0:n], in0=pk2[RR, 0:n], in1=cs[RR, 1, 0:n], op=ALU.mult),
                     reads=[pk2_b, cs_b], writes=[t2_b])
                for h in range(4):
                    kts = [KT_b[h][t] for t in tiles]
                    k.op("dve", lambda e: e.tensor_tensor(out=KT[h][RR, s0:s0 + n], in0=t1[RR, 0:n], in1=t2[RR, 0:n], op=ALU.add),
                         reads=[t1_b, t2_b], writes=kts)
                    pkn, pkn_b = PJ.get()
                    k.op("pe", lambda e: e.matmul(pkn[0:64, 0:n], lhsT=wukv[:, h * 128:h * 128 + 64], rhs=ckvn[:, 0:n],
                                                  start=True, stop=True), reads=[wp_b, ckvn_b], writes=[pkn_b])
                    k.op("dve", lambda e: e.tensor_tensor(out=KT[h][0:64, s0:s0 + n], in0=pkn[0:64, 0:n], in1=rkv[0:64, 0:n], op=ALU.mult),
                         reads=[pkn_b, rkv_b], writes=kts)
                for j, t in enumerate(tiles):
                    pv, pv_b = PJ.get()
                    k.op("pe", lambda e: e.matmul(pv[:, 0:256], lhsT=ckvn[:, j * 128:(j + 1) * 128],
                                                  rhs=wv[:, :],
                                                  start=True, stop=True), reads=[wp_b, ckvn_b], writes=[pv_b])
                    k.op("act", lambda e: e.activation(out=V[:, t, :, 0:64], in_=pv[:, 0:256].rearrange("p (h c) -> p h c", h=4),
                                                       func=AF.Identity, scale=rv[:, j:j + 1]), reads=[pv_b, rv_b], writes=[V_b[t]])
                ot, ot_b = otile[ci_ % 2], otile_b[ci_ % 2]
                for h in range(4):
                    po, po_b = attention(KT[h], KT_b[h], QT[h], QT_b[h], V, V_b, 96, scale, tiles, h, ot, ot_b, PT, PT_b, pti, STP, OP)
                    attn_norm(po, po_b, nt, h, ot, ot_b, den, den_b)
                for j, t in enumerate(tiles):
                    r0 = PADR if t == 0 else 0
                    k.dma("sp", mixd[t, r0:128, 768:1024], ot[r0:128, j, :], reads=[ot_b], writes=[mixd_b[t][2]])
        k.barrier()

    def phase_out(l, ci):
        PG = PsPool(banks[0:6])
        with ExitStack() as es:
            A = lambda nm, shp, dt: es.enter_context(nc.sbuf_tensor(un(nm), shp, dt))
            wout = A("wout", [128, KC, D], BF16); wout_b = Buf()
            mt = [A("mt", [128, D], BF16) for _ in range(2)]; mt_b = [Buf(), Buf()]
            mixT = [A("mixT", [128, KC, 128], BF16) for _ in range(2)]; mixT_b = [Buf(), Buf()]
            gb = A("gb", [128, 2, D], F32); gb_b = Buf()
            lnb = make_ln_bufs(A)
            k.dma("pool", wout[:, :, :], W["w_out"][l].rearrange("(kc p) n -> p kc n", p=128), writes=[wout_b])
            k.dma("sp", gb[:, 0, :], W["ln2_g"][l].rearrange("(o d) -> o d", o=1).to_broadcast([128, D]), writes=[gb_b])
            k.dma("sp", gb[:, 1, :], W["ln2_b"][l].rearrange("(o d) -> o d", o=1).to_broadcast([128, D]), writes=[gb_b])

            def load_m(t):
                p = t % 2
                if t == 0:
                    k.op("dve", lambda e: e.memset(mt[p][:, :], 0.0), writes=[mt_b[p]])
                r0 = PADR if t == 0 else 0
                k.dma("sp", mt[p][r0:128, :], mixd[t, r0:128, :], reads=mixd_b[t], writes=[mt_b[p]])
                load_h(t, lnb[p])

            load_m(0)
            for t in range(NT):
                p = t % 2
                if t + 1 < NT:
                    load_m(t + 1)
                tb, tb_b = TP.get()
                tbv = tb[:, :].bitcast(BF16)
                k.op("pe", lambda e: [e.transpose(out=tbv[:, kc * 128:(kc + 1) * 128], in_=mt[p][:, kc * 128:(kc + 1) * 128],
                                                  identity=ident_bf[:, :]) for kc in range(KC)][-1],
                     reads=[mt_b[p], cb], writes=[tb_b])
                k.op("act", lambda e: e.activation(out=mixT[p][:, :, :], in_=tbv.rearrange("p (kc c) -> p kc c", kc=KC),
                                                   func=AF.Identity), reads=[tb_b], writes=[mixT_b[p]])
                ys = []
                for hf in range(2):
                    py, py_b = PG.get()
                    k.op("pe", lambda e: [e.matmul(py[:, :], lhsT=mixT[p][:, kc, :], rhs=wout[:, kc, hf * 512:(hf + 1) * 512],
                                                   start=(kc == 0), stop=(kc == KC - 1)) for kc in range(KC)][-1],
                         reads=[mixT_b[p], wout_b], writes=[py_b])
                    ys.append((py, py_b))
                ln_tile(t, ys, 1.0, gb, gb_b, ci, lnb[p], False)
        k.barrier()

    def run():
        phase_init()
        if only is not None:
            for ph in only:
                try:
                    dict(ssd=phase_ssd, fox=phase_fox, mla=phase_mla)[ph](0)
                except _StopPhase:
                    k.barrier()
            if not _cpstop:
                for ph in only:
                    dump_mix("mix_0", dict(ssd=(0, 512), fox=(512, 768), mla=(768, 1024))[ph])
            return
        for l in range(depth):
            last = (l == depth - 1)
            phase_ffn(l, 1, l * 6 + 0, False)
            dump_h("h1_%d" % l)
            if stop_after == "h1_%d" % l:
                return
            phase_ssd(l)
            phase_fox(l)
            phase_mla(l)
            dump_mix("mix_%d" % l)
            if stop_after == "mix_%d" % l:
                return
            phase_out(l, l * 6 + 2)
            dump_h("h2_%d" % l)
            if stop_after == "h2_%d" % l:
                return
            phase_ffn(l, 2, l * 6 + 4, last)
            dump_h("h3_%d" % l)
            if stop_after == "h3_%d" % l:
                return

    run()
    k.barrier()
    k.finish(out_b + dbg_b)
    build.stats = dict(nins=k.nins, nwait=k.nwait)
    return nc


def host_consts(NT):
    S = NT * 128
    bf = ml_dtypes.bfloat16
    idx = np.arange(128)
    c = {}
    c["c_ident_bf"] = np.eye(128, dtype=np.float32).astype(bf)
    c["c_ident_f"] = np.eye(128, dtype=np.float32)
    c["c_tri"] = (idx[:, None] <= idx[None, :]).astype(np.float32)
    mneg = np.where(idx[:, None] > idx[None, :], -30000.0, 0.0).astype(np.float32)
    c["c_maskneg"] = mneg.astype(bf)
    c["c_maskrep"] = np.tile(mneg, (1, 4)).astype(bf)
    pos = (np.arange(S) - PADR).astype(np.float32)
    inv_freq = (1.0 / (np.float32(10000.0) ** (np.arange(0, 32, 2, dtype=np.float32) / np.float32(32)))).astype(np.float32)
    ang = pos[None, :] * inv_freq[:, None]
    cos = np.cos(ang).astype(np.float32)
    sin = np.sin(ang).astype(np.float32)
    c["c_cos"] = np.concatenate([cos, cos], 0)
    c["c_sin"] = np.concatenate([-sin, sin], 0)
    aug = np.zeros((4, 6), np.float32)
    aug[0, 0] = -8.0
    aug[1, 1] = -8.0
    aug[2, 2] = 1.0
    aug[3, 2] = 1.0
    aug[2, 3] = 8.0
    aug[3, 4] = 8.0
    aug[0, 5] = 1.0
    aug[1, 5] = 1.0
    c["c_aug"] = aug
    return c


_CACHE = {}


def kernel(**inputs):
    x = np.ascontiguousarray(inputs["x"], dtype=np.float32)
    B, SEQ, _ = x.shape
    NT = SEQ // 128 + 1
    key = (NT,)
    if key not in _CACHE:
        _CACHE[key] = build(NT)
    nc = _CACHE[key]
    consts = host_consts(NT)
    shared = {name: np.ascontiguousarray(inputs[name], dtype=np.float32) for name, _ in PARAM_SHAPES}
    shared["meta"] = np.ascontiguousarray(inputs["meta"], dtype=np.float32)
    shared.update(consts)
    in_maps = []
    for b in range(B):
        m = dict(shared)
        m["x"] = x[b]
        in_maps.append(m)
    res = run_bass_kernel_spmd(nc, in_maps, core_ids=list(range(B)))
    out = np.stack([np.asarray(res.results[b]["out"], dtype=np.float32) for b in range(B)], 0)
    return out
```

```python
import numpy as np
import ml_dtypes
from contextlib import ExitStack
import concourse.bass as bass
import concourse.mybir as mybir
from concourse.bass_utils import run_bass_kernel_spmd

F32 = mybir.dt.float32
BF16 = mybir.dt.bfloat16
AF = mybir.ActivationFunctionType
ALU = mybir.AluOpType

D = 1024
F = 2816
NFC = 22
KC = 8
N_IN = 2476
DEPTH = 2
ALPHA = float((2 * DEPTH) ** 0.25)
EPS = 1e-5
PADR = 112


class Buf:
    __slots__ = ("w", "r", "name")

    def __init__(self, name=""):
        self.w = {}
        self.r = {}
        self.name = name


class KB:
    NDMA = 40

    def __init__(self, nc):
        self.nc = nc
        self.eng = dict(pe=nc.tensor, act=nc.scalar, dve=nc.vector, pool=nc.gpsimd, sp=nc.sync)
        self.sem = {k: nc.alloc_semaphore("s_" + k) for k in self.eng}
        self.cnt = {k: 0 for k in self.eng}
        self.seen = {k: {} for k in self.eng}
        self.dsem = [nc.alloc_semaphore("d%d" % i) for i in range(self.NDMA)]
        self.dval = [0] * self.NDMA
        self.dq = dict(sp=list(range(0, 24)), pool=list(range(24, self.NDMA)))
        self.dnext = dict(sp=0, pool=0)
        self.nwait = 0
        self.nins = 0

    def _wait(self, e, s, v):
        sid = id(s)
        if self.seen[e].get(sid, 0) < v:
            self.eng[e].wait_ge(s, v)
            self.seen[e][sid] = v
            self.nwait += 1

    def _deps(self, e, reads, writes):
        need = {}
        for b in reads:
            for sid, (s, v) in b.w.items():
                if need.get(sid, (None, 0))[1] < v:
                    need[sid] = (s, v)
        for b in writes:
            for d in (b.w, b.r):
                for sid, (s, v) in d.items():
                    if need.get(sid, (None, 0))[1] < v:
                        need[sid] = (s, v)
        if e == "pe":
            need.pop(id(self.sem["pe"]), None)
        for sid, (s, v) in need.items():
            self._wait(e, s, v)

    def _mark(self, s, v, reads, writes):
        sid = id(s)
        for b in reads:
            b.r[sid] = (s, v)
        for b in writes:
            b.w[sid] = (s, v)

    def op(self, e, fn, reads=(), writes=()):
        self._deps(e, reads, writes)
        ins = fn(self.eng[e])
        self.cnt[e] += 1
        ins.then_inc(self.sem[e], 1)
        self._mark(self.sem[e], self.cnt[e], reads, writes)
        self.nins += 1
        return ins

    def dma(self, q, out, in_, reads=(), writes=(), **kw):
        self._deps(q, reads, writes)
        lst = self.dq[q]
        i = lst[self.dnext[q] % len(lst)]
        self.dnext[q] += 1
        s = self.dsem[i]
        self._wait(q, s, self.dval[i])
        ins = self.eng[q].dma_start(out=out, in_=in_, **kw)
        self.dval[i] += 16
        ins.then_inc(s, 16)
        self._mark(s, self.dval[i], reads, writes)
        self.nins += 1
        return ins

    def barrier(self):
        for e in self.eng:
            for kk in self.eng:
                if kk != e and self.cnt[kk] > 0:
                    self._wait(e, self.sem[kk], self.cnt[kk])
            for i in range(self.NDMA):
                if self.dval[i] > 0:
                    self._wait(e, self.dsem[i], self.dval[i])

    def finish(self, bufs):
        for b in bufs:
            for sid, (s, v) in b.w.items():
                self._wait("sp", s, v)


class PsPool:
    def __init__(self, banks):
        self.banks = banks
        self.i = 0

    def get(self):
        b = self.banks[self.i % len(self.banks)]
        self.i += 1
        return b


def chunk_list(NT):
    out = [[0]]
    t = 1
    while t < NT:
        out.append(list(range(t, min(t + 4, NT))))
        t += 4
    return out


PARAM_SHAPES = [
    ("ffn1_w_gate", [D, F]), ("ffn1_w_up", [D, F]), ("ffn1_w_down", [F, D]),
    ("ln1_g", [D]), ("ln1_b", [D]), ("w_in", [D, N_IN]), ("conv_w", [4, 768]), ("conv_b", [768]),
    ("dt_bias", [8]), ("a_log", [8]), ("d_skip", [8]), ("ssd_norm_g", [512]), ("fox_f_b", [4]),
    ("mla_q_norm_g", [256]), ("mla_w_uq", [256, 384]), ("mla_kv_norm_g", [128]), ("mla_w_ukv", [128, 512]),
    ("w_out", [D, D]), ("ln2_g", [D]), ("ln2_b", [D]),
    ("ffn2_w_gate", [D, F]), ("ffn2_w_up", [D, F]), ("ffn2_w_down", [F, D]),
    ("ln3_g", [D]), ("ln3_b", [D]),
]


def build(NT, depth=DEPTH, dbg=None, stop_after=None, only=None):
    S = NT * 128
    nc = bass.Bass("TRN2", target_bir_lowering=False)
    k = KB(nc)
    uid = [0]

    def un(name):
        uid[0] += 1
        return "%s_%d" % (name, uid[0])

    import os as _os
    _cpstop = int(_os.environ.get("SSD_STOP", "0"))

    class _StopPhase(Exception):
        pass

    def cp(n):
        if _cpstop and n == _cpstop:
            raise _StopPhase()

    def din(name, shape, dt=F32):
        return nc.dram_tensor(name, shape, dt, kind="ExternalInput").ap()

    x_in = din("x", [(NT - 1) * 128, D])
    meta_in = din("meta", [16, D])
    W = {name: din(name, [DEPTH] + shp) for name, shp in PARAM_SHAPES}
    c_ident_bf = din("c_ident_bf", [128, 128], BF16)
    c_ident_f = din("c_ident_f", [128, 128])
    c_tri = din("c_tri", [128, 128])
    c_maskneg = din("c_maskneg", [128, 128], BF16)
    c_maskrep = din("c_maskrep", [128, 512], BF16)
    c_cos = din("c_cos", [32, S])
    c_sin = din("c_sin", [32, S])
    c_aug = din("c_aug", [4, 6])
    out_d = nc.dram_tensor("out", [(NT - 1) * 128, D], F32, kind="ExternalOutput").ap()
    hres = nc.dram_tensor("hres", [NT, 128, D], F32).ap()
    mixd = nc.dram_tensor("mixd", [NT, 128, D], BF16).ap()
    hres_b = [Buf("hres%d" % t) for t in range(NT)]
    mixd_b = [[Buf() for _ in range(3)] for t in range(NT)]
    out_b = [Buf() for t in range(NT)]
    dbg_outs = {}
    if dbg:
        for name in dbg:
            if name.startswith("h"):
                dbg_outs[name] = nc.dram_tensor("dbg_" + name, [NT, 128, D], F32, kind="ExternalOutput").ap()
            else:
                dbg_outs[name] = nc.dram_tensor("dbg_" + name, [NT, 128, D], BF16, kind="ExternalOutput").ap()
    dbg_b = []

    PA = nc.alloc_sbuf_tensor
    hT = PA("hT", [128, KC, S], BF16)
    hT_b = [Buf("hT%d" % t) for t in range(NT)]
    ident_bf = PA("ident_bf", [128, 128], BF16)
    ident_f = PA("ident_f", [128, 128], F32)
    tri = PA("tri", [128, 128], F32)
    ones_f = PA("ones_f", [128, 128], F32)
    maskneg = PA("maskneg", [128, 128], BF16)
    maskrep = PA("maskrep", [128, 512], BF16)
    tri_bf = PA("tri_bf", [128, 128], BF16)
    ones_bf = PA("ones_bf", [128, 128], BF16)
    negh = PA("negh", [128, 512], F32)
    aug = PA("aug", [128, 6], F32)
    lncol = PA("lncol", [128, DEPTH * 6, KC], F32)
    cb = Buf("consts")

    banks = []
    for i in range(8):
        banks.append((nc.alloc_psum_tensor("bank%d" % i, [128, 512], F32), Buf("bank%d" % i)))

    k.dma("sp", ident_bf[:, :], c_ident_bf, writes=[cb])
    k.dma("sp", ident_f[:, :], c_ident_f, writes=[cb])
    k.dma("sp", tri[:, :], c_tri, writes=[cb])
    k.dma("sp", maskneg[:, :], c_maskneg, writes=[cb])
    k.dma("sp", maskrep[:, :], c_maskrep, writes=[cb])
    k.dma("sp", aug[64:68, :], c_aug, writes=[cb])
    k.op("dve", lambda e: e.memset(ones_f[:, :], 1.0), writes=[cb])
    k.op("dve", lambda e: e.memset(ones_bf[:, :], 1.0), writes=[cb])
    k.op("dve", lambda e: e.tensor_copy(out=tri_bf[:, :], in_=tri[:, :]), reads=[cb], writes=[cb])
    k.op("dve", lambda e: e.memset(negh[:, :], -0.5), writes=[cb])
    for l in range(depth):
        for i, nm in enumerate(["ln1_g", "ln1_b", "ln2_g", "ln2_b", "ln3_g", "ln3_b"]):
            k.dma("sp", lncol[:, l * 6 + i, :], W[nm][l].rearrange("(kc p) -> p kc", p=128), writes=[cb],
                  allow_slow_non_contiguous=True)
    k.op("dve", lambda e: e.memset(hT[:, :, 0:PADR], 0.0), writes=[hT_b[0]])

    tile_cols = lambda t: (t * 128, (t + 1) * 128)

    def make_ln_bufs(A):
        bufs = []
        for p in range(2):
            d = dict(hin=A("hin", [128, D], F32), yh=A("yh", [128, D], F32), xnb=A("xnb", [128, D], BF16),
                     st=A("st", [128, 12], F32), mv=A("mv", [128, 4], F32))
            d.update(hin_b=Buf(), yh_b=Buf(), xnb_b=Buf(), st_b=Buf(), mv_b=Buf())
            bufs.append(d)
        return bufs

    def load_h(t, lb):
        k.dma("sp", lb["hin"][:, :], hres[t], reads=[hres_b[t]], writes=[lb["hin_b"]])

    def ln_tile(t, ys, coef, gb, gb_b, ci, lb, final):
        hin, yh, xnb, st, mv = lb["hin"], lb["yh"], lb["xnb"], lb["st"], lb["mv"]
        hin_b, yh_b, xnb_b, st_b, mv_b = lb["hin_b"], lb["yh_b"], lb["xnb_b"], lb["st_b"], lb["mv_b"]
        for hf in range(2):
            k.op("act", lambda e: e.activation(out=yh[:, hf * 512:(hf + 1) * 512], in_=ys[hf][0][:, :],
                                               func=AF.Identity, scale=float(coef)),
                 reads=[ys[hf][1]], writes=[yh_b])
        k.op("dve", lambda e: e.scalar_tensor_tensor(out=yh[:, :], in0=hin[:, :], scalar=ALPHA, in1=yh[:, :],
                                                     op0=ALU.mult, op1=ALU.add), reads=[hin_b, yh_b], writes=[yh_b])
        for hf in range(2):
            k.op("dve", lambda e: e.bn_stats(out=st[:, hf * 6:(hf + 1) * 6], in_=yh[:, hf * 512:(hf + 1) * 512]),
                 reads=[yh_b], writes=[st_b])
        k.op("dve", lambda e: e.bn_aggr(out=mv[:, 0:2], in_=st[:, :]), reads=[st_b], writes=[mv_b])
        k.op("dve", lambda e: e.tensor_scalar(out=mv[:, 2:3], in0=mv[:, 1:2], scalar1=EPS, scalar2=None, op0=ALU.add),
             reads=[mv_b], writes=[mv_b])
        k.op("pool", lambda e: e.tensor_tensor(out=mv[:, 2:3], in0=mv[:, 2:3], in1=negh[:, 0:1], op=ALU.pow),
             reads=[mv_b, cb], writes=[mv_b])
        k.op("dve", lambda e: e.scalar_tensor_tensor(out=mv[:, 3:4], in0=mv[:, 0:1], scalar=-1.0, in1=mv[:, 2:3],
                                                     op0=ALU.mult, op1=ALU.mult), reads=[mv_b], writes=[mv_b])
        k.op("dve", lambda e: e.tensor_scalar(out=hin[:, :], in0=yh[:, :], scalar1=mv[:, 0:1], scalar2=mv[:, 2:3],
                                              op0=ALU.subtract, op1=ALU.mult), reads=[yh_b, mv_b], writes=[hin_b])
        k.op("act", lambda e: e.activation(out=xnb[:, :], in_=yh[:, :], func=AF.Identity, scale=mv[:, 2:3],
                                           bias=mv[:, 3:4]), reads=[yh_b, mv_b], writes=[xnb_b])
        k.op("dve", lambda e: e.tensor_tensor(out=yh[:, :], in0=hin[:, :], in1=gb[:, 0, :], op=ALU.mult),
             reads=[hin_b, gb_b], writes=[yh_b])
        k.op("pool", lambda e: e.tensor_tensor(out=yh[:, :], in0=yh[:, :], in1=gb[:, 1, :], op=ALU.add),
             reads=[yh_b, gb_b], writes=[yh_b])
        r0 = PADR if t == 0 else 0
        k.dma("sp", hres[t, r0:128, :], yh[r0:128, :], reads=[yh_b], writes=[hres_b[t]])
        if final and t > 0:
            k.dma("sp", out_d[(t - 1) * 128:t * 128, :], yh[:, :], reads=[yh_b], writes=[out_b[t]])
        tb, tb_b = TP.get()
        tbv = tb[:, :].bitcast(BF16)
        k.op("pe", lambda e: [e.transpose(out=tbv[:, kc * 128:(kc + 1) * 128], in_=xnb[:, kc * 128:(kc + 1) * 128],
                                          identity=ident_bf[:, :]) for kc in range(KC)][-1],
             reads=[xnb_b, cb], writes=[tb_b])
        c0 = PADR if t == 0 else 0
        for kc in range(KC):
            k.op("act", lambda e: e.activation(out=hT[:, kc, t * 128 + c0:(t + 1) * 128],
                                               in_=tbv[:, kc * 128 + c0:(kc + 1) * 128], func=AF.Identity,
                                               scale=lncol[:, ci, kc:kc + 1], bias=lncol[:, ci + 1, kc:kc + 1]),
                 reads=[tb_b, cb], writes=[hT_b[t]])

    def dump_h(name):
        if dbg and name in dbg_outs:
            k.barrier()
            b = Buf()
            k.dma("sp", dbg_outs[name], hres, reads=hres_b, writes=[b])
            dbg_b.append(b)
            k.barrier()

    def dump_mix(name, cols=(0, D)):
        if dbg and name in dbg_outs:
            k.barrier()
            b = Buf()
            for t in range(NT):
                r0 = PADR if t == 0 else 0
                k.dma("sp", dbg_outs[name][t, r0:128, cols[0]:cols[1]], mixd[t, r0:128, cols[0]:cols[1]], reads=mixd_b[t], writes=[b])
            dbg_b.append(b)
            k.barrier()

    TP = PsPool(banks[6:8])

    def phase_init():
        with ExitStack() as es:
            A = lambda nm, shp, dt: es.enter_context(nc.sbuf_tensor(un(nm), shp, dt))
            zt = A("zt", [128, D], F32)
            zt_b = Buf()
            hin = [A("hin0", [128, D], F32) for _ in range(2)]
            hb = [A("hb0", [128, D], BF16) for _ in range(2)]
            hin_b = [Buf(), Buf()]
            hb_b = [Buf(), Buf()]
            k.op("dve", lambda e: e.memset(zt[:, :], 0.0), writes=[zt_b])
            k.dma("sp", hres[0], zt[:, :], reads=[zt_b], writes=[hres_b[0]])
            k.dma("sp", hres[0, PADR:128, :], meta_in, writes=[hres_b[0]])
            for t in range(1, NT):
                k.dma("sp", hres[t], x_in[(t - 1) * 128:t * 128, :], writes=[hres_b[t]])
            for t in range(NT):
                p = t % 2
                k.dma("sp", hin[p][:, :], hres[t], reads=[hres_b[t]], writes=[hin_b[p]])
                k.op("act", lambda e: e.activation(out=hb[p][:, :], in_=hin[p][:, :], func=AF.Identity),
                     reads=[hin_b[p]], writes=[hb_b[p]])
                tb, tb_b = TP.get()
                tbv = tb[:, :].bitcast(BF16)
                k.op("pe", lambda e: [e.transpose(out=tbv[:, kc * 128:(kc + 1) * 128],
                                                  in_=hb[p][:, kc * 128:(kc + 1) * 128],
                                                  identity=ident_bf[:, :]) for kc in range(KC)][-1],
                     reads=[hb_b[p], cb], writes=[tb_b])
                c0 = PADR if t == 0 else 0
                k.op("dve", lambda e: e.tensor_copy(
                    out=hT[:, :, t * 128 + c0:(t + 1) * 128],
                    in_=tbv.rearrange("p (kc c) -> p kc c", kc=KC)[:, :, c0:128]),
                     reads=[tb_b], writes=[hT_b[t]])
        k.barrier()

    def phase_ffn(l, which, ci, final):
        wg = W["ffn%d_w_gate" % which][l].rearrange("(kc p) f -> p kc f", p=128)
        wu = W["ffn%d_w_up" % which][l].rearrange("(kc p) f -> p kc f", p=128)
        wdn = W["ffn%d_w_down" % which][l]
        lg = W["ln%d_g" % (1 if which == 1 else 3)][l]
        lb_ = W["ln%d_b" % (1 if which == 1 else 3)][l]
        PMAXT = 9
        passes = []
        t = 0
        while t < NT:
            passes.append(list(range(t, min(t + PMAXT, NT))))
            t += PMAXT
        if len(passes) > 1 and len(passes[-1]) < 4:
            allt = list(range(NT))
            h = (NT + 1) // 2
            passes = [allt[:h], allt[h:]]
        PG = PsPool(banks[0:6])
        with ExitStack() as es:
            A = lambda nm, shp, dt: es.enter_context(nc.sbuf_tensor(un(nm), shp, dt))
            actT = A("actT", [128, NFC, PMAXT * 128], BF16)
            actT_b = [Buf() for _ in range(PMAXT)]
            wd = A("wd", [128, NFC, D], BF16)
            wd_b = [Buf() for _ in range(NFC)]
            wgu = [A("wgu", [128, 2, KC, 256], BF16) for _ in range(2)]
            wgu_b = [Buf(), Buf()]
            stmp = [A("stmp", [128, 512], F32) for _ in range(2)]
            stmp_b = [Buf(), Buf()]
            gb = A("gb", [128, 2, D], F32)
            gb_b = Buf()
            lnb = make_ln_bufs(A)
            k.dma("sp", gb[:, 0, :], lg.rearrange("(o d) -> o d", o=1).to_broadcast([128, D]), writes=[gb_b])
            k.dma("sp", gb[:, 1, :], lb_.rearrange("(o d) -> o d", o=1).to_broadcast([128, D]), writes=[gb_b])
            si = 0
            for ptiles in passes:
                p0 = ptiles[0] * 128
                chunks = []
                tl = list(ptiles)
                if tl[0] == 0:
                    chunks.append((0, 128, [0]))
                    tl = tl[1:]
                while tl:
                    grp = tl[:4]
                    tl = tl[4:]
                    chunks.append((grp[0] * 128, len(grp) * 128, grp))
                for fcp in range(NFC // 2):
                    wb, wb_b = wgu[fcp % 2], wgu_b[fcp % 2]
                    k.dma("pool", wb[:, 0], wg[:, :, fcp * 256:(fcp + 1) * 256], writes=[wb_b])
                    k.dma("pool", wb[:, 1], wu[:, :, fcp * 256:(fcp + 1) * 256], writes=[wb_b])
                    for j in range(2):
                        fc = fcp * 2 + j
                        k.dma("pool", wd[:, fc, :], wdn[fc * 128:(fc + 1) * 128, :], writes=[wd_b[fc]])
                    for j in range(2):
                        fc = fcp * 2 + j
                        for (s0, n, tiles) in chunks:
                            pg, pg_b = PG.get()
                            pu, pu_b = PG.get()
                            hb = [hT_b[t] for t in tiles]
                            k.op("pe", lambda e: [e.matmul(pg[:, 0:n], lhsT=wb[:, 0, kc, j * 128:(j + 1) * 128],
                                                           rhs=hT[:, kc, s0:s0 + n], start=(kc == 0),
                                                           stop=(kc == KC - 1)) for kc in range(KC)][-1],
                                 reads=[wb_b] + hb, writes=[pg_b])
                            k.op("pe", lambda e: [e.matmul(pu[:, 0:n], lhsT=wb[:, 1, kc, j * 128:(j + 1) * 128],
                                                           rhs=hT[:, kc, s0:s0 + n], start=(kc == 0),
                                                           stop=(kc == KC - 1)) for kc in range(KC)][-1],
                                 reads=[wb_b] + hb, writes=[pu_b])
                            sp_, sp_b = stmp[si % 2], stmp_b[si % 2]
                            si += 1
                            k.op("act", lambda e: e.activation(out=sp_[:, 0:n], in_=pg[:, 0:n], func=AF.Silu),
                                 reads=[pg_b], writes=[sp_b])
                            k.op("dve", lambda e: e.tensor_tensor(out=actT[:, fc, s0 - p0:s0 - p0 + n], in0=sp_[:, 0:n],
                                                                  in1=pu[:, 0:n], op=ALU.mult),
                                 reads=[sp_b, pu_b], writes=[actT_b[t - ptiles[0]] for t in tiles])
                load_h(ptiles[0], lnb[ptiles[0] % 2])
                for ti, t in enumerate(ptiles):
                    if ti + 1 < len(ptiles):
                        load_h(ptiles[ti + 1], lnb[ptiles[ti + 1] % 2])
                    ys = []
                    for hf in range(2):
                        py, py_b = PG.get()
                        k.op("pe", lambda e: [e.matmul(py[:, :], lhsT=actT[:, fc, ti * 128:(ti + 1) * 128],
                                                       rhs=wd[:, fc, hf * 512:(hf + 1) * 512], start=(fc == 0),
                                                       stop=(fc == NFC - 1)) for fc in range(NFC)][-1],
                             reads=[actT_b[ti]] + wd_b, writes=[py_b])
                        ys.append((py, py_b))
                    ln_tile(t, ys, 0.5, gb, gb_b, ci, lnb[t % 2], final)
        k.barrier()

    def attention(KT, KT_b, QT, QT_b, V, V_b, Kd, scale, tiles, h, otile, otile_b, PT, PT_b, pti, STP, OP):
        first, nt, last = tiles[0], len(tiles), tiles[-1]
        n = nt * 128
        po, po_b = OP.get()
        for kt in range(last + 1):
            jk = kt - first
            q0 = 0 if kt < first else jk * 128
            ps_, ps_b = STP.get()
            kcols = slice(kt * 128, (kt + 1) * 128)
            if kt < first:
                k.op("pe", lambda e: e.matmul(ps_[:, 0:n], lhsT=KT[0:Kd, kcols], rhs=QT[0:Kd, 0:n], start=True, stop=True),
                     reads=[KT_b[kt], QT_b], writes=[ps_b])
            else:
                def f(e):
                    e.matmul(ps_[:, q0:q0 + 128], lhsT=KT[0:Kd, kcols], rhs=QT[0:Kd, q0:q0 + 128], start=True, stop=False)
                    r = e.matmul(ps_[:, q0:q0 + 128], lhsT=ident_bf[:, :], rhs=maskneg[:, :], start=False, stop=True)
                    if q0 + 128 < n:
                        r = e.matmul(ps_[:, q0 + 128:n], lhsT=KT[0:Kd, kcols], rhs=QT[0:Kd, q0 + 128:n], start=True, stop=True)
                    return r
                k.op("pe", f, reads=[KT_b[kt], QT_b, cb], writes=[ps_b])
            pt, pt_b = PT[pti[0] % len(PT)], PT_b[pti[0] % len(PT)]
            pti[0] += 1
            k.op("act", lambda e: e.activation(out=pt[:, q0:n], in_=ps_[:, q0:n], func=AF.Exp, scale=float(scale)),
                 reads=[ps_b], writes=[pt_b])
            j0 = max(0, jk)
            k.op("pe", lambda e: [e.matmul(po[:, j * 128:j * 128 + 66], lhsT=pt[:, j * 128:(j + 1) * 128],
                                           rhs=V[:, kt, h, 0:66], start=(kt == 0 and j == j0), stop=(kt == first + j),
                                           skip_group_check=True)
                                  for j in range(j0, nt)][-1],
                 reads=[pt_b, V_b[kt]], writes=[po_b])
        return po, po_b

    def attn_norm(po, po_b, nt, h, otile, otile_b, den, den_b):
        pov = po[:, :].rearrange("p (j c) -> p j c", c=128)
        k.op("dve", lambda e: e.tensor_scalar(out=den[:, 0:nt], in0=pov[:, 0:nt, 64], scalar1=1e-30, scalar2=None,
                                              op0=ALU.add), reads=[po_b], writes=[den_b])
        k.op("dve", lambda e: e.reciprocal(out=den[:, 0:nt], in_=den[:, 0:nt]), reads=[den_b], writes=[den_b])
        k.op("dve", lambda e: e.tensor_tensor(out=otile[:, 0:nt, h * 64:(h + 1) * 64], in0=pov[:, 0:nt, 0:64],
                                              in1=den[:, 0:nt].unsqueeze(2).to_broadcast([128, nt, 64]), op=ALU.mult),
             reads=[po_b, den_b], writes=[otile_b])

    def phase_ssd(l):
        win = W["w_in"][l].rearrange("(kc p) n -> p kc n", p=128)
        PJ = PsPool(banks[0:2])
        PD = PsPool(banks[2:4])
        b_yd, b_yo, b_st, b_sm = banks[4], banks[5], banks[6], banks[7]
        with ExitStack() as es:
            A = lambda nm, shp, dt: es.enter_context(nc.sbuf_tensor(un(nm), shp, dt))
            wss = A("wss", [128, KC, 1288], BF16); wss_b = Buf()
            cw = A("cw", [128, 6, 4], F32); cbias = A("cbias", [128, 6], F32)
            dtb = A("dtb", [128, 8], F32); Ab = A("Ab", [128, 8], F32); dsk = A("dsk", [128, 8], F32)
            ngb = A("ngb", [128, 512], F32)
            pb = Buf("ssd_params")
            xraw = A("xraw", [128, 6, 515], F32); xraw_b = [Buf() for _ in range(6)]
            acc = [A("acc", [128, 512], F32) for _ in range(2)]; acc_b = [Buf(), Buf()]
            xsT = [A("xsT", [128, 512], F32) for _ in range(2)]; xsT_b = [Buf(), Buf()]
            BT = A("BT", [128, 512], BF16); BT_b = Buf()
            CT = A("CT", [128, 512], BF16); CT_b = Buf()
            BTg = [A("BTg", [128, 512], BF16) for _ in range(2)]; BTg_b = Buf()
            CTg = [A("CTg", [128, 512], BF16) for _ in range(2)]; CTg_b = Buf()
            xs_tm = A("xs_tm", [128, 4, 512], F32); xs_tm_b = [Buf() for _ in range(4)]
            B_tm = A("B_tm", [128, 4, 128], BF16); B_tm_b = Buf()
            smP = [A("sm", [128, 64], F32) for _ in range(2)]; smP_b = [Buf(), Buf()]
            R = [A("R", [128, 1024], BF16) for _ in range(2)]; R_b = Buf()
            negA = [A("negA", [128, 1024], BF16) for _ in range(2)]; negA_b = Buf()
            smb = A("smb", [128, 16], BF16); smb_b = Buf()
            alo = A("alo", [128, 8], F32)
            E = A("E", [128, 1024], F32); E_b = Buf()
            MTP = [A("MT", [128, 1024], BF16) for _ in range(2)]; MTP_b = [Buf(), Buf()]
            xdtP = [A("xdt", [128, 512], BF16) for _ in range(2)]; xdtP_b = [Buf(), Buf()]
            xwP = [A("xw", [128, 512], BF16) for _ in range(2)]; xwP_b = [Buf(), Buf()]
            y1 = A("y1", [128, 512], F32); y1_b = Buf()
            y2 = A("y2", [128, 512], F32); y2_b = Buf()
            Sst = A("Sst", [128, 256], F32); Sst_b = Buf()
            Stmp = A("Stmp", [128, 256], F32); Stmp_b = Buf()
            Sbf = A("Sbf", [128, 256], BF16); Sbf_b = Buf()
            sz = A("sz", [128, 512], F32); sz_b = Buf()
            junk = A("junk", [128, 256], F32); junk_b = Buf()
            ss = A("ss", [128, 2], F32); ss_b = Buf()
            yo = [A("yo", [128, 512], BF16) for _ in range(2)]; yo_b = [Buf(), Buf()]

            k.dma("pool", wss[:, :, :], win[:, :, 0:1288], writes=[wss_b])
            for j in range(4):
                k.dma("sp", cw[:, :, j], W["conv_w"][l, j].rearrange("(cc p) -> p cc", p=128), writes=[pb],
                      allow_slow_non_contiguous=True)
            k.dma("sp", cbias[:, :], W["conv_b"][l].rearrange("(cc p) -> p cc", p=128), writes=[pb],
                  allow_slow_non_contiguous=True)
            bc = lambda ap, n_: ap.rearrange("(o d) -> o d", o=1).to_broadcast([128, n_])
            k.dma("sp", dtb[:, :], bc(W["dt_bias"][l], 8), writes=[pb])
            k.dma("sp", Ab[:, :], bc(W["a_log"][l], 8), writes=[pb])
            k.dma("sp", dsk[:, :], bc(W["d_skip"][l], 8), writes=[pb])
            k.dma("sp", ngb[:, :], bc(W["ssd_norm_g"][l], 512), writes=[pb])
            k.op("act", lambda e: e.activation(out=Ab[:, :], in_=Ab[:, :], func=AF.Exp), reads=[pb], writes=[pb])
            k.op("dve", lambda e: e.tensor_scalar(out=Ab[:, :], in0=Ab[:, :], scalar1=-1.0, scalar2=None, op0=ALU.mult),
                 reads=[pb], writes=[pb])
            k.op("dve", lambda e: e.memset(xraw[:, :, 0:3], 0.0), writes=xraw_b)
            for g in range(2):
                k.op("dve", lambda e: e.memset(BTg[g][:, :], 0.0), writes=[BTg_b])
                k.op("dve", lambda e: e.memset(CTg[g][:, :], 0.0), writes=[CTg_b])
            k.op("dve", lambda e: e.memset(Sst[:, :], 0.0), writes=[Sst_b])
            k.op("dve", lambda e: e.memset(Sbf[:, :], 0.0), writes=[Sbf_b])
            cp(1)

            xi = 0
            for tiles in chunk_list(NT):
                s0, nt = tiles[0] * 128, len(tiles)
                n = nt * 128
                hb = [hT_b[t] for t in tiles]
                for cc in range(6):
                    pj, pj_b = PJ.get()
                    k.op("pe", lambda e: [e.matmul(pj[:, 0:n], lhsT=wss[:, kc, 512 + cc * 128:512 + (cc + 1) * 128],
                                                   rhs=hT[:, kc, s0:s0 + n], start=(kc == 0), stop=(kc == KC - 1))
                                          for kc in range(KC)][-1], reads=[wss_b] + hb, writes=[pj_b])
                    k.op("act", lambda e: e.activation(out=xraw[:, cc, 3:3 + n], in_=pj[:, 0:n], func=AF.Identity),
                         reads=[pj_b], writes=[xraw_b[cc]])
                    ac, ac_b = acc[cc % 2], acc_b[cc % 2]
                    k.op("dve", lambda e: e.tensor_scalar(out=ac[:, 0:n], in0=xraw[:, cc, 3:3 + n], scalar1=cw[:, cc, 3:4],
                                                          scalar2=None, op0=ALU.mult), reads=[xraw_b[cc], pb], writes=[ac_b])
                    for j in (2, 1, 0):
                        k.op("dve", lambda e: e.scalar_tensor_tensor(out=ac[:, 0:n], in0=xraw[:, cc, j:j + n],
                                                                     scalar=cw[:, cc, j:j + 1], in1=ac[:, 0:n],
                                                                     op0=ALU.mult, op1=ALU.add),
                             reads=[xraw_b[cc], pb, ac_b], writes=[ac_b])
                    k.op("dve", lambda e: e.tensor_copy(out=xraw[:, cc, 0:3], in_=xraw[:, cc, n:n + 3]),
                         reads=[xraw_b[cc]], writes=[xraw_b[cc]])
                    if cc < 4:
                        xo, xo_b = xsT[xi % 2], xsT_b[xi % 2]
                        xi += 1
                    elif cc == 4:
                        xo, xo_b = BT, BT_b
                    else:
                        xo, xo_b = CT, CT_b
                    k.op("act", lambda e: e.activation(out=xo[:, 0:n], in_=ac[:, 0:n], func=AF.Silu, bias=cbias[:, cc:cc + 1]),
                         reads=[ac_b, pb], writes=[xo_b])
                    if tiles[0] == 0:
                        k.op("dve", lambda e: e.memset(xo[:, 0:PADR], 0.0), writes=[xo_b])
                    if cc >= 4:
                        tg, tg_b = (BTg, BTg_b) if cc == 4 else (CTg, CTg_b)
                        for g in range(2):
                            k.op("act", lambda e: e.activation(out=tg[g][g * 64:(g + 1) * 64, 0:n], in_=xo[g * 64:(g + 1) * 64, 0:n],
                                                               func=AF.Identity), reads=[xo_b], writes=[tg_b])
                    cp(2)
                    if cc < 4:
                        pt_, pt_b = PJ.get()
                        k.op("pe", lambda e: [e.transpose(out=pt_[:, j * 128:(j + 1) * 128], in_=xo[:, j * 128:(j + 1) * 128],
                                                          identity=ident_f[:, :]) for j in range(nt)][-1],
                             reads=[xo_b, cb], writes=[pt_b])
                        k.op("act", lambda e: e.activation(
                            out=xs_tm[:, 0:nt, cc * 128:(cc + 1) * 128],
                            in_=pt_[:, 0:n].rearrange("p (j c) -> p j c", c=128), func=AF.Identity),
                             reads=[pt_b], writes=xs_tm_b[0:nt])
                        cp(3)
                    elif cc == 4:
                        pt_, pt_b = PJ.get()
                        ptv = pt_[:, :].bitcast(BF16)
                        k.op("pe", lambda e: [e.transpose(out=ptv[:, j * 128:(j + 1) * 128], in_=xo[:, j * 128:(j + 1) * 128],
                                                          identity=ident_bf[:, :]) for j in range(nt)][-1],
                             reads=[xo_b, cb], writes=[pt_b])
                        k.op("act", lambda e: e.activation(out=B_tm[:, 0:nt, :],
                                                           in_=ptv[:, 0:n].rearrange("p (j c) -> p j c", c=128),
                                                           func=AF.Identity), reads=[pt_b], writes=[B_tm_b])
                cp(4)
                tctx = {}
                def h1(j, t):
                    c0, c1 = t * 128, (t + 1) * 128
                    jc = slice(j * 128, (j + 1) * 128)
                    pp = j % 2
                    sm, sm_b = smP[pp], smP_b[pp]
                    MT, MT_b = MTP[pp], MTP_b[pp]
                    xdt, xdt_b = xdtP[pp], xdtP_b[pp]
                    xw, xw_b = xwP[pp], xwP_b[pp]
                    pz, pz_b = PJ.get()
                    k.op("pe", lambda e: [e.matmul(pz[:, :], lhsT=hT[:, kc, c0:c1], rhs=wss[:, kc, 0:512], start=(kc == 0),
                                                   stop=(kc == KC - 1)) for kc in range(KC)][-1],
                         reads=[wss_b, hT_b[t]], writes=[pz_b])
                    yield
                    psm, psm_b = b_sm
                    k.op("pe", lambda e: [e.matmul(psm[:, 0:8], lhsT=hT[:, kc, c0:c1], rhs=wss[:, kc, 1280:1288],
                                                   start=(kc == 0), stop=(kc == KC - 1)) for kc in range(KC)][-1],
                         reads=[wss_b, hT_b[t]], writes=[psm_b])
                    yield
                    dtr, e1, dt_, a_, ct, ea, dd, dec, cds = (sm[:, 0:8], sm[:, 8:16], sm[:, 16:24], sm[:, 24:32],
                                                              sm[:, 32:48], sm[:, 48:56], sm[:, 56:64], None, None)
                    k.op("dve", lambda e: e.tensor_tensor(out=dtr, in0=psm[:, 0:8], in1=dtb[:, :], op=ALU.add),
                         reads=[psm_b, pb], writes=[sm_b])
                    yield
                    k.op("act", lambda e: e.activation(out=e1, in_=dtr, func=AF.Exp), reads=[sm_b], writes=[sm_b])
                    yield
                    k.op("act", lambda e: e.activation(out=dt_, in_=e1, func=AF.Ln, bias=1.0), reads=[sm_b], writes=[sm_b])
                    yield
                    if t == 0:
                        k.op("dve", lambda e: e.memset(sm[0:PADR, 16:24], 0.0), reads=[sm_b], writes=[sm_b])
                        yield
                    k.op("dve", lambda e: e.tensor_tensor(out=a_, in0=dt_, in1=Ab[:, :], op=ALU.mult),
                         reads=[sm_b, pb], writes=[sm_b])
                    yield
                    k.op("dve", lambda e: e.tensor_copy(out=smb[:, 0:8], in_=a_), reads=[sm_b], writes=[smb_b])
                    yield
                    k.op("dve", lambda e: e.tensor_tensor(out=alo[:, :], in0=a_, in1=smb[:, 0:8], op=ALU.subtract),
                         reads=[sm_b, smb_b], writes=[smb_b])
                    yield
                    k.op("dve", lambda e: e.tensor_copy(out=smb[:, 8:16], in_=alo[:, :]), reads=[smb_b], writes=[smb_b])
                    yield
                    k.op("pe", lambda e: [e.matmul(psm[:, 8:16], lhsT=tri_bf[:, :], rhs=smb[:, 0:8], start=True, stop=False),
                                          e.matmul(psm[:, 8:16], lhsT=tri_bf[:, :], rhs=smb[:, 8:16], start=False, stop=True),
                                          e.matmul(psm[:, 16:24], lhsT=ones_bf[:, :], rhs=smb[:, 0:8], start=True, stop=False),
                                          e.matmul(psm[:, 16:24], lhsT=ones_bf[:, :], rhs=smb[:, 8:16], start=False, stop=True)][-1],
                         reads=[smb_b, cb], writes=[psm_b])
                    yield
                    k.op("act", lambda e: e.activation(out=ct, in_=psm[:, 8:24], func=AF.Identity), reads=[psm_b], writes=[sm_b])
                    yield
                    cum, tot = sm[:, 32:40], sm[:, 40:48]
                    k.op("act", lambda e: e.activation(out=ea, in_=cum, func=AF.Exp), reads=[sm_b], writes=[sm_b])
                    yield
                    k.op("dve", lambda e: e.tensor_tensor(out=dd, in0=tot, in1=cum, op=ALU.subtract), reads=[sm_b], writes=[sm_b])
                    yield
                    k.op("act", lambda e: e.activation(out=dd, in_=dd, func=AF.Exp), reads=[sm_b], writes=[sm_b])
                    yield
                    k.op("act", lambda e: e.activation(out=sm[0:64, 8:12], in_=sm[0:64, 40:44], func=AF.Exp), reads=[sm_b], writes=[sm_b])
                    yield
                    k.op("act", lambda e: e.activation(out=sm[64:128, 8:12], in_=sm[64:128, 44:48], func=AF.Exp), reads=[sm_b], writes=[sm_b])
                    yield
                    cds_ = sm[:, 8:12]
                    for i2 in range(2):
                        a_bc = smb[:, i2 * 8:(i2 + 1) * 8].unsqueeze(2).to_broadcast([128, 8, 128])
                        k.op("dve", lambda e: e.tensor_tensor(out=R[i2][:, :].rearrange("p (h c) -> p h c", h=8),
                                                              in0=tri_bf[:, :].unsqueeze(1).to_broadcast([128, 8, 128]),
                                                              in1=a_bc, op=ALU.mult), reads=[smb_b, cb], writes=[R_b])
                        yield
                        k.op("dve", lambda e: e.tensor_scalar(out=negA[i2][:, :].rearrange("p (h c) -> p h c", h=8), in0=a_bc,
                                                              scalar1=-1.0, scalar2=None, op0=ALU.mult),
                             reads=[smb_b], writes=[negA_b])
                        yield
                    k.op("pe", lambda e: [e.matmul(psm[:, 256 + g * 128:256 + (g + 1) * 128], lhsT=BTg[g][:, jc],
                                                   rhs=CT[:, jc], start=True, stop=True) for g in range(2)][-1],
                         reads=[BTg_b, CT_b], writes=[psm_b])
                    yield
                    for g in range(2):
                        pD, pD_b = PD.get()
                        hs = slice(g * 512, (g + 1) * 512)
                        k.op("pe", lambda e: [e.matmul(pD[:, :], lhsT=ones_bf[:, :], rhs=R[0][:, hs], start=True, stop=False),
                                              e.matmul(pD[:, :], lhsT=ones_bf[:, :], rhs=R[1][:, hs], start=False, stop=False),
                                              e.matmul(pD[:, :], lhsT=tri_bf[:, :], rhs=negA[0][:, hs], start=False, stop=False),
                                              e.matmul(pD[:, :], lhsT=tri_bf[:, :], rhs=negA[1][:, hs], start=False, stop=False),
                                              e.matmul(pD[:, :], lhsT=ident_bf[:, :], rhs=maskrep[:, :], start=False, stop=True)][-1],
                             reads=[R_b, negA_b, cb], writes=[pD_b])
                        yield
                        k.op("act", lambda e: e.activation(out=E[:, hs], in_=pD[:, :], func=AF.Exp), reads=[pD_b], writes=[E_b])
                        yield
                        for h4 in range(4):
                            hc = slice(g * 512 + h4 * 128, g * 512 + (h4 + 1) * 128)
                            k.op("dve", lambda e: e.tensor_tensor(out=MT[:, hc], in0=E[:, hc],
                                                                  in1=psm[:, 256 + g * 128:256 + (g + 1) * 128],
                                                                  op=ALU.mult), reads=[E_b, psm_b], writes=[MT_b])
                            yield
                    xs3 = xs_tm[:, j, :].rearrange("p (h c) -> p h c", h=8)
                    k.op("dve", lambda e: e.tensor_tensor(out=xdt[:, :].rearrange("p (h c) -> p h c", h=8), in0=xs3,
                                                           in1=dt_.unsqueeze(2).to_broadcast([128, 8, 64]), op=ALU.mult),
                         reads=[xs_tm_b[j], sm_b], writes=[xdt_b])
                    yield
                    k.op("dve", lambda e: e.tensor_tensor(out=xw[:, :].rearrange("p (h c) -> p h c", h=8),
                                                           in0=xdt[:, :].rearrange("p (h c) -> p h c", h=8),
                                                           in1=dd.unsqueeze(2).to_broadcast([128, 8, 64]), op=ALU.mult),
                         reads=[xdt_b, sm_b], writes=[xw_b])
                    yield
                    tctx[j] = (pz, pz_b)
                def h2(j, t):
                    c0, c1 = t * 128, (t + 1) * 128
                    jc = slice(j * 128, (j + 1) * 128)
                    pp = j % 2
                    sm, sm_b = smP[pp], smP_b[pp]
                    MT, MT_b = MTP[pp], MTP_b[pp]
                    xdt, xdt_b = xdtP[pp], xdtP_b[pp]
                    xw, xw_b = xwP[pp], xwP_b[pp]
                    pz, pz_b = tctx[j]
                    dt_, ea, dd, cds_ = sm[:, 16:24], sm[:, 48:56], sm[:, 56:64], sm[:, 8:12]
                    xs3 = xs_tm[:, j, :].rearrange("p (h c) -> p h c", h=8)
                    pyd, pyd_b = b_yd
                    k.op("pe", lambda e: [e.matmul(pyd[:, hh * 64:(hh + 1) * 64], lhsT=MT[:, hh * 128:(hh + 1) * 128],
                                                   rhs=xdt[:, hh * 64:(hh + 1) * 64], start=True, stop=True) for hh in range(8)][-1],
                         reads=[MT_b, xdt_b], writes=[pyd_b])
                    yield
                    pyo, pyo_b = b_yo
                    k.op("pe", lambda e: [e.matmul(pyo[:, g * 256:(g + 1) * 256], lhsT=CTg[g][:, jc],
                                                   rhs=Sbf[:, :], start=True, stop=True) for g in range(2)][-1],
                         reads=[CTg_b, Sbf_b], writes=[pyo_b])
                    yield
                    pst, pst_b = b_st
                    k.op("pe", lambda e: [e.matmul(pst[:, g * 256:(g + 1) * 256], lhsT=B_tm[:, j, :],
                                                   rhs=xw[:, g * 256:(g + 1) * 256], start=True, stop=True) for g in range(2)][-1],
                         reads=[B_tm_b, xw_b], writes=[pst_b])
                    yield
                    k.op("dve", lambda e: e.tensor_tensor(out=y1[:, :].rearrange("p (h c) -> p h c", h=8),
                                                          in0=pyo[:, :].rearrange("p (h c) -> p h c", h=8),
                                                          in1=ea.unsqueeze(2).to_broadcast([128, 8, 64]), op=ALU.mult),
                         reads=[pyo_b, sm_b], writes=[y1_b])
                    yield
                    k.op("dve", lambda e: e.tensor_tensor(out=y1[:, :], in0=y1[:, :], in1=pyd[:, :], op=ALU.add),
                         reads=[y1_b, pyd_b], writes=[y1_b])
                    yield
                    k.op("dve", lambda e: e.tensor_tensor(out=y2[:, :].rearrange("p (h c) -> p h c", h=8), in0=xs3,
                                                           in1=dsk[:, :].unsqueeze(2).to_broadcast([128, 8, 64]), op=ALU.mult),
                         reads=[xs_tm_b[j], pb], writes=[y2_b])
                    yield
                    k.op("dve", lambda e: e.tensor_tensor(out=y1[:, :], in0=y1[:, :], in1=y2[:, :], op=ALU.add),
                         reads=[y1_b, y2_b], writes=[y1_b])
                    yield
                    for g in range(2):
                        rs = slice(g * 64, (g + 1) * 64)
                        k.op("dve", lambda e: e.tensor_tensor(out=Stmp[rs, :].rearrange("p (h c) -> p h c", h=4),
                                                              in0=Sst[rs, :].rearrange("p (h c) -> p h c", h=4),
                                                              in1=cds_[rs, :].unsqueeze(2).to_broadcast([64, 4, 64]),
                                                              op=ALU.mult), reads=[Sst_b, sm_b], writes=[Stmp_b])
                        yield
                        k.op("dve", lambda e: e.tensor_tensor(out=Sst[rs, :], in0=Stmp[rs, :], in1=pst[rs, g * 256:(g + 1) * 256],
                                                              op=ALU.add), reads=[Stmp_b, pst_b], writes=[Sst_b])
                        yield
                    k.op("act", lambda e: e.activation(out=Sbf[:, :], in_=Sst[:, :], func=AF.Identity), reads=[Sst_b], writes=[Sbf_b])
                    yield
                    k.op("act", lambda e: e.activation(out=sz[:, :], in_=pz[:, :], func=AF.Silu), reads=[pz_b], writes=[sz_b])
                    yield
                    k.op("dve", lambda e: e.tensor_tensor(out=y1[:, :], in0=y1[:, :], in1=sz[:, :], op=ALU.mult),
                         reads=[y1_b, sz_b], writes=[y1_b])
                    yield
                    for g in range(2):
                        k.op("dve", lambda e: e.bn_stats(out=junk[:, g * 6:(g + 1) * 6], in_=y1[:, g * 256:(g + 1) * 256]),
                             reads=[y1_b], writes=[junk_b])
                        yield
                        k.op("dve", lambda e: e.bn_aggr(out=junk[:, 16 + g * 2:18 + g * 2], in_=junk[:, g * 6:(g + 1) * 6]),
                             reads=[junk_b], writes=[junk_b])
                        yield
                        k.op("dve", lambda e: e.scalar_tensor_tensor(out=ss[:, g:g + 1], in0=junk[:, 16 + g * 2:17 + g * 2],
                                                                     scalar=junk[:, 16 + g * 2:17 + g * 2],
                                                                     in1=junk[:, 17 + g * 2:18 + g * 2], op0=ALU.mult, op1=ALU.add),
                             reads=[junk_b], writes=[ss_b])
                        yield
                    k.op("dve", lambda e: e.tensor_scalar(out=ss[:, :], in0=ss[:, :], scalar1=EPS, scalar2=None,
                                                          op0=ALU.add), reads=[ss_b], writes=[ss_b])
                    yield
                    k.op("pool", lambda e: e.tensor_tensor(out=ss[:, :], in0=ss[:, :], in1=negh[:, 0:2], op=ALU.pow),
                         reads=[ss_b, cb], writes=[ss_b])
                    yield
                    yb, yb_b = yo[t % 2], yo_b[t % 2]
                    for g in range(2):
                        gs = slice(g * 256, (g + 1) * 256)
                        k.op("dve", lambda e: e.scalar_tensor_tensor(out=yb[:, gs], in0=y1[:, gs], scalar=ss[:, g:g + 1],
                                                                     in1=ngb[:, gs], op0=ALU.mult, op1=ALU.mult),
                             reads=[y1_b, ss_b, pb], writes=[yb_b])
                        yield
                    r0 = PADR if t == 0 else 0
                    k.dma("sp", mixd[t, r0:128, 0:512], yb[r0:128, :], reads=[yb_b], writes=[mixd_b[t][0]])
                    yield
                def lockstep(gens):
                    gens = list(gens)
                    while gens:
                        for g_ in list(gens):
                            try:
                                next(g_)
                            except StopIteration:
                                gens.remove(g_)
                lockstep([h1(0, tiles[0])])
                for j in range(len(tiles)):
                    gl = [h2(j, tiles[j])]
                    if j + 1 < len(tiles):
                        gl.append(h1(j + 1, tiles[j + 1]))
                    lockstep(gl)
        k.barrier()

    def phase_fox(l):
        win = W["w_in"][l].rearrange("(kc p) n -> p kc n", p=128)
        PJ = PsPool(banks[0:2])
        STP = PsPool(banks[2:5] + banks[7:8])
        OP = PsPool(banks[5:7])
        with ExitStack() as es:
            A = lambda nm, shp, dt: es.enter_context(nc.sbuf_tensor(un(nm), shp, dt))
            wf = A("wf", [128, KC, 772], BF16); wf_b = Buf()
            wqa = A("wqa", [128, 4, KC, 68], BF16); wqa_b = Buf()
            fbn = A("fbn", [128, 4], F32); fbn_b = Buf()
            KT = [A("KTf", [128, S], BF16) for _ in range(4)]
            KT_b = [[Buf() for _ in range(NT)] for _ in range(4)]
            V = A("Vf", [128, NT, 4, 66], BF16); V_b = [Buf() for _ in range(NT)]
            QT = [A("QTf", [128, 512], BF16) for _ in range(4)]; QT_b = [Buf() for _ in range(4)]
            ef = A("ef", [128, 512], F32); ef_b = Buf()
            c4 = A("c4", [128, 512], F32); c4_b = Buf()
            chi = A("chi", [128, 512], BF16); chi_b = Buf()
            clo = A("clo", [128, 512], F32); clo_b = Buf()
            ones4 = A("ones4", [128, 512], F32); ones4_b = Buf()
            cprev = A("cprev", [128, 4], F32); cprev_b = Buf()
            PT = [A("PT", [128, 512], BF16) for _ in range(4)]; PT_b = [Buf() for _ in range(4)]
            pti = [0]
            otile = [A("otile", [128, 4, 256], BF16) for _ in range(2)]; otile_b = [Buf(), Buf()]
            den = A("den", [128, 4], F32); den_b = Buf()

            k.dma("pool", wf[:, :, :], win[:, :, 1288:2060], writes=[wf_b])
            k.dma("sp", fbn[64:68, :], W["fox_f_b"][l].rearrange("(o d) -> o d", o=1).to_broadcast([4, 4]), writes=[fbn_b])
            k.op("dve", lambda e: e.tensor_scalar(out=fbn[64:68, :], in0=fbn[64:68, :], scalar1=-1.0, scalar2=None, op0=ALU.mult),
                 reads=[fbn_b], writes=[fbn_b])
            for h in range(4):
                k.op("dve", lambda e: e.tensor_copy(out=wqa[:, h, :, 0:64], in_=wf[:, :, h * 64:(h + 1) * 64]),
                     reads=[wf_b], writes=[wqa_b])
                k.op("dve", lambda e: e.tensor_copy(out=wqa[:, h, :, 64:68],
                                                    in_=wf[:, :, 768 + h:769 + h].to_broadcast([128, KC, 4])),
                     reads=[wf_b], writes=[wqa_b])
            k.op("dve", lambda e: e.memset(V[:, :, :, 64:66], 0.0), writes=V_b)
            k.op("dve", lambda e: e.memset(V[:, :, :, 64:65], 1.0), writes=V_b)
            k.op("dve", lambda e: e.memset(V[0:PADR, 0, :, 64:65], 0.0), writes=[V_b[0]])
            k.op("dve", lambda e: e.memset(ones4[64:68, :], 1.0), writes=[ones4_b])
            k.op("dve", lambda e: e.memset(cprev[64:68, :], 0.0), writes=[cprev_b])
            R4 = slice(64, 68)
            for ci_, tiles in enumerate(chunk_list(NT)):
                s0, nt = tiles[0] * 128, len(tiles)
                n = nt * 128
                hb = [hT_b[t] for t in tiles]
                for h in range(4):
                    pq, pq_b = PJ.get()
                    k.op("pe", lambda e: [e.matmul(pq[0:68, 0:n], lhsT=wqa[:, h, kc, :], rhs=hT[:, kc, s0:s0 + n],
                                                   start=(kc == 0), stop=(kc == KC - 1)) for kc in range(KC)][-1],
                         reads=[wqa_b] + hb, writes=[pq_b])
                    k.op("act", lambda e: e.activation(out=QT[h][0:64, 0:n], in_=pq[0:64, 0:n], func=AF.Identity),
                         reads=[pq_b], writes=[QT_b[h]])
                    k.op("act", lambda e: e.activation(out=ef[R4, 0:n], in_=pq[R4, 0:n], func=AF.Exp, scale=-1.0,
                                                       bias=fbn[R4, h:h + 1]), reads=[pq_b, fbn_b], writes=[ef_b])
                    k.op("act", lambda e: e.activation(out=ef[R4, 0:n], in_=ef[R4, 0:n], func=AF.Ln, bias=1.0),
                         reads=[ef_b], writes=[ef_b])
                    k.op("dve", lambda e: e.tensor_tensor_scan(out=c4[R4, 0:n], data0=ones4[R4, 0:n], data1=ef[R4, 0:n],
                                                               initial=cprev[R4, h:h + 1], op0=ALU.mult, op1=ALU.subtract),
                         reads=[ones4_b, ef_b, cprev_b], writes=[c4_b])
                    k.op("dve", lambda e: e.tensor_copy(out=cprev[R4, h:h + 1], in_=c4[R4, n - 1:n]), reads=[c4_b], writes=[cprev_b])
                    k.op("dve", lambda e: e.tensor_copy(out=chi[R4, 0:n], in_=c4[R4, 0:n]), reads=[c4_b], writes=[chi_b])
                    k.op("dve", lambda e: e.tensor_tensor(out=clo[R4, 0:n], in0=c4[R4, 0:n], in1=chi[R4, 0:n], op=ALU.subtract),
                         reads=[c4_b, chi_b], writes=[clo_b])
                    kts = [KT_b[h][t] for t in tiles]
                    k.op("dve", lambda e: e.tensor_scalar(out=c4[R4, 0:n], in0=chi[R4, 0:n], scalar1=aug[R4, 0:1], scalar2=aug[R4, 2:3],
                                                          op0=ALU.mult, op1=ALU.add), reads=[chi_b, cb, c4_b], writes=[c4_b])
                    k.op("dve", lambda e: e.scalar_tensor_tensor(out=KT[h][R4, s0:s0 + n], in0=clo[R4, 0:n], scalar=aug[R4, 1:2],
                                                                 in1=c4[R4, 0:n], op0=ALU.mult, op1=ALU.add),
                         reads=[clo_b, c4_b, cb], writes=kts)
                    k.op("dve", lambda e: e.tensor_scalar(out=c4[R4, 0:n], in0=chi[R4, 0:n], scalar1=aug[R4, 3:4], scalar2=aug[R4, 5:6],
                                                          op0=ALU.mult, op1=ALU.add), reads=[chi_b, cb, c4_b], writes=[c4_b])
                    k.op("dve", lambda e: e.scalar_tensor_tensor(out=QT[h][R4, 0:n], in0=clo[R4, 0:n], scalar=aug[R4, 4:5],
                                                                 in1=c4[R4, 0:n], op0=ALU.mult, op1=ALU.add),
                         reads=[clo_b, c4_b, cb], writes=[QT_b[h]])
                    pk, pk_b = PJ.get()
                    k.op("pe", lambda e: [e.matmul(pk[0:64, 0:n], lhsT=wf[:, kc, 256 + h * 64:256 + (h + 1) * 64],
                                                   rhs=hT[:, kc, s0:s0 + n], start=(kc == 0), stop=(kc == KC - 1))
                                          for kc in range(KC)][-1], reads=[wf_b] + hb, writes=[pk_b])
                    k.op("act", lambda e: e.activation(out=KT[h][0:64, s0:s0 + n], in_=pk[0:64, 0:n], func=AF.Identity),
                         reads=[pk_b], writes=kts)
                for j, t in enumerate(tiles):
                    pv, pv_b = PJ.get()
                    k.op("pe", lambda e: [e.matmul(pv[:, 0:256], lhsT=hT[:, kc, t * 128:(t + 1) * 128], rhs=wf[:, kc, 512:768],
                                                   start=(kc == 0), stop=(kc == KC - 1)) for kc in range(KC)][-1],
                         reads=[wf_b, hT_b[t]], writes=[pv_b])
                    k.op("act", lambda e: e.activation(out=V[:, t, :, 0:64], in_=pv[:, 0:256].rearrange("p (h c) -> p h c", h=4),
                                                       func=AF.Identity), reads=[pv_b], writes=[V_b[t]])
                ot, ot_b = otile[ci_ % 2], otile_b[ci_ % 2]
                for h in range(4):
                    po, po_b = attention(KT[h], KT_b[h], QT[h], QT_b[h], V, V_b, 68, 0.125, tiles, h, ot, ot_b, PT, PT_b, pti, STP, OP)
                    attn_norm(po, po_b, nt, h, ot, ot_b, den, den_b)
                for j, t in enumerate(tiles):
                    r0 = PADR if t == 0 else 0
                    k.dma("sp", mixd[t, r0:128, 512:768], ot[r0:128, j, :], reads=[ot_b], writes=[mixd_b[t][1]])
        k.barrier()

    def phase_mla(l):
        win = W["w_in"][l].rearrange("(kc p) n -> p kc n", p=128)
        PJ = PsPool(banks[0:2])
        STP = PsPool(banks[2:5])
        OP = PsPool(banks[5:7])
        b_x = banks[7]
        scale = float(96 ** -0.5)
        with ExitStack() as es:
            A = lambda nm, shp, dt: es.enter_context(nc.sbuf_tensor(un(nm), shp, dt))
            wm = A("wm", [128, KC, 416], BF16); wm_b = Buf()
            wuq = A("wuq", [128, 2, 384], BF16); wuqs = A("wuqs", [128, 2, 4, 96], BF16)
            wukv = A("wukv", [128, 512], BF16)
            wv = A("wv", [128, 256], BF16)
            wkr = A("wkr", [128, 2, KC, 96], BF16)
            qg = A("qg", [128, 2], F32); kvg = A("kvg", [128, 1], F32)
            wp_b = Buf()
            KT = [A("KTm", [128, S], BF16) for _ in range(4)]
            KT_b = [[Buf() for _ in range(NT)] for _ in range(4)]
            V = A("Vm", [128, NT, 4, 66], BF16); V_b = [Buf() for _ in range(NT)]
            QT = [A("QTm", [128, 512], BF16) for _ in range(4)]; QT_b = [Buf() for _ in range(4)]
            cqn = A("cqn", [128, 2, 512], BF16); cqn_b = Buf()
            ckvn = A("ckvn", [128, 512], BF16); ckvn_b = Buf()
            sqq = [A("sqq", [128, 512], BF16) for _ in range(2)]; sqq_b = [Buf(), Buf()]
            sqk = A("sqk", [128, 512], BF16); sqk_b = Buf()
            rq = A("rq", [128, 512], F32); rq_b = Buf()
            rkv = A("rkv", [128, 512], F32); rkv_b = Buf()
            rv = A("rv", [128, 4], F32); rv_b = Buf()
            cs = A("cs", [128, 2, 512], F32); cs_b = Buf()
            csr = A("csr", [128, 2, 512], F32); csr_b = Buf()
            t1 = A("t1", [128, 512], F32); t1_b = Buf()
            t2 = A("t2", [128, 512], F32); t2_b = Buf()
            PT = [A("PT", [128, 512], BF16) for _ in range(4)]; PT_b = [Buf() for _ in range(4)]
            pti = [0]
            otile = [A("otile", [128, 4, 256], BF16) for _ in range(2)]; otile_b = [Buf(), Buf()]
            den = A("den", [128, 4], F32); den_b = Buf()

            k.dma("pool", wm[:, :, :], win[:, :, 2060:2476], writes=[wm_b])
            k.dma("pool", wuq[:, :, :], W["mla_w_uq"][l].rearrange("(j p) n -> p j n", p=128), writes=[wp_b])
            k.dma("pool", wukv[:, :], W["mla_w_ukv"][l], writes=[wp_b])
            k.dma("sp", qg[:, :], W["mla_q_norm_g"][l].rearrange("(j p) -> p j", p=128), writes=[wp_b],
                  allow_slow_non_contiguous=True)
            k.dma("sp", kvg[:, :], W["mla_kv_norm_g"][l].rearrange("(p o) -> p o", o=1), writes=[wp_b])
            wuq4 = wuq[:, :, :].rearrange("p j (h c) -> p j h c", h=4)
            k.op("dve", lambda e: e.tensor_copy(out=wuqs[:, :, :, :], in_=wuq4), reads=[wp_b], writes=[wp_b])
            k.op("dve", lambda e: e.tensor_copy(out=wuqs[:, :, :, 64:80], in_=wuq4[:, :, :, 80:96]), reads=[wp_b], writes=[wp_b])
            k.op("dve", lambda e: e.tensor_copy(out=wuqs[:, :, :, 80:96], in_=wuq4[:, :, :, 64:80]), reads=[wp_b], writes=[wp_b])
            k.op("dve", lambda e: e.tensor_copy(out=wv[:, :].rearrange("p (h c) -> p h c", h=4),
                                                in_=wukv[:, :].rearrange("p (h c) -> p h c", h=4)[:, :, 64:128]),
                 reads=[wp_b], writes=[wp_b])
            k.op("dve", lambda e: e.memset(wkr[:, :, :, :], 0.0), writes=[wp_b])
            k.op("dve", lambda e: e.tensor_copy(out=wkr[:, 0, :, 64:96], in_=wm[:, :, 384:416]), reads=[wm_b, wp_b], writes=[wp_b])
            k.op("dve", lambda e: e.tensor_copy(out=wkr[:, 1, :, 64:80], in_=wm[:, :, 400:416]), reads=[wm_b, wp_b], writes=[wp_b])
            k.op("dve", lambda e: e.tensor_copy(out=wkr[:, 1, :, 80:96], in_=wm[:, :, 384:400]), reads=[wm_b, wp_b], writes=[wp_b])
            k.op("dve", lambda e: e.memset(V[:, :, :, 64:66], 0.0), writes=V_b)
            k.op("dve", lambda e: e.memset(V[:, :, :, 64:65], 1.0), writes=V_b)
            k.op("dve", lambda e: e.memset(V[0:PADR, 0, :, 64:65], 0.0), writes=[V_b[0]])
            RR = slice(64, 96)
            for ci_, tiles in enumerate(chunk_list(NT)):
                s0, nt = tiles[0] * 128, len(tiles)
                n = nt * 128
                hb = [hT_b[t] for t in tiles]
                k.dma("sp", cs[RR, 0, 0:n], c_cos[:, s0:s0 + n], writes=[cs_b])
                k.dma("sp", cs[RR, 1, 0:n], c_sin[:, s0:s0 + n], writes=[cs_b])
                for j2 in range(2):
                    pc, pc_b = PJ.get()
                    k.op("pe", lambda e: [e.matmul(pc[:, 0:n], lhsT=wm[:, kc, j2 * 128:(j2 + 1) * 128], rhs=hT[:, kc, s0:s0 + n],
                                                   start=(kc == 0), stop=(kc == KC - 1)) for kc in range(KC)][-1],
                         reads=[wm_b] + hb, writes=[pc_b])
                    k.op("act", lambda e: e.activation(out=cqn[:, j2, 0:n], in_=pc[:, 0:n], func=AF.Identity, scale=qg[:, j2:j2 + 1]),
                         reads=[pc_b, wp_b], writes=[cqn_b])
                    k.op("act", lambda e: e.activation(out=sqq[j2][:, 0:n], in_=pc[:, 0:n], func=AF.Square),
                         reads=[pc_b], writes=[sqq_b[j2]])
                px, px_b = b_x
                k.op("pe", lambda e: [e.matmul(px[:, 0:n], lhsT=ones_bf[:, :], rhs=sqq[0][:, 0:n], start=True, stop=False),
                                      e.matmul(px[:, 0:n], lhsT=ones_bf[:, :], rhs=sqq[1][:, 0:n], start=False, stop=True)][-1],
                     reads=sqq_b + [cb], writes=[px_b])
                k.op("dve", lambda e: e.tensor_scalar(out=rq[:, 0:n], in0=px[:, 0:n], scalar1=1.0 / 256, scalar2=EPS,
                                                      op0=ALU.mult, op1=ALU.add), reads=[px_b], writes=[rq_b])
                k.op("act", lambda e: e.activation(out=rq[:, 0:n], in_=rq[:, 0:n], func=AF.Ln), reads=[rq_b], writes=[rq_b])
                k.op("act", lambda e: e.activation(out=rq[:, 0:n], in_=rq[:, 0:n], func=AF.Exp, scale=-0.5), reads=[rq_b], writes=[rq_b])
                pc, pc_b = PJ.get()
                k.op("pe", lambda e: [e.matmul(pc[:, 0:n], lhsT=wm[:, kc, 256:384], rhs=hT[:, kc, s0:s0 + n],
                                               start=(kc == 0), stop=(kc == KC - 1)) for kc in range(KC)][-1],
                     reads=[wm_b] + hb, writes=[pc_b])
                k.op("act", lambda e: e.activation(out=ckvn[:, 0:n], in_=pc[:, 0:n], func=AF.Identity, scale=kvg[:, 0:1]),
                     reads=[pc_b, wp_b], writes=[ckvn_b])
                k.op("act", lambda e: e.activation(out=sqk[:, 0:n], in_=pc[:, 0:n], func=AF.Square), reads=[pc_b], writes=[sqk_b])
                px, px_b = b_x
                k.op("pe", lambda e: e.matmul(px[:, 0:n], lhsT=ones_bf[:, :], rhs=sqk[:, 0:n], start=True, stop=True),
                     reads=[sqk_b, cb], writes=[px_b])
                k.op("dve", lambda e: e.tensor_scalar(out=rkv[:, 0:n], in0=px[:, 0:n], scalar1=1.0 / 128, scalar2=EPS,
                                                      op0=ALU.mult, op1=ALU.add), reads=[px_b], writes=[rkv_b])
                k.op("act", lambda e: e.activation(out=rkv[:, 0:n], in_=rkv[:, 0:n], func=AF.Ln), reads=[rkv_b], writes=[rkv_b])
                k.op("act", lambda e: e.activation(out=rkv[:, 0:n], in_=rkv[:, 0:n], func=AF.Exp, scale=-0.5), reads=[rkv_b], writes=[rkv_b])
                px, px_b = b_x
                k.op("pe", lambda e: [e.matmul(px[:, 2 * j:2 * j + 2], lhsT=sqk[:, j * 128:(j + 1) * 128], rhs=ones_bf[:, 0:2],
                                               start=True, stop=True) for j in range(nt)][-1],
                     reads=[sqk_b, cb], writes=[px_b])
                k.op("dve", lambda e: e.tensor_scalar(out=rv[:, 0:nt], in0=px[:, 0:2 * nt].rearrange("p (j c) -> p j c", c=2)[:, :, 0],
                                                      scalar1=1.0 / 128, scalar2=EPS,
                                                      op0=ALU.mult, op1=ALU.add), reads=[px_b], writes=[rv_b])
                k.op("pool", lambda e: e.tensor_tensor(out=rv[:, 0:nt], in0=rv[:, 0:nt], in1=negh[:, 0:nt], op=ALU.pow),
                     reads=[rv_b, cb], writes=[rv_b])
                for i2 in range(2):
                    k.op("dve", lambda e: e.tensor_tensor(out=csr[RR, i2, 0:n], in0=cs[RR, i2, 0:n], in1=rq[RR, 0:n], op=ALU.mult),
                         reads=[cs_b, rq_b], writes=[csr_b])
                for h in range(4):
                    pq1, pq1_b = PJ.get()
                    pq2, pq2_b = PJ.get()
                    k.op("pe", lambda e: [e.matmul(pq1[0:96, 0:n], lhsT=wuq[:, j2, h * 96:(h + 1) * 96], rhs=cqn[:, j2, 0:n],
                                                   start=(j2 == 0), stop=(j2 == 1)) for j2 in range(2)][-1],
                         reads=[wp_b, cqn_b], writes=[pq1_b])
                    k.op("pe", lambda e: [e.matmul(pq2[0:96, 0:n], lhsT=wuqs[:, j2, h, :], rhs=cqn[:, j2, 0:n],
                                                   start=(j2 == 0), stop=(j2 == 1)) for j2 in range(2)][-1],
                         reads=[wp_b, cqn_b], writes=[pq2_b])
                    k.op("dve", lambda e: e.tensor_tensor(out=QT[h][0:64, 0:n], in0=pq1[0:64, 0:n], in1=rq[0:64, 0:n], op=ALU.mult),
                         reads=[pq1_b, rq_b], writes=[QT_b[h]])
                    k.op("dve", lambda e: e.tensor_tensor(out=t1[RR, 0:n], in0=pq1[RR, 0:n], in1=csr[RR, 0, 0:n], op=ALU.mult),
                         reads=[pq1_b, csr_b], writes=[t1_b])
                    k.op("dve", lambda e: e.tensor_tensor(out=t2[RR, 0:n], in0=pq2[RR, 0:n], in1=csr[RR, 1, 0:n], op=ALU.mult),
                         reads=[pq2_b, csr_b], writes=[t2_b])
                    k.op("dve", lambda e: e.tensor_tensor(out=QT[h][RR, 0:n], in0=t1[RR, 0:n], in1=t2[RR, 0:n], op=ALU.add),
                         reads=[t1_b, t2_b], writes=[QT_b[h]])
                pk1, pk1_b = PJ.get()
                pk2, pk2_b = PJ.get()
                for i2, (pk, pk_b) in enumerate([(pk1, pk1_b), (pk2, pk2_b)]):
                    k.op("pe", lambda e: [e.matmul(pk[0:96, 0:n], lhsT=wkr[:, i2, kc, :], rhs=hT[:, kc, s0:s0 + n],
                                                   start=(kc == 0), stop=(kc == KC - 1)) for kc in range(KC)][-1],
                         reads=[wp_b] + hb, writes=[pk_b])
                k.op("dve", lambda e: e.tensor_tensor(out=t1[RR, 0:n], in0=pk1[RR, 0:n], in1=cs[RR, 0, 0:n], op=ALU.mult),
                     reads=[pk1_b, cs_b], writes=[t1_b])
                k.op("dve", lambda e: e.tensor_tensor(out=t2[RR, 0:n], in0=pk2[RR, 0:n], in1=cs[RR, 1, 0:n], op=ALU.mult),
                     reads=[pk2_b, cs_b], writes=[t2_b])
                for h in range(4):
                    kts = [KT_b[h][t] for t in tiles]
                    k.op("dve", lambda e: e.tensor_tensor(out=KT[h][RR, s0:s0 + n], in0=t1[RR, 0:n], in1=t2[RR, 0:n], op=ALU.add),
                         reads=[t1_b, t2_b], writes=kts)
                    pkn, pkn_b = PJ.get()
                    k.op("pe", lambda e: e.matmul(pkn[0:64, 0:n], lhsT=wukv[:, h * 128:h * 128 + 64], rhs=ckvn[:, 0:n],
                                                  start=True, stop=True), reads=[wp_b, ckvn_b], writes=[pkn_b])
                    k.op("dve", lambda e: e.tensor_tensor(out=KT[h][0:64, s0:s0 + n], in0=pkn[0:64, 0:n], in1=rkv[0:64, 0:n], op=ALU.mult),
                         reads=[pkn_b, rkv_b], writes=kts)
                for j, t in enumerate(tiles):
                    pv, pv_b = PJ.get()
                    k.op("pe", lambda e: e.matmul(pv[:, 0:256], lhsT=ckvn[:, j * 128:(j + 1) * 128],
                                                  rhs=wv[:, :],
                                                  start=True, stop=True), reads=[wp_b, ckvn_b], writes=[pv_b])
                    k.op("act", lambda e: e.activation(out=V[:, t, :, 0:64], in_=pv[:, 0:256].rearrange("p (h c) -> p h c", h=4),
                                                       func=AF.Identity, scale=rv[:, j:j + 1]), reads=[pv_b, rv_b], writes=[V_b[t]])
                ot, ot_b = otile[ci_ % 2], otile_b[ci_ % 2]
                for h in range(4):
                    po, po_b = attention(KT[h], KT_b[h], QT[h], QT_b[h], V, V_b, 96, scale, tiles, h, ot, ot_b, PT, PT_b, pti, STP, OP)
                    attn_norm(po, po_b, nt, h, ot, ot_b, den, den_b)
                for j, t in enumerate(tiles):
                    r0 = PADR if t == 0 else 0
                    k.dma("sp", mixd[t, r0:128, 768:1024], ot[r0:128, j, :], reads=[ot_b], writes=[mixd_b[t][2]])
        k.barrier()

    def phase_out(l, ci):
        PG = PsPool(banks[0:6])
        with ExitStack() as es:
            A = lambda nm, shp, dt: es.enter_context(nc.sbuf_tensor(un(nm), shp, dt))
            wout = A("wout", [128, KC, D], BF16); wout_b = Buf()
            mt = [A("mt", [128, D], BF16) for _ in range(2)]; mt_b = [Buf(), Buf()]
            mixT = [A("mixT", [128, KC, 128], BF16) for _ in range(2)]; mixT_b = [Buf(), Buf()]
            gb = A("gb", [128, 2, D], F32); gb_b = Buf()
            lnb = make_ln_bufs(A)
            k.dma("pool", wout[:, :, :], W["w_out"][l].rearrange("(kc p) n -> p kc n", p=128), writes=[wout_b])
            k.dma("sp", gb[:, 0, :], W["ln2_g"][l].rearrange("(o d) -> o d", o=1).to_broadcast([128, D]), writes=[gb_b])
            k.dma("sp", gb[:, 1, :], W["ln2_b"][l].rearrange("(o d) -> o d", o=1).to_broadcast([128, D]), writes=[gb_b])

            def load_m(t):
                p = t % 2
                if t == 0:
                    k.op("dve", lambda e: e.memset(mt[p][:, :], 0.0), writes=[mt_b[p]])
                r0 = PADR if t == 0 else 0
                k.dma("sp", mt[p][r0:128, :], mixd[t, r0:128, :], reads=mixd_b[t], writes=[mt_b[p]])
                load_h(t, lnb[p])

            load_m(0)
            for t in range(NT):
                p = t % 2
                if t + 1 < NT:
                    load_m(t + 1)
                tb, tb_b = TP.get()
                tbv = tb[:, :].bitcast(BF16)
                k.op("pe", lambda e: [e.transpose(out=tbv[:, kc * 128:(kc + 1) * 128], in_=mt[p][:, kc * 128:(kc + 1) * 128],
                                                  identity=ident_bf[:, :]) for kc in range(KC)][-1],
                     reads=[mt_b[p], cb], writes=[tb_b])
                k.op("act", lambda e: e.activation(out=mixT[p][:, :, :], in_=tbv.rearrange("p (kc c) -> p kc c", kc=KC),
                                                   func=AF.Identity), reads=[tb_b], writes=[mixT_b[p]])
                ys = []
                for hf in range(2):
                    py, py_b = PG.get()
                    k.op("pe", lambda e: [e.matmul(py[:, :], lhsT=mixT[p][:, kc, :], rhs=wout[:, kc, hf * 512:(hf + 1) * 512],
                                                   start=(kc == 0), stop=(kc == KC - 1)) for kc in range(KC)][-1],
                         reads=[mixT_b[p], wout_b], writes=[py_b])
                    ys.append((py, py_b))
                ln_tile(t, ys, 1.0, gb, gb_b, ci, lnb[p], False)
        k.barrier()

    def run():
        phase_init()
        if only is not None:
            for ph in only:
                try:
                    dict(ssd=phase_ssd, fox=phase_fox, mla=phase_mla)[ph](0)
                except _StopPhase:
                    k.barrier()
            if not _cpstop:
                for ph in only:
                    dump_mix("mix_0", dict(ssd=(0, 512), fox=(512, 768), mla=(768, 1024))[ph])
            return
        for l in range(depth):
            last = (l == depth - 1)
            phase_ffn(l, 1, l * 6 + 0, False)
            dump_h("h1_%d" % l)
            if stop_after == "h1_%d" % l:
                return
            phase_ssd(l)
            phase_fox(l)
            phase_mla(l)
            dump_mix("mix_%d" % l)
            if stop_after == "mix_%d" % l:
                return
            phase_out(l, l * 6 + 2)
            dump_h("h2_%d" % l)
            if stop_after == "h2_%d" % l:
                return
            phase_ffn(l, 2, l * 6 + 4, last)
            dump_h("h3_%d" % l)
            if stop_after == "h3_%d" % l:
                return

    run()
    k.barrier()
    k.finish(out_b + dbg_b)
    build.stats = dict(nins=k.nins, nwait=k.nwait)
    return nc


def host_consts(NT):
    S = NT * 128
    bf = ml_dtypes.bfloat16
    idx = np.arange(128)
    c = {}
    c["c_ident_bf"] = np.eye(128, dtype=np.float32).astype(bf)
    c["c_ident_f"] = np.eye(128, dtype=np.float32)
    c["c_tri"] = (idx[:, None] <= idx[None, :]).astype(np.float32)
    mneg = np.where(idx[:, None] > idx[None, :], -30000.0, 0.0).astype(np.float32)
    c["c_maskneg"] = mneg.astype(bf)
    c["c_maskrep"] = np.tile(mneg, (1, 4)).astype(bf)
    pos = (np.arange(S) - PADR).astype(np.float32)
    inv_freq = (1.0 / (np.float32(10000.0) ** (np.arange(0, 32, 2, dtype=np.float32) / np.float32(32)))).astype(np.float32)
    ang = pos[None, :] * inv_freq[:, None]
    cos = np.cos(ang).astype(np.float32)
    sin = np.sin(ang).astype(np.float32)
    c["c_cos"] = np.concatenate([cos, cos], 0)
    c["c_sin"] = np.concatenate([-sin, sin], 0)
    aug = np.zeros((4, 6), np.float32)
    aug[0, 0] = -8.0
    aug[1, 1] = -8.0
    aug[2, 2] = 1.0
    aug[3, 2] = 1.0
    aug[2, 3] = 8.0
    aug[3, 4] = 8.0
    aug[0, 5] = 1.0
    aug[1, 5] = 1.0
    c["c_aug"] = aug
    return c


_CACHE = {}


def kernel(**inputs):
    x = np.ascontiguousarray(inputs["x"], dtype=np.float32)
    B, SEQ, _ = x.shape
    NT = SEQ // 128 + 1
    key = (NT,)
    if key not in _CACHE:
        _CACHE[key] = build(NT)
    nc = _CACHE[key]
    consts = host_consts(NT)
    shared = {name: np.ascontiguousarray(inputs[name], dtype=np.float32) for name, _ in PARAM_SHAPES}
    shared["meta"] = np.ascontiguousarray(inputs["meta"], dtype=np.float32)
    shared.update(consts)
    in_maps = []
    for b in range(B):
        m = dict(shared)
        m["x"] = x[b]
        in_maps.append(m)
    res = run_bass_kernel_spmd(nc, in_maps, core_ids=list(range(B)))
    out = np.stack([np.asarray(res.results[b]["out"], dtype=np.float32) for b in range(B)], 0)
    return out
```

```python
import numpy as np
import ml_dtypes
from contextlib import ExitStack
import concourse.bass as bass
import concourse.mybir as mybir
from concourse.bass_utils import run_bass_kernel_spmd

F32 = mybir.dt.float32
BF16 = mybir.dt.bfloat16
AF = mybir.ActivationFunctionType
ALU = mybir.AluOpType

D = 1024
F = 2816
NFC = 22
KC = 8
N_IN = 2476
DEPTH = 2
ALPHA = float((2 * DEPTH) ** 0.25)
EPS = 1e-5
PADR = 112


class Buf:
    __slots__ = ("w", "r", "name")

    def __init__(self, name=""):
        self.w = {}
        self.r = {}
        self.name = name


class KB:
    NDMA = 40

    def __init__(self, nc):
        self.nc = nc
        self.eng = dict(pe=nc.tensor, act=nc.scalar, dve=nc.vector, pool=nc.gpsimd, sp=nc.sync)
        self.sem = {k: nc.alloc_semaphore("s_" + k) for k in self.eng}
        self.cnt = {k: 0 for k in self.eng}
        self.seen = {k: {} for k in self.eng}
        self.dsem = [nc.alloc_semaphore("d%d" % i) for i in range(self.NDMA)]
        self.dval = [0] * self.NDMA
        self.dq = dict(sp=list(range(0, 24)), pool=list(range(24, self.NDMA)))
        self.dnext = dict(sp=0, pool=0)
        self.nwait = 0
        self.nins = 0

    def _wait(self, e, s, v):
        sid = id(s)
        if self.seen[e].get(sid, 0) < v:
            self.eng[e].wait_ge(s, v)
            self.seen[e][sid] = v
            self.nwait += 1

    def _deps(self, e, reads, writes):
        need = {}
        for b in reads:
            for sid, (s, v) in b.w.items():
                if need.get(sid, (None, 0))[1] < v:
                    need[sid] = (s, v)
        for b in writes:
            for d in (b.w, b.r):
                for sid, (s, v) in d.items():
                    if need.get(sid, (None, 0))[1] < v:
                        need[sid] = (s, v)
        if e == "pe":
            need.pop(id(self.sem["pe"]), None)
        for sid, (s, v) in need.items():
            self._wait(e, s, v)

    def _mark(self, s, v, reads, writes):
        sid = id(s)
        for b in reads:
            b.r[sid] = (s, v)
        for b in writes:
            b.w[sid] = (s, v)

    def op(self, e, fn, reads=(), writes=()):
        self._deps(e, reads, writes)
        ins = fn(self.eng[e])
        self.cnt[e] += 1
        ins.then_inc(self.sem[e], 1)
        self._mark(self.sem[e], self.cnt[e], reads, writes)
        self.nins += 1
        return ins

    def dma(self, q, out, in_, reads=(), writes=(), **kw):
        self._deps(q, reads, writes)
        lst = self.dq[q]
        i = lst[self.dnext[q] % len(lst)]
        self.dnext[q] += 1
        s = self.dsem[i]
        self._wait(q, s, self.dval[i])
        ins = self.eng[q].dma_start(out=out, in_=in_, **kw)
        self.dval[i] += 16
        ins.then_inc(s, 16)
        self._mark(s, self.dval[i], reads, writes)
        self.nins += 1
        return ins

    def barrier(self):
        for e in self.eng:
            for kk in self.eng:
                if kk != e and self.cnt[kk] > 0:
                    self._wait(e, self.sem[kk], self.cnt[kk])
            for i in range(self.NDMA):
                if self.dval[i] > 0:
                    self._wait(e, self.dsem[i], self.dval[i])

    def finish(self, bufs):
        for b in bufs:
            for sid, (s, v) in b.w.items():
                self._wait("sp", s, v)


class PsPool:
    def __init__(self, banks):
        self.banks = banks
        self.i = 0

    def get(self):
        b = self.banks[self.i % len(self.banks)]
        self.i += 1
        return b


def chunk_list(NT):
    out = [[0]]
    t = 1
    while t < NT:
        out.append(list(range(t, min(t + 4, NT))))
        t += 4
    return out


PARAM_SHAPES = [
    ("ffn1_w_gate", [D, F]), ("ffn1_w_up", [D, F]), ("ffn1_w_down", [F, D]),
    ("ln1_g", [D]), ("ln1_b", [D]), ("w_in", [D, N_IN]), ("conv_w", [4, 768]), ("conv_b", [768]),
    ("dt_bias", [8]), ("a_log", [8]), ("d_skip", [8]), ("ssd_norm_g", [512]), ("fox_f_b", [4]),
    ("mla_q_norm_g", [256]), ("mla_w_uq", [256, 384]), ("mla_kv_norm_g", [128]), ("mla_w_ukv", [128, 512]),
    ("w_out", [D, D]), ("ln2_g", [D]), ("ln2_b", [D]),
    ("ffn2_w_gate", [D, F]), ("ffn2_w_up", [D, F]), ("ffn2_w_down", [F, D]),
    ("ln3_g", [D]), ("ln3_b", [D]),
]


def build(NT, depth=DEPTH, dbg=None, stop_after=None, only=None):
    S = NT * 128
    nc = bass.Bass("TRN2", target_bir_lowering=False)
    k = KB(nc)
    uid = [0]

    def un(name):
        uid[0] += 1
        return "%s_%d" % (name, uid[0])

    import os as _os
    _cpstop = int(_os.environ.get("SSD_STOP", "0"))

    class _StopPhase(Exception):
        pass

    def cp(n):
        if _cpstop and n == _cpstop:
            raise _StopPhase()

    def din(name, shape, dt=F32):
        return nc.dram_tensor(name, shape, dt, kind="ExternalInput").ap()

    x_in = din("x", [(NT - 1) * 128, D])
    meta_in = din("meta", [16, D])
    W = {name: din(name, [DEPTH] + shp) for name, shp in PARAM_SHAPES}
    c_ident_bf = din("c_ident_bf", [128, 128], BF16)
    c_ident_f = din("c_ident_f", [128, 128])
    c_tri = din("c_tri", [128, 128])
    c_maskneg = din("c_maskneg", [128, 128], BF16)
    c_maskrep = din("c_maskrep", [128, 512], BF16)
    c_cos = din("c_cos", [32, S])
    c_sin = din("c_sin", [32, S])
    c_aug = din("c_aug", [4, 6])
    out_d = nc.dram_tensor("out", [(NT - 1) * 128, D], F32, kind="ExternalOutput").ap()
    hres = nc.dram_tensor("hres", [NT, 128, D], F32).ap()
    mixd = nc.dram_tensor("mixd", [NT, 128, D], BF16).ap()
    hres_b = [Buf("hres%d" % t) for t in range(NT)]
    mixd_b = [[Buf() for _ in range(3)] for t in range(NT)]
    out_b = [Buf() for t in range(NT)]
    dbg_outs = {}
    if dbg:
        for name in dbg:
            if name.startswith("h"):
                dbg_outs[name] = nc.dram_tensor("dbg_" + name, [NT, 128, D], F32, kind="ExternalOutput").ap()
            else:
                dbg_outs[name] = nc.dram_tensor("dbg_" + name, [NT, 128, D], BF16, kind="ExternalOutput").ap()
    dbg_b = []

    PA = nc.alloc_sbuf_tensor
    hT = PA("hT", [128, KC, S], BF16)
    hT_b = [Buf("hT%d" % t) for t in range(NT)]
    ident_bf = PA("ident_bf", [128, 128], BF16)
    ident_f = PA("ident_f", [128, 128], F32)
    tri = PA("tri", [128, 128], F32)
    ones_f = PA("ones_f", [128, 128], F32)
    maskneg = PA("maskneg", [128, 128], BF16)
    maskrep = PA("maskrep", [128, 512], BF16)
    tri_bf = PA("tri_bf", [128, 128], BF16)
    ones_bf = PA("ones_bf", [128, 128], BF16)
    negh = PA("negh", [128, 512], F32)
    aug = PA("aug", [128, 6], F32)
    lncol = PA("lncol", [128, DEPTH * 6, KC], F32)
    cb = Buf("consts")

    banks = []
    for i in range(8):
        banks.append((nc.alloc_psum_tensor("bank%d" % i, [128, 512], F32), Buf("bank%d" % i)))

    k.dma("sp", ident_bf[:, :], c_ident_bf, writes=[cb])
    k.dma("sp", ident_f[:, :], c_ident_f, writes=[cb])
    k.dma("sp", tri[:, :], c_tri, writes=[cb])
    k.dma("sp", maskneg[:, :], c_maskneg, writes=[cb])
    k.dma("sp", maskrep[:, :], c_maskrep, writes=[cb])
    k.dma("sp", aug[64:68, :], c_aug, writes=[cb])
    k.op("dve", lambda e: e.memset(ones_f[:, :], 1.0), writes=[cb])
    k.op("dve", lambda e: e.memset(ones_bf[:, :], 1.0), writes=[cb])
    k.op("dve", lambda e: e.tensor_copy(out=tri_bf[:, :], in_=tri[:, :]), reads=[cb], writes=[cb])
    k.op("dve", lambda e: e.memset(negh[:, :], -0.5), writes=[cb])
    for l in range(depth):
        for i, nm in enumerate(["ln1_g", "ln1_b", "ln2_g", "ln2_b", "ln3_g", "ln3_b"]):
            k.dma("sp", lncol[:, l * 6 + i, :], W[nm][l].rearrange("(kc p) -> p kc", p=128), writes=[cb],
                  allow_slow_non_contiguous=True)
    k.op("dve", lambda e: e.memset(hT[:, :, 0:PADR], 0.0), writes=[hT_b[0]])

    tile_cols = lambda t: (t * 128, (t + 1) * 128)

    def make_ln_bufs(A):
        bufs = []
        for p in range(2):
            d = dict(hin=A("hin", [128, D], F32), yh=A("yh", [128, D], F32), xnb=A("xnb", [128, D], BF16),
                     st=A("st", [128, 12], F32), mv=A("mv", [128, 4], F32))
            d.update(hin_b=Buf(), yh_b=Buf(), xnb_b=Buf(), st_b=Buf(), mv_b=Buf())
            bufs.append(d)
        return bufs

    def load_h(t, lb):
        k.dma("sp", lb["hin"][:, :], hres[t], reads=[hres_b[t]], writes=[lb["hin_b"]])

    def ln_tile(t, ys, coef, gb, gb_b, ci, lb, final):
        hin, yh, xnb, st, mv = lb["hin"], lb["yh"], lb["xnb"], lb["st"], lb["mv"]
        hin_b, yh_b, xnb_b, st_b, mv_b = lb["hin_b"], lb["yh_b"], lb["xnb_b"], lb["st_b"], lb["mv_b"]
        for hf in range(2):
            k.op("act", lambda e: e.activation(out=yh[:, hf * 512:(hf + 1) * 512], in_=ys[hf][0][:, :],
                                               func=AF.Identity, scale=float(coef)),
                 reads=[ys[hf][1]], writes=[yh_b])
        k.op("dve", lambda e: e.scalar_tensor_tensor(out=yh[:, :], in0=hin[:, :], scalar=ALPHA, in1=yh[:, :],
                                                     op0=ALU.mult, op1=ALU.add), reads=[hin_b, yh_b], writes=[yh_b])
        for hf in range(2):
            k.op("dve", lambda e: e.bn_stats(out=st[:, hf * 6:(hf + 1) * 6], in_=yh[:, hf * 512:(hf + 1) * 512]),
                 reads=[yh_b], writes=[st_b])
        k.op("dve", lambda e: e.bn_aggr(out=mv[:, 0:2], in_=st[:, :]), reads=[st_b], writes=[mv_b])
        k.op("dve", lambda e: e.tensor_scalar(out=mv[:, 2:3], in0=mv[:, 1:2], scalar1=EPS, scalar2=None, op0=ALU.add),
             reads=[mv_b], writes=[mv_b])
        k.op("pool", lambda e: e.tensor_tensor(out=mv[:, 2:3], in0=mv[:, 2:3], in1=negh[:, 0:1], op=ALU.pow),
             reads=[mv_b, cb], writes=[mv_b])
        k.op("dve", lambda e: e.scalar_tensor_tensor(out=mv[:, 3:4], in0=mv[:, 0:1], scalar=-1.0, in1=mv[:, 2:3],
                                                     op0=ALU.mult, op1=ALU.mult), reads=[mv_b], writes=[mv_b])
        k.op("dve", lambda e: e.tensor_scalar(out=hin[:, :], in0=yh[:, :], scalar1=mv[:, 0:1], scalar2=mv[:, 2:3],
                                              op0=ALU.subtract, op1=ALU.mult), reads=[yh_b, mv_b], writes=[hin_b])
        k.op("act", lambda e: e.activation(out=xnb[:, :], in_=yh[:, :], func=AF.Identity, scale=mv[:, 2:3],
                                           bias=mv[:, 3:4]), reads=[yh_b, mv_b], writes=[xnb_b])
        k.op("dve", lambda e: e.tensor_tensor(out=yh[:, :], in0=hin[:, :], in1=gb[:, 0, :], op=ALU.mult),
             reads=[hin_b, gb_b], writes=[yh_b])
        k.op("pool", lambda e: e.tensor_tensor(out=yh[:, :], in0=yh[:, :], in1=gb[:, 1, :], op=ALU.add),
             reads=[yh_b, gb_b], writes=[yh_b])
        r0 = PADR if t == 0 else 0
        k.dma("sp", hres[t, r0:128, :], yh[r0:128, :], reads=[yh_b], writes=[hres_b[t]])
        if final and t > 0:
            k.dma("sp", out_d[(t - 1) * 128:t * 128, :], yh[:, :], reads=[yh_b], writes=[out_b[t]])
        tb, tb_b = TP.get()
        tbv = tb[:, :].bitcast(BF16)
        k.op("pe", lambda e: [e.transpose(out=tbv[:, kc * 128:(kc + 1) * 128], in_=xnb[:, kc * 128:(kc + 1) * 128],
                                          identity=ident_bf[:, :]) for kc in range(KC)][-1],
             reads=[xnb_b, cb], writes=[tb_b])
        c0 = PADR if t == 0 else 0
        for kc in range(KC):
            k.op("act", lambda e: e.activation(out=hT[:, kc, t * 128 + c0:(t + 1) * 128],
                                               in_=tbv[:, kc * 128 + c0:(kc + 1) * 128], func=AF.Identity,
                                               scale=lncol[:, ci, kc:kc + 1], bias=lncol[:, ci + 1, kc:kc + 1]),
                 reads=[tb_b, cb], writes=[hT_b[t]])

    def dump_h(name):
        if dbg and name in dbg_outs:
            k.barrier()
            b = Buf()
            k.dma("sp", dbg_outs[name], hres, reads=hres_b, writes=[b])
            dbg_b.append(b)
            k.barrier()

    def dump_mix(name, cols=(0, D)):
        if dbg and name in dbg_outs:
            k.barrier()
            b = Buf()
            for t in range(NT):
                r0 = PADR if t == 0 else 0
                k.dma("sp", dbg_outs[name][t, r0:128, cols[0]:cols[1]], mixd[t, r0:128, cols[0]:cols[1]], reads=mixd_b[t], writes=[b])
            dbg_b.append(b)
            k.barrier()

    TP = PsPool(banks[6:8])

    def phase_init():
        with ExitStack() as es:
            A = lambda nm, shp, dt: es.enter_context(nc.sbuf_tensor(un(nm), shp, dt))
            zt = A("zt", [128, D], F32)
            zt_b = Buf()
            hin = [A("hin0", [128, D], F32) for _ in range(2)]
            hb = [A("hb0", [128, D], BF16) for _ in range(2)]
            hin_b = [Buf(), Buf()]
            hb_b = [Buf(), Buf()]
            k.op("dve", lambda e: e.memset(zt[:, :], 0.0), writes=[zt_b])
            k.dma("sp", hres[0], zt[:, :], reads=[zt_b], writes=[hres_b[0]])
            k.dma("sp", hres[0, PADR:128, :], meta_in, writes=[hres_b[0]])
            for t in range(1, NT):
                k.dma("sp", hres[t], x_in[(t - 1) * 128:t * 128, :], writes=[hres_b[t]])
            for t in range(NT):
                p = t % 2
                k.dma("sp", hin[p][:, :], hres[t], reads=[hres_b[t]], writes=[hin_b[p]])
                k.op("act", lambda e: e.activation(out=hb[p][:, :], in_=hin[p][:, :], func=AF.Identity),
                     reads=[hin_b[p]], writes=[hb_b[p]])
                tb, tb_b = TP.get()
                tbv = tb[:, :].bitcast(BF16)
                k.op("pe", lambda e: [e.transpose(out=tbv[:, kc * 128:(kc + 1) * 128],
                                                  in_=hb[p][:, kc * 128:(kc + 1) * 128],
                                                  identity=ident_bf[:, :]) for kc in range(KC)][-1],
                     reads=[hb_b[p], cb], writes=[tb_b])
                c0 = PADR if t == 0 else 0
                k.op("dve", lambda e: e.tensor_copy(
                    out=hT[:, :, t * 128 + c0:(t + 1) * 128],
                    in_=tbv.rearrange("p (kc c) -> p kc c", kc=KC)[:, :, c0:128]),
                     reads=[tb_b], writes=[hT_b[t]])
        k.barrier()

    def phase_ffn(l, which, ci, final):
        wg = W["ffn%d_w_gate" % which][l].rearrange("(kc p) f -> p kc f", p=128)
        wu = W["ffn%d_w_up" % which][l].rearrange("(kc p) f -> p kc f", p=128)
        wdn = W["ffn%d_w_down" % which][l]
        lg = W["ln%d_g" % (1 if which == 1 else 3)][l]
        lb_ = W["ln%d_b" % (1 if which == 1 else 3)][l]
        PMAXT = 9
        passes = []
        t = 0
        while t < NT:
            passes.append(list(range(t, min(t + PMAXT, NT))))
            t += PMAXT
        if len(passes) > 1 and len(passes[-1]) < 4:
            allt = list(range(NT))
            h = (NT + 1) // 2
            passes = [allt[:h], allt[h:]]
        PG = PsPool(banks[0:6])
        with ExitStack() as es:
            A = lambda nm, shp, dt: es.enter_context(nc.sbuf_tensor(un(nm), shp, dt))
            actT = A("actT", [128, NFC, PMAXT * 128], BF16)
            actT_b = [Buf() for _ in range(PMAXT)]
            wd = A("wd", [128, NFC, D], BF16)
            wd_b = [Buf() for _ in range(NFC)]
            wgu = [A("wgu", [128, 2, KC, 256], BF16) for _ in range(2)]
            wgu_b = [Buf(), Buf()]
            stmp = [A("stmp", [128, 512], F32) for _ in range(3)]
            stmp_b = [Buf(), Buf(), Buf()]
            gb = A("gb", [128, 2, D], F32)
            gb_b = Buf()
            lnb = make_ln_bufs(A)
            k.dma("sp", gb[:, 0, :], lg.rearrange("(o d) -> o d", o=1).to_broadcast([128, D]), writes=[gb_b])
            k.dma("sp", gb[:, 1, :], lb_.rearrange("(o d) -> o d", o=1).to_broadcast([128, D]), writes=[gb_b])
            si = 0
            for ptiles in passes:
                p0 = ptiles[0] * 128
                chunks = []
                tl = list(ptiles)
                if tl[0] == 0:
                    chunks.append((0, 128, [0]))
                    tl = tl[1:]
                while tl:
                    grp = tl[:4]
                    tl = tl[4:]
                    chunks.append((grp[0] * 128, len(grp) * 128, grp))
                for fcp in range(NFC // 2):
                    wb, wb_b = wgu[fcp % 2], wgu_b[fcp % 2]
                    k.dma("pool", wb[:, 0], wg[:, :, fcp * 256:(fcp + 1) * 256], writes=[wb_b])
                    k.dma("pool", wb[:, 1], wu[:, :, fcp * 256:(fcp + 1) * 256], writes=[wb_b])
                    for j in range(2):
                        fc = fcp * 2 + j
                        k.dma("pool", wd[:, fc, :], wdn[fc * 128:(fc + 1) * 128, :], writes=[wd_b[fc]])
                    for j in range(2):
                        fc = fcp * 2 + j
                        for (s0, n, tiles) in chunks:
                            pg, pg_b = PG.get()
                            pu, pu_b = PG.get()
                            hb = [hT_b[t] for t in tiles]
                            k.op("pe", lambda e: [e.matmul(pg[:, 0:n], lhsT=wb[:, 0, kc, j * 128:(j + 1) * 128],
                                                           rhs=hT[:, kc, s0:s0 + n], start=(kc == 0),
                                                           stop=(kc == KC - 1)) for kc in range(KC)][-1],
                                 reads=[wb_b] + hb, writes=[pg_b])
                            k.op("pe", lambda e: [e.matmul(pu[:, 0:n], lhsT=wb[:, 1, kc, j * 128:(j + 1) * 128],
                                                           rhs=hT[:, kc, s0:s0 + n], start=(kc == 0),
                                                           stop=(kc == KC - 1)) for kc in range(KC)][-1],
                                 reads=[wb_b] + hb, writes=[pu_b])
                            sp_, sp_b = stmp[si % 3], stmp_b[si % 3]
                            si += 1
                            k.op("act", lambda e: e.activation(out=sp_[:, 0:n], in_=pg[:, 0:n], func=AF.Silu),
                                 reads=[pg_b], writes=[sp_b])
                            k.op("dve", lambda e: e.tensor_tensor(out=actT[:, fc, s0 - p0:s0 - p0 + n], in0=sp_[:, 0:n],
                                                                  in1=pu[:, 0:n], op=ALU.mult),
                                 reads=[sp_b, pu_b], writes=[actT_b[t - ptiles[0]] for t in tiles])
                load_h(ptiles[0], lnb[ptiles[0] % 2])
                for ti, t in enumerate(ptiles):
                    if ti + 1 < len(ptiles):
                        load_h(ptiles[ti + 1], lnb[ptiles[ti + 1] % 2])
                    ys = []
                    for hf in range(2):
                        py, py_b = PG.get()
                        k.op("pe", lambda e: [e.matmul(py[:, :], lhsT=actT[:, fc, ti * 128:(ti + 1) * 128],
                                                       rhs=wd[:, fc, hf * 512:(hf + 1) * 512], start=(fc == 0),
                                                       stop=(fc == NFC - 1)) for fc in range(NFC)][-1],
                             reads=[actT_b[ti]] + wd_b, writes=[py_b])
                        ys.append((py, py_b))
                    ln_tile(t, ys, 0.5, gb, gb_b, ci, lnb[t % 2], final)
        k.barrier()

    def attention(KT, KT_b, QT, QT_b, V, V_b, Kd, scale, tiles, h, otile, otile_b, PT, PT_b, pti, STP, OP):
        first, nt, last = tiles[0], len(tiles), tiles[-1]
        n = nt * 128
        po, po_b = OP.get()
        for kt in range(last + 1):
            jk = kt - first
            q0 = 0 if kt < first else jk * 128
            ps_, ps_b = STP.get()
            kcols = slice(kt * 128, (kt + 1) * 128)
            if kt < first:
                k.op("pe", lambda e: e.matmul(ps_[:, 0:n], lhsT=KT[0:Kd, kcols], rhs=QT[0:Kd, 0:n], start=True, stop=True),
                     reads=[KT_b[kt], QT_b], writes=[ps_b])
            else:
                def f(e):
                    e.matmul(ps_[:, q0:q0 + 128], lhsT=KT[0:Kd, kcols], rhs=QT[0:Kd, q0:q0 + 128], start=True, stop=False)
                    r = e.matmul(ps_[:, q0:q0 + 128], lhsT=ident_bf[:, :], rhs=maskneg[:, :], start=False, stop=True)
                    if q0 + 128 < n:
                        r = e.matmul(ps_[:, q0 + 128:n], lhsT=KT[0:Kd, kcols], rhs=QT[0:Kd, q0 + 128:n], start=True, stop=True)
                    return r
                k.op("pe", f, reads=[KT_b[kt], QT_b, cb], writes=[ps_b])
            pt, pt_b = PT[pti[0] % len(PT)], PT_b[pti[0] % len(PT)]
            pti[0] += 1
            k.op("act", lambda e: e.activation(out=pt[:, q0:n], in_=ps_[:, q0:n], func=AF.Exp, scale=float(scale)),
                 reads=[ps_b], writes=[pt_b])
            j0 = max(0, jk)
            k.op("pe", lambda e: [e.matmul(po[:, j * 128:j * 128 + 66], lhsT=pt[:, j * 128:(j + 1) * 128],
                                           rhs=V[:, kt, h, 0:66], start=(kt == 0 and j == j0), stop=(kt == first + j),
                                           skip_group_check=True)
                                  for j in range(j0, nt)][-1],
                 reads=[pt_b, V_b[kt]], writes=[po_b])
        return po, po_b

    def attn_norm(po, po_b, nt, h, otile, otile_b, den, den_b):
        pov = po[:, :].rearrange("p (j c) -> p j c", c=128)
        k.op("dve", lambda e: e.tensor_scalar(out=den[:, 0:nt], in0=pov[:, 0:nt, 64], scalar1=1e-30, scalar2=None,
                                              op0=ALU.add), reads=[po_b], writes=[den_b])
        k.op("dve", lambda e: e.reciprocal(out=den[:, 0:nt], in_=den[:, 0:nt]), reads=[den_b], writes=[den_b])
        k.op("dve", lambda e: e.tensor_tensor(out=otile[:, 0:nt, h * 64:(h + 1) * 64], in0=pov[:, 0:nt, 0:64],
                                              in1=den[:, 0:nt].unsqueeze(2).to_broadcast([128, nt, 64]), op=ALU.mult),
             reads=[po_b, den_b], writes=[otile_b])

    def phase_ssd(l):
        win = W["w_in"][l].rearrange("(kc p) n -> p kc n", p=128)
        PJ = PsPool(banks[0:2])
        PD = PsPool(banks[2:4])
        b_yd, b_yo, b_st, b_sm = banks[4], banks[5], banks[6], banks[7]
        with ExitStack() as es:
            A = lambda nm, shp, dt: es.enter_context(nc.sbuf_tensor(un(nm), shp, dt))
            wss = A("wss", [128, KC, 1288], BF16); wss_b = Buf()
            cw = A("cw", [128, 6, 4], F32); cbias = A("cbias", [128, 6], F32)
            dtb = A("dtb", [128, 8], F32); Ab = A("Ab", [128, 8], F32); dsk = A("dsk", [128, 8], F32)
            ngb = A("ngb", [128, 512], F32)
            pb = Buf("ssd_params")
            xraw = A("xraw", [128, 6, 515], F32); xraw_b = [Buf() for _ in range(6)]
            acc = [A("acc", [128, 512], F32) for _ in range(2)]; acc_b = [Buf(), Buf()]
            xsT = [A("xsT", [128, 512], F32) for _ in range(2)]; xsT_b = [Buf(), Buf()]
            BT = A("BT", [128, 512], BF16); BT_b = Buf()
            CT = A("CT", [128, 512], BF16); CT_b = Buf()
            BTg = [A("BTg", [128, 512], BF16) for _ in range(2)]; BTg_b = Buf()
            CTg = [A("CTg", [128, 512], BF16) for _ in range(2)]; CTg_b = Buf()
            xs_tm = A("xs_tm", [128, 4, 512], F32); xs_tm_b = [Buf() for _ in range(4)]
            B_tm = A("B_tm", [128, 4, 128], BF16); B_tm_b = Buf()
            smP = [A("sm", [128, 64], F32) for _ in range(2)]; smP_b = [Buf(), Buf()]
            R = [A("R", [128, 1024], BF16) for _ in range(2)]; R_b = Buf()
            negA = [A("negA", [128, 1024], BF16) for _ in range(2)]; negA_b = Buf()
            smb = A("smb", [128, 16], BF16); smb_b = Buf()
            alo = A("alo", [128, 8], F32)
            E = A("E", [128, 1024], F32); E_b = Buf()
            MTP = [A("MT", [128, 1024], BF16) for _ in range(2)]; MTP_b = [Buf(), Buf()]
            xdtP = [A("xdt", [128, 512], BF16) for _ in range(2)]; xdtP_b = [Buf(), Buf()]
            xwP = [A("xw", [128, 512], BF16) for _ in range(2)]; xwP_b = [Buf(), Buf()]
            y1 = A("y1", [128, 512], F32); y1_b = Buf()
            y2 = A("y2", [128, 512], F32); y2_b = Buf()
            Sst = A("Sst", [128, 256], F32); Sst_b = Buf()
            Stmp = A("Stmp", [128, 256], F32); Stmp_b = Buf()
            Sbf = A("Sbf", [128, 256], BF16); Sbf_b = Buf()
            sz = A("sz", [128, 512], F32); sz_b = Buf()
            junk = A("junk", [128, 256], F32); junk_b = Buf()
            ss = A("ss", [128, 2], F32); ss_b = Buf()
            yo = [A("yo", [128, 512], BF16) for _ in range(2)]; yo_b = [Buf(), Buf()]

            k.dma("pool", wss[:, :, :], win[:, :, 0:1288], writes=[wss_b])
            for j in range(4):
                k.dma("sp", cw[:, :, j], W["conv_w"][l, j].rearrange("(cc p) -> p cc", p=128), writes=[pb],
                      allow_slow_non_contiguous=True)
            k.dma("sp", cbias[:, :], W["conv_b"][l].rearrange("(cc p) -> p cc", p=128), writes=[pb],
                  allow_slow_non_contiguous=True)
            bc = lambda ap, n_: ap.rearrange("(o d) -> o d", o=1).to_broadcast([128, n_])
            k.dma("sp", dtb[:, :], bc(W["dt_bias"][l], 8), writes=[pb])
            k.dma("sp", Ab[:, :], bc(W["a_log"][l], 8), writes=[pb])
            k.dma("sp", dsk[:, :], bc(W["d_skip"][l], 8), writes=[pb])
            k.dma("sp", ngb[:, :], bc(W["ssd_norm_g"][l], 512), writes=[pb])
            k.op("act", lambda e: e.activation(out=Ab[:, :], in_=Ab[:, :], func=AF.Exp), reads=[pb], writes=[pb])
            k.op("dve", lambda e: e.tensor_scalar(out=Ab[:, :], in0=Ab[:, :], scalar1=-1.0, scalar2=None, op0=ALU.mult),
                 reads=[pb], writes=[pb])
            k.op("dve", lambda e: e.memset(xraw[:, :, 0:3], 0.0), writes=xraw_b)
            for g in range(2):
                k.op("dve", lambda e: e.memset(BTg[g][:, :], 0.0), writes=[BTg_b])
                k.op("dve", lambda e: e.memset(CTg[g][:, :], 0.0), writes=[CTg_b])
            k.op("dve", lambda e: e.memset(Sst[:, :], 0.0), writes=[Sst_b])
            k.op("dve", lambda e: e.memset(Sbf[:, :], 0.0), writes=[Sbf_b])
            cp(1)

            xi = 0
            for tiles in chunk_list(NT):
                s0, nt = tiles[0] * 128, len(tiles)
                n = nt * 128
                hb = [hT_b[t] for t in tiles]
                for cc in range(6):
                    pj, pj_b = PJ.get()
                    k.op("pe", lambda e: [e.matmul(pj[:, 0:n], lhsT=wss[:, kc, 512 + cc * 128:512 + (cc + 1) * 128],
                                                   rhs=hT[:, kc, s0:s0 + n], start=(kc == 0), stop=(kc == KC - 1))
                                          for kc in range(KC)][-1], reads=[wss_b] + hb, writes=[pj_b])
                    k.op("act", lambda e: e.activation(out=xraw[:, cc, 3:3 + n], in_=pj[:, 0:n], func=AF.Identity),
                         reads=[pj_b], writes=[xraw_b[cc]])
                    ac, ac_b = acc[cc % 2], acc_b[cc % 2]
                    k.op("dve", lambda e: e.tensor_scalar(out=ac[:, 0:n], in0=xraw[:, cc, 3:3 + n], scalar1=cw[:, cc, 3:4],
                                                          scalar2=None, op0=ALU.mult), reads=[xraw_b[cc], pb], writes=[ac_b])
                    for j in (2, 1, 0):
                        k.op("dve", lambda e: e.scalar_tensor_tensor(out=ac[:, 0:n], in0=xraw[:, cc, j:j + n],
                                                                     scalar=cw[:, cc, j:j + 1], in1=ac[:, 0:n],
                                                                     op0=ALU.mult, op1=ALU.add),
                             reads=[xraw_b[cc], pb, ac_b], writes=[ac_b])
                    k.op("dve", lambda e: e.tensor_copy(out=xraw[:, cc, 0:3], in_=xraw[:, cc, n:n + 3]),
                         reads=[xraw_b[cc]], writes=[xraw_b[cc]])
                    if cc < 4:
                        xo, xo_b = xsT[xi % 2], xsT_b[xi % 2]
                        xi += 1
                    elif cc == 4:
                        xo, xo_b = BT, BT_b
                    else:
                        xo, xo_b = CT, CT_b
                    k.op("act", lambda e: e.activation(out=xo[:, 0:n], in_=ac[:, 0:n], func=AF.Silu, bias=cbias[:, cc:cc + 1]),
                         reads=[ac_b, pb], writes=[xo_b])
                    if tiles[0] == 0:
                        k.op("dve", lambda e: e.memset(xo[:, 0:PADR], 0.0), writes=[xo_b])
                    if cc >= 4:
                        tg, tg_b = (BTg, BTg_b) if cc == 4 else (CTg, CTg_b)
                        for g in range(2):
                            k.op("act", lambda e: e.activation(out=tg[g][g * 64:(g + 1) * 64, 0:n], in_=xo[g * 64:(g + 1) * 64, 0:n],
                                                               func=AF.Identity), reads=[xo_b], writes=[tg_b])
                    cp(2)
                    if cc < 4:
                        pt_, pt_b = PJ.get()
                        k.op("pe", lambda e: [e.transpose(out=pt_[:, j * 128:(j + 1) * 128], in_=xo[:, j * 128:(j + 1) * 128],
                                                          identity=ident_f[:, :]) for j in range(nt)][-1],
                             reads=[xo_b, cb], writes=[pt_b])
                        k.op("act", lambda e: e.activation(
                            out=xs_tm[:, 0:nt, cc * 128:(cc + 1) * 128],
                            in_=pt_[:, 0:n].rearrange("p (j c) -> p j c", c=128), func=AF.Identity),
                             reads=[pt_b], writes=xs_tm_b[0:nt])
                        cp(3)
                    elif cc == 4:
                        pt_, pt_b = PJ.get()
                        ptv = pt_[:, :].bitcast(BF16)
                        k.op("pe", lambda e: [e.transpose(out=ptv[:, j * 128:(j + 1) * 128], in_=xo[:, j * 128:(j + 1) * 128],
                                                          identity=ident_bf[:, :]) for j in range(nt)][-1],
                             reads=[xo_b, cb], writes=[pt_b])
                        k.op("act", lambda e: e.activation(out=B_tm[:, 0:nt, :],
                                                           in_=ptv[:, 0:n].rearrange("p (j c) -> p j c", c=128),
                                                           func=AF.Identity), reads=[pt_b], writes=[B_tm_b])
                cp(4)
                tctx = {}
                def h1(j, t):
                    c0, c1 = t * 128, (t + 1) * 128
                    jc = slice(j * 128, (j + 1) * 128)
                    pp = j % 2
                    sm, sm_b = smP[pp], smP_b[pp]
                    MT, MT_b = MTP[pp], MTP_b[pp]
                    xdt, xdt_b = xdtP[pp], xdtP_b[pp]
                    xw, xw_b = xwP[pp], xwP_b[pp]
                    pz, pz_b = PJ.get()
                    k.op("pe", lambda e: [e.matmul(pz[:, :], lhsT=hT[:, kc, c0:c1], rhs=wss[:, kc, 0:512], start=(kc == 0),
                                                   stop=(kc == KC - 1)) for kc in range(KC)][-1],
                         reads=[wss_b, hT_b[t]], writes=[pz_b])
                    yield
                    psm, psm_b = b_sm
                    k.op("pe", lambda e: [e.matmul(psm[:, 0:8], lhsT=hT[:, kc, c0:c1], rhs=wss[:, kc, 1280:1288],
                                                   start=(kc == 0), stop=(kc == KC - 1)) for kc in range(KC)][-1],
                         reads=[wss_b, hT_b[t]], writes=[psm_b])
                    yield
                    dtr, e1, dt_, a_, ct, ea, dd, dec, cds = (sm[:, 0:8], sm[:, 8:16], sm[:, 16:24], sm[:, 24:32],
                                                              sm[:, 32:48], sm[:, 48:56], sm[:, 56:64], None, None)
                    k.op("dve", lambda e: e.tensor_tensor(out=dtr, in0=psm[:, 0:8], in1=dtb[:, :], op=ALU.add),
                         reads=[psm_b, pb], writes=[sm_b])
                    yield
                    k.op("act", lambda e: e.activation(out=e1, in_=dtr, func=AF.Exp), reads=[sm_b], writes=[sm_b])
                    yield
                    k.op("act", lambda e: e.activation(out=dt_, in_=e1, func=AF.Ln, bias=1.0), reads=[sm_b], writes=[sm_b])
                    yield
                    if t == 0:
                        k.op("dve", lambda e: e.memset(sm[0:PADR, 16:24], 0.0), reads=[sm_b], writes=[sm_b])
                        yield
                    k.op("dve", lambda e: e.tensor_tensor(out=a_, in0=dt_, in1=Ab[:, :], op=ALU.mult),
                         reads=[sm_b, pb], writes=[sm_b])
                    yield
                    k.op("dve", lambda e: e.tensor_copy(out=smb[:, 0:8], in_=a_), reads=[sm_b], writes=[smb_b])
                    yield
                    k.op("dve", lambda e: e.tensor_tensor(out=alo[:, :], in0=a_, in1=smb[:, 0:8], op=ALU.subtract),
                         reads=[sm_b, smb_b], writes=[smb_b])
                    yield
                    k.op("dve", lambda e: e.tensor_copy(out=smb[:, 8:16], in_=alo[:, :]), reads=[smb_b], writes=[smb_b])
                    yield
                    k.op("pe", lambda e: [e.matmul(psm[:, 8:16], lhsT=tri_bf[:, :], rhs=smb[:, 0:8], start=True, stop=False),
                                          e.matmul(psm[:, 8:16], lhsT=tri_bf[:, :], rhs=smb[:, 8:16], start=False, stop=True),
                                          e.matmul(psm[:, 16:24], lhsT=ones_bf[:, :], rhs=smb[:, 0:8], start=True, stop=False),
                                          e.matmul(psm[:, 16:24], lhsT=ones_bf[:, :], rhs=smb[:, 8:16], start=False, stop=True)][-1],
                         reads=[smb_b, cb], writes=[psm_b])
                    yield
                    k.op("act", lambda e: e.activation(out=ct, in_=psm[:, 8:24], func=AF.Identity), reads=[psm_b], writes=[sm_b])
                    yield
                    cum, tot = sm[:, 32:40], sm[:, 40:48]
                    k.op("act", lambda e: e.activation(out=ea, in_=cum, func=AF.Exp), reads=[sm_b], writes=[sm_b])
                    yield
                    k.op("dve", lambda e: e.tensor_tensor(out=dd, in0=tot, in1=cum, op=ALU.subtract), reads=[sm_b], writes=[sm_b])
                    yield
                    k.op("act", lambda e: e.activation(out=dd, in_=dd, func=AF.Exp), reads=[sm_b], writes=[sm_b])
                    yield
                    k.op("act", lambda e: e.activation(out=sm[0:64, 8:12], in_=sm[0:64, 40:44], func=AF.Exp), reads=[sm_b], writes=[sm_b])
                    yield
                    k.op("act", lambda e: e.activation(out=sm[64:128, 8:12], in_=sm[64:128, 44:48], func=AF.Exp), reads=[sm_b], writes=[sm_b])
                    yield
                    cds_ = sm[:, 8:12]
                    for i2 in range(2):
                        a_bc = smb[:, i2 * 8:(i2 + 1) * 8].unsqueeze(2).to_broadcast([128, 8, 128])
                        k.op("dve", lambda e: e.tensor_tensor(out=R[i2][:, :].rearrange("p (h c) -> p h c", h=8),
                                                              in0=tri_bf[:, :].unsqueeze(1).to_broadcast([128, 8, 128]),
                                                              in1=a_bc, op=ALU.mult), reads=[smb_b, cb], writes=[R_b])
                        yield
                        k.op("dve", lambda e: e.tensor_scalar(out=negA[i2][:, :].rearrange("p (h c) -> p h c", h=8), in0=a_bc,
                                                              scalar1=-1.0, scalar2=None, op0=ALU.mult),
                             reads=[smb_b], writes=[negA_b])
                        yield
                    k.op("pe", lambda e: [e.matmul(psm[:, 256 + g * 128:256 + (g + 1) * 128], lhsT=BTg[g][:, jc],
                                                   rhs=CT[:, jc], start=True, stop=True) for g in range(2)][-1],
                         reads=[BTg_b, CT_b], writes=[psm_b])
                    yield
                    for g in range(2):
                        pD, pD_b = PD.get()
                        hs = slice(g * 512, (g + 1) * 512)
                        k.op("pe", lambda e: [e.matmul(pD[:, :], lhsT=ones_bf[:, :], rhs=R[0][:, hs], start=True, stop=False),
                                              e.matmul(pD[:, :], lhsT=ones_bf[:, :], rhs=R[1][:, hs], start=False, stop=False),
                                              e.matmul(pD[:, :], lhsT=tri_bf[:, :], rhs=negA[0][:, hs], start=False, stop=False),
                                              e.matmul(pD[:, :], lhsT=tri_bf[:, :], rhs=negA[1][:, hs], start=False, stop=False),
                                              e.matmul(pD[:, :], lhsT=ident_bf[:, :], rhs=maskrep[:, :], start=False, stop=True)][-1],
                             reads=[R_b, negA_b, cb], writes=[pD_b])
                        yield
                        k.op("act", lambda e: e.activation(out=E[:, hs], in_=pD[:, :], func=AF.Exp), reads=[pD_b], writes=[E_b])
                        yield
                        for h4 in range(4):
                            hc = slice(g * 512 + h4 * 128, g * 512 + (h4 + 1) * 128)
                            k.op("dve", lambda e: e.tensor_tensor(out=MT[:, hc], in0=E[:, hc],
                                                                  in1=psm[:, 256 + g * 128:256 + (g + 1) * 128],
                                                                  op=ALU.mult), reads=[E_b, psm_b], writes=[MT_b])
                            yield
                    xs3 = xs_tm[:, j, :].rearrange("p (h c) -> p h c", h=8)
                    k.op("dve", lambda e: e.tensor_tensor(out=xdt[:, :].rearrange("p (h c) -> p h c", h=8), in0=xs3,
                                                           in1=dt_.unsqueeze(2).to_broadcast([128, 8, 64]), op=ALU.mult),
                         reads=[xs_tm_b[j], sm_b], writes=[xdt_b])
                    yield
                    k.op("dve", lambda e: e.tensor_tensor(out=xw[:, :].rearrange("p (h c) -> p h c", h=8),
                                                           in0=xdt[:, :].rearrange("p (h c) -> p h c", h=8),
                                                           in1=dd.unsqueeze(2).to_broadcast([128, 8, 64]), op=ALU.mult),
                         reads=[xdt_b, sm_b], writes=[xw_b])
                    yield
                    tctx[j] = (pz, pz_b)
                def h2(j, t):
                    c0, c1 = t * 128, (t + 1) * 128
                    jc = slice(j * 128, (j + 1) * 128)
                    pp = j % 2
                    sm, sm_b = smP[pp], smP_b[pp]
                    MT, MT_b = MTP[pp], MTP_b[pp]
                    xdt, xdt_b = xdtP[pp], xdtP_b[pp]
                    xw, xw_b = xwP[pp], xwP_b[pp]
                    pz, pz_b = tctx[j]
                    dt_, ea, dd, cds_ = sm[:, 16:24], sm[:, 48:56], sm[:, 56:64], sm[:, 8:12]
                    xs3 = xs_tm[:, j, :].rearrange("p (h c) -> p h c", h=8)
                    pyd, pyd_b = b_yd
                    k.op("pe", lambda e: [e.matmul(pyd[:, hh * 64:(hh + 1) * 64], lhsT=MT[:, hh * 128:(hh + 1) * 128],
                                                   rhs=xdt[:, hh * 64:(hh + 1) * 64], start=True, stop=True) for hh in range(8)][-1],
                         reads=[MT_b, xdt_b], writes=[pyd_b])
                    yield
                    pyo, pyo_b = b_yo
                    k.op("pe", lambda e: [e.matmul(pyo[:, g * 256:(g + 1) * 256], lhsT=CTg[g][:, jc],
                                                   rhs=Sbf[:, :], start=True, stop=True) for g in range(2)][-1],
                         reads=[CTg_b, Sbf_b], writes=[pyo_b])
                    yield
                    pst, pst_b = b_st
                    k.op("pe", lambda e: [e.matmul(pst[:, g * 256:(g + 1) * 256], lhsT=B_tm[:, j, :],
                                                   rhs=xw[:, g * 256:(g + 1) * 256], start=True, stop=True) for g in range(2)][-1],
                         reads=[B_tm_b, xw_b], writes=[pst_b])
                    yield
                    k.op("dve", lambda e: e.tensor_tensor(out=y1[:, :].rearrange("p (h c) -> p h c", h=8),
                                                          in0=pyo[:, :].rearrange("p (h c) -> p h c", h=8),
                                                          in1=ea.unsqueeze(2).to_broadcast([128, 8, 64]), op=ALU.mult),
                         reads=[pyo_b, sm_b], writes=[y1_b])
                    yield
                    k.op("dve", lambda e: e.tensor_tensor(out=y1[:, :], in0=y1[:, :], in1=pyd[:, :], op=ALU.add),
                         reads=[y1_b, pyd_b], writes=[y1_b])
                    yield
                    k.op("dve", lambda e: e.tensor_tensor(out=y2[:, :].rearrange("p (h c) -> p h c", h=8), in0=xs3,
                                                           in1=dsk[:, :].unsqueeze(2).to_broadcast([128, 8, 64]), op=ALU.mult),
                         reads=[xs_tm_b[j], pb], writes=[y2_b])
                    yield
                    k.op("dve", lambda e: e.tensor_tensor(out=y1[:, :], in0=y1[:, :], in1=y2[:, :], op=ALU.add),
                         reads=[y1_b, y2_b], writes=[y1_b])
                    yield
                    for g in range(2):
                        rs = slice(g * 64, (g + 1) * 64)
                        k.op("dve", lambda e: e.tensor_tensor(out=Stmp[rs, :].rearrange("p (h c) -> p h c", h=4),
                                                              in0=Sst[rs, :].rearrange("p (h c) -> p h c", h=4),
                                                              in1=cds_[rs, :].unsqueeze(2).to_broadcast([64, 4, 64]),
                                                              op=ALU.mult), reads=[Sst_b, sm_b], writes=[Stmp_b])
                        yield
                        k.op("dve", lambda e: e.tensor_tensor(out=Sst[rs, :], in0=Stmp[rs, :], in1=pst[rs, g * 256:(g + 1) * 256],
                                                              op=ALU.add), reads=[Stmp_b, pst_b], writes=[Sst_b])
                        yield
                    k.op("act", lambda e: e.activation(out=Sbf[:, :], in_=Sst[:, :], func=AF.Identity), reads=[Sst_b], writes=[Sbf_b])
                    yield
                    k.op("act", lambda e: e.activation(out=sz[:, :], in_=pz[:, :], func=AF.Silu), reads=[pz_b], writes=[sz_b])
                    yield
                    k.op("dve", lambda e: e.tensor_tensor(out=y1[:, :], in0=y1[:, :], in1=sz[:, :], op=ALU.mult),
                         reads=[y1_b, sz_b], writes=[y1_b])
                    yield
                    for g in range(2):
                        k.op("dve", lambda e: e.bn_stats(out=junk[:, g * 6:(g + 1) * 6], in_=y1[:, g * 256:(g + 1) * 256]),
                             reads=[y1_b], writes=[junk_b])
                        yield
                        k.op("dve", lambda e: e.bn_aggr(out=junk[:, 16 + g * 2:18 + g * 2], in_=junk[:, g * 6:(g + 1) * 6]),
                             reads=[junk_b], writes=[junk_b])
                        yield
                        k.op("dve", lambda e: e.scalar_tensor_tensor(out=ss[:, g:g + 1], in0=junk[:, 16 + g * 2:17 + g * 2],
                                                                     scalar=junk[:, 16 + g * 2:17 + g * 2],
                                                                     in1=junk[:, 17 + g * 2:18 + g * 2], op0=ALU.mult, op1=ALU.add),
                             reads=[junk_b], writes=[ss_b])
                        yield
                    k.op("dve", lambda e: e.tensor_scalar(out=ss[:, :], in0=ss[:, :], scalar1=EPS, scalar2=None,
                                                          op0=ALU.add), reads=[ss_b], writes=[ss_b])
                    yield
                    k.op("pool", lambda e: e.tensor_tensor(out=ss[:, :], in0=ss[:, :], in1=negh[:, 0:2], op=ALU.pow),
                         reads=[ss_b, cb], writes=[ss_b])
                    yield
                    yb, yb_b = yo[t % 2], yo_b[t % 2]
                    for g in range(2):
                        gs = slice(g * 256, (g + 1) * 256)
                        k.op("dve", lambda e: e.scalar_tensor_tensor(out=yb[:, gs], in0=y1[:, gs], scalar=ss[:, g:g + 1],
                                                                     in1=ngb[:, gs], op0=ALU.mult, op1=ALU.mult),
                             reads=[y1_b, ss_b, pb], writes=[yb_b])
                        yield
                    r0 = PADR if t == 0 else 0
                    k.dma("sp", mixd[t, r0:128, 0:512], yb[r0:128, :], reads=[yb_b], writes=[mixd_b[t][0]])
                    yield
                def lockstep(gens):
                    gens = list(gens)
                    while gens:
                        for g_ in list(gens):
                            try:
                                next(g_)
                            except StopIteration:
                                gens.remove(g_)
                lockstep([h1(0, tiles[0])])
                for j in range(len(tiles)):
                    gl = [h2(j, tiles[j])]
                    if j + 1 < len(tiles):
                        gl.append(h1(j + 1, tiles[j + 1]))
                    lockstep(gl)
        k.barrier()

    def phase_fox(l):
        win = W["w_in"][l].rearrange("(kc p) n -> p kc n", p=128)
        PJ = PsPool(banks[0:2])
        STP = PsPool(banks[2:5] + banks[7:8])
        OP = PsPool(banks[5:7])
        with ExitStack() as es:
            A = lambda nm, shp, dt: es.enter_context(nc.sbuf_tensor(un(nm), shp, dt))
            wf = A("wf", [128, KC, 772], BF16); wf_b = Buf()
            wqa = A("wqa", [128, 4, KC, 68], BF16); wqa_b = Buf()
            fbn = A("fbn", [128, 4], F32); fbn_b = Buf()
            KT = [A("KTf", [128, S], BF16) for _ in range(4)]
            KT_b = [[Buf() for _ in range(NT)] for _ in range(4)]
            V = A("Vf", [128, NT, 4, 66], BF16); V_b = [Buf() for _ in range(NT)]
            QT = [A("QTf", [128, 512], BF16) for _ in range(4)]; QT_b = [Buf() for _ in range(4)]
            ef = A("ef", [128, 512], F32); ef_b = Buf()
            c4 = A("c4", [128, 512], F32); c4_b = Buf()
            chi = A("chi", [128, 512], BF16); chi_b = Buf()
            clo = A("clo", [128, 512], F32); clo_b = Buf()
            ones4 = A("ones4", [128, 512], F32); ones4_b = Buf()
            cprev = A("cprev", [128, 4], F32); cprev_b = Buf()
            PT = [A("PT", [128, 512], BF16) for _ in range(4)]; PT_b = [Buf() for _ in range(4)]
            pti = [0]
            otile = [A("otile", [128, 4, 256], BF16) for _ in range(2)]; otile_b = [Buf(), Buf()]
            den = A("den", [128, 4], F32); den_b = Buf()

            k.dma("pool", wf[:, :, :], win[:, :, 1288:2060], writes=[wf_b])
            k.dma("sp", fbn[64:68, :], W["fox_f_b"][l].rearrange("(o d) -> o d", o=1).to_broadcast([4, 4]), writes=[fbn_b])
            k.op("dve", lambda e: e.tensor_scalar(out=fbn[64:68, :], in0=fbn[64:68, :], scalar1=-1.0, scalar2=None, op0=ALU.mult),
                 reads=[fbn_b], writes=[fbn_b])
            for h in range(4):
                k.op("dve", lambda e: e.tensor_copy(out=wqa[:, h, :, 0:64], in_=wf[:, :, h * 64:(h + 1) * 64]),
                     reads=[wf_b], writes=[wqa_b])
                k.op("dve", lambda e: e.tensor_copy(out=wqa[:, h, :, 64:68],
                                                    in_=wf[:, :, 768 + h:769 + h].to_broadcast([128, KC, 4])),
                     reads=[wf_b], writes=[wqa_b])
            k.op("dve", lambda e: e.memset(V[:, :, :, 64:66], 0.0), writes=V_b)
            k.op("dve", lambda e: e.memset(V[:, :, :, 64:65], 1.0), writes=V_b)
            k.op("dve", lambda e: e.memset(V[0:PADR, 0, :, 64:65], 0.0), writes=[V_b[0]])
            k.op("dve", lambda e: e.memset(ones4[64:68, :], 1.0), writes=[ones4_b])
            k.op("dve", lambda e: e.memset(cprev[64:68, :], 0.0), writes=[cprev_b])
            R4 = slice(64, 68)
            for ci_, tiles in enumerate(chunk_list(NT)):
                s0, nt = tiles[0] * 128, len(tiles)
                n = nt * 128
                hb = [hT_b[t] for t in tiles]
                for h in range(4):
                    pq, pq_b = PJ.get()
                    k.op("pe", lambda e: [e.matmul(pq[0:68, 0:n], lhsT=wqa[:, h, kc, :], rhs=hT[:, kc, s0:s0 + n],
                                                   start=(kc == 0), stop=(kc == KC - 1)) for kc in range(KC)][-1],
                         reads=[wqa_b] + hb, writes=[pq_b])
                    k.op("act", lambda e: e.activation(out=QT[h][0:64, 0:n], in_=pq[0:64, 0:n], func=AF.Identity),
                         reads=[pq_b], writes=[QT_b[h]])
                    k.op("act", lambda e: e.activation(out=ef[R4, 0:n], in_=pq[R4, 0:n], func=AF.Exp, scale=-1.0,
                                                       bias=fbn[R4, h:h + 1]), reads=[pq_b, fbn_b], writes=[ef_b])
                    k.op("act", lambda e: e.activation(out=ef[R4, 0:n], in_=ef[R4, 0:n], func=AF.Ln, bias=1.0),
                         reads=[ef_b], writes=[ef_b])
                    k.op("dve", lambda e: e.tensor_tensor_scan(out=c4[R4, 0:n], data0=ones4[R4, 0:n], data1=ef[R4, 0:n],
                                                               initial=cprev[R4, h:h + 1], op0=ALU.mult, op1=ALU.subtract),
                         reads=[ones4_b, ef_b, cprev_b], writes=[c4_b])
                    k.op("dve", lambda e: e.tensor_copy(out=cprev[R4, h:h + 1], in_=c4[R4, n - 1:n]), reads=[c4_b], writes=[cprev_b])
                    k.op("dve", lambda e: e.tensor_copy(out=chi[R4, 0:n], in_=c4[R4, 0:n]), reads=[c4_b], writes=[chi_b])
                    k.op("dve", lambda e: e.tensor_tensor(out=clo[R4, 0:n], in0=c4[R4, 0:n], in1=chi[R4, 0:n], op=ALU.subtract),
                         reads=[c4_b, chi_b], writes=[clo_b])
                    kts = [KT_b[h][t] for t in tiles]
                    k.op("dve", lambda e: e.tensor_scalar(out=c4[R4, 0:n], in0=chi[R4, 0:n], scalar1=aug[R4, 0:1], scalar2=aug[R4, 2:3],
                                                          op0=ALU.mult, op1=ALU.add), reads=[chi_b, cb, c4_b], writes=[c4_b])
                    k.op("dve", lambda e: e.scalar_tensor_tensor(out=KT[h][R4, s0:s0 + n], in0=clo[R4, 0:n], scalar=aug[R4, 1:2],
                                                                 in1=c4[R4, 0:n], op0=ALU.mult, op1=ALU.add),
                         reads=[clo_b, c4_b, cb], writes=kts)
                    k.op("dve", lambda e: e.tensor_scalar(out=c4[R4, 0:n], in0=chi[R4, 0:n], scalar1=aug[R4, 3:4], scalar2=aug[R4, 5:6],
                                                          op0=ALU.mult, op1=ALU.add), reads=[chi_b, cb, c4_b], writes=[c4_b])
                    k.op("dve", lambda e: e.scalar_tensor_tensor(out=QT[h][R4, 0:n], in0=clo[R4, 0:n], scalar=aug[R4, 4:5],
                                                                 in1=c4[R4, 0:n], op0=ALU.mult, op1=ALU.add),
                         reads=[clo_b, c4_b, cb], writes=[QT_b[h]])
                    pk, pk_b = PJ.get()
                    k.op("pe", lambda e: [e.matmul(pk[0:64, 0:n], lhsT=wf[:, kc, 256 + h * 64:256 + (h + 1) * 64],
                                                   rhs=hT[:, kc, s0:s0 + n], start=(kc == 0), stop=(kc == KC - 1))
                                          for kc in range(KC)][-1], reads=[wf_b] + hb, writes=[pk_b])
                    k.op("act", lambda e: e.activation(out=KT[h][0:64, s0:s0 + n], in_=pk[0:64, 0:n], func=AF.Identity),
                         reads=[pk_b], writes=kts)
                for j, t in enumerate(tiles):
                    pv, pv_b = PJ.get()
                    k.op("pe", lambda e: [e.matmul(pv[:, 0:256], lhsT=hT[:, kc, t * 128:(t + 1) * 128], rhs=wf[:, kc, 512:768],
                                                   start=(kc == 0), stop=(kc == KC - 1)) for kc in range(KC)][-1],
                         reads=[wf_b, hT_b[t]], writes=[pv_b])
                    k.op("act", lambda e: e.activation(out=V[:, t, :, 0:64], in_=pv[:, 0:256].rearrange("p (h c) -> p h c", h=4),
                                                       func=AF.Identity), reads=[pv_b], writes=[V_b[t]])
                ot, ot_b = otile[ci_ % 2], otile_b[ci_ % 2]
                for h in range(4):
                    po, po_b = attention(KT[h], KT_b[h], QT[h], QT_b[h], V, V_b, 68, 0.125, tiles, h, ot, ot_b, PT, PT_b, pti, STP, OP)
                    attn_norm(po, po_b, nt, h, ot, ot_b, den, den_b)
                for j, t in enumerate(tiles):
                    r0 = PADR if t == 0 else 0
                    k.dma("sp", mixd[t, r0:128, 512:768], ot[r0:128, j, :], reads=[ot_b], writes=[mixd_b[t][1]])
        k.barrier()

    def phase_mla(l):
        win = W["w_in"][l].rearrange("(kc p) n -> p kc n", p=128)
        PJ = PsPool(banks[0:2])
        STP = PsPool(banks[2:5])
        OP = PsPool(banks[5:7])
        b_x = banks[7]
        scale = float(96 ** -0.5)
        with ExitStack() as es:
            A = lambda nm, shp, dt: es.enter_context(nc.sbuf_tensor(un(nm), shp, dt))
            wm = A("wm", [128, KC, 416], BF16); wm_b = Buf()
            wuq = A("wuq", [128, 2, 384], BF16); wuqs = A("wuqs", [128, 2, 4, 96], BF16)
            wukv = A("wukv", [128, 512], BF16)
            wv = A("wv", [128, 256], BF16)
            wkr = A("wkr", [128, 2, KC, 96], BF16)
            qg = A("qg", [128, 2], F32); kvg = A("kvg", [128, 1], F32)
            wp_b = Buf()
            KT = [A("KTm", [128, S], BF16) for _ in range(4)]
            KT_b = [[Buf() for _ in range(NT)] for _ in range(4)]
            V = A("Vm", [128, NT, 4, 66], BF16); V_b = [Buf() for _ in range(NT)]
            QT = [A("QTm", [128, 512], BF16) for _ in range(4)]; QT_b = [Buf() for _ in range(4)]
            cqn = A("cqn", [128, 2, 512], BF16); cqn_b = Buf()
            ckvn = A("ckvn", [128, 512], BF16); ckvn_b = Buf()
            sqq = [A("sqq", [128, 512], BF16) for _ in range(2)]; sqq_b = [Buf(), Buf()]
            sqk = A("sqk", [128, 512], BF16); sqk_b = Buf()
            rq = A("rq", [128, 512], F32); rq_b = Buf()
            rkv = A("rkv", [128, 512], F32); rkv_b = Buf()
            rv = A("rv", [128, 4], F32); rv_b = Buf()
            cs = A("cs", [128, 2, 512], F32); cs_b = Buf()
            csr = A("csr", [128, 2, 512], F32); csr_b = Buf()
            t1 = A("t1", [128, 512], F32); t1_b = Buf()
            t2 = A("t2", [128, 512], F32); t2_b = Buf()
            PT = [A("PT", [128, 512], BF16) for _ in range(4)]; PT_b = [Buf() for _ in range(4)]
            pti = [0]
            otile = [A("otile", [128, 4, 256], BF16) for _ in range(2)]; otile_b = [Buf(), Buf()]
            den = A("den", [128, 4], F32); den_b = Buf()

            k.dma("pool", wm[:, :, :], win[:, :, 2060:2476], writes=[wm_b])
            k.dma("pool", wuq[:, :, :], W["mla_w_uq"][l].rearrange("(j p) n -> p j n", p=128), writes=[wp_b])
            k.dma("pool", wukv[:, :], W["mla_w_ukv"][l], writes=[wp_b])
            k.dma("sp", qg[:, :], W["mla_q_norm_g"][l].rearrange("(j p) -> p j", p=128), writes=[wp_b],
                  allow_slow_non_contiguous=True)
            k.dma("sp", kvg[:, :], W["mla_kv_norm_g"][l].rearrange("(p o) -> p o", o=1), writes=[wp_b])
            wuq4 = wuq[:, :, :].rearrange("p j (h c) -> p j h c", h=4)
            k.op("dve", lambda e: e.tensor_copy(out=wuqs[:, :, :, :], in_=wuq4), reads=[wp_b], writes=[wp_b])
            k.op("dve", lambda e: e.tensor_copy(out=wuqs[:, :, :, 64:80], in_=wuq4[:, :, :, 80:96]), reads=[wp_b], writes=[wp_b])
            k.op("dve", lambda e: e.tensor_copy(out=wuqs[:, :, :, 80:96], in_=wuq4[:, :, :, 64:80]), reads=[wp_b], writes=[wp_b])
            k.op("dve", lambda e: e.tensor_copy(out=wv[:, :].rearrange("p (h c) -> p h c", h=4),
                                                in_=wukv[:, :].rearrange("p (h c) -> p h c", h=4)[:, :, 64:128]),
                 reads=[wp_b], writes=[wp_b])
            k.op("dve", lambda e: e.memset(wkr[:, :, :, :], 0.0), writes=[wp_b])
            k.op("dve", lambda e: e.tensor_copy(out=wkr[:, 0, :, 64:96], in_=wm[:, :, 384:416]), reads=[wm_b, wp_b], writes=[wp_b])
            k.op("dve", lambda e: e.tensor_copy(out=wkr[:, 1, :, 64:80], in_=wm[:, :, 400:416]), reads=[wm_b, wp_b], writes=[wp_b])
            k.op("dve", lambda e: e.tensor_copy(out=wkr[:, 1, :, 80:96], in_=wm[:, :, 384:400]), reads=[wm_b, wp_b], writes=[wp_b])
            k.op("dve", lambda e: e.memset(V[:, :, :, 64:66], 0.0), writes=V_b)
            k.op("dve", lambda e: e.memset(V[:, :, :, 64:65], 1.0), writes=V_b)
            k.op("dve", lambda e: e.memset(V[0:PADR, 0, :, 64:65], 0.0), writes=[V_b[0]])
            RR = slice(64, 96)
            for ci_, tiles in enumerate(chunk_list(NT)):
                s0, nt = tiles[0] * 128, len(tiles)
                n = nt * 128
                hb = [hT_b[t] for t in tiles]
                k.dma("sp", cs[RR, 0, 0:n], c_cos[:, s0:s0 + n], writes=[cs_b])
                k.dma("sp", cs[RR, 1, 0:n], c_sin[:, s0:s0 + n], writes=[cs_b])
                for j2 in range(2):
                    pc, pc_b = PJ.get()
                    k.op("pe", lambda e: [e.matmul(pc[:, 0:n], lhsT=wm[:, kc, j2 * 128:(j2 + 1) * 128], rhs=hT[:, kc, s0:s0 + n],
                                                   start=(kc == 0), stop=(kc == KC - 1)) for kc in range(KC)][-1],
                         reads=[wm_b] + hb, writes=[pc_b])
                    k.op("act", lambda e: e.activation(out=cqn[:, j2, 0:n], in_=pc[:, 0:n], func=AF.Identity, scale=qg[:, j2:j2 + 1]),
                         reads=[pc_b, wp_b], writes=[cqn_b])
                    k.op("act", lambda e: e.activation(out=sqq[j2][:, 0:n], in_=pc[:, 0:n], func=AF.Square),
                         reads=[pc_b], writes=[sqq_b[j2]])
                px, px_b = b_x
                k.op("pe", lambda e: [e.matmul(px[:, 0:n], lhsT=ones_bf[:, :], rhs=sqq[0][:, 0:n], start=True, stop=False),
                                      e.matmul(px[:, 0:n], lhsT=ones_bf[:, :], rhs=sqq[1][:, 0:n], start=False, stop=True)][-1],
                     reads=sqq_b + [cb], writes=[px_b])
                k.op("dve", lambda e: e.tensor_scalar(out=rq[:, 0:n], in0=px[:, 0:n], scalar1=1.0 / 256, scalar2=EPS,
                                                      op0=ALU.mult, op1=ALU.add), reads=[px_b], writes=[rq_b])
                k.op("act", lambda e: e.activation(out=rq[:, 0:n], in_=rq[:, 0:n], func=AF.Ln), reads=[rq_b], writes=[rq_b])
                k.op("act", lambda e: e.activation(out=rq[:, 0:n], in_=rq[:, 0:n], func=AF.Exp, scale=-0.5), reads=[rq_b], writes=[rq_b])
                pc, pc_b = PJ.get()
                k.op("pe", lambda e: [e.matmul(pc[:, 0:n], lhsT=wm[:, kc, 256:384], rhs=hT[:, kc, s0:s0 + n],
                                               start=(kc == 0), stop=(kc == KC - 1)) for kc in range(KC)][-1],
                     reads=[wm_b] + hb, writes=[pc_b])
                k.op("act", lambda e: e.activation(out=ckvn[:, 0:n], in_=pc[:, 0:n], func=AF.Identity, scale=kvg[:, 0:1]),
                     reads=[pc_b, wp_b], writes=[ckvn_b])
                k.op("act", lambda e: e.activation(out=sqk[:, 0:n], in_=pc[:, 0:n], func=AF.Square), reads=[pc_b], writes=[sqk_b])
                px, px_b = b_x
                k.op("pe", lambda e: e.matmul(px[:, 0:n], lhsT=ones_bf[:, :], rhs=sqk[:, 0:n], start=True, stop=True),
                     reads=[sqk_b, cb], writes=[px_b])
                k.op("dve", lambda e: e.tensor_scalar(out=rkv[:, 0:n], in0=px[:, 0:n], scalar1=1.0 / 128, scalar2=EPS,
                                                      op0=ALU.mult, op1=ALU.add), reads=[px_b], writes=[rkv_b])
                k.op("act", lambda e: e.activation(out=rkv[:, 0:n], in_=rkv[:, 0:n], func=AF.Ln), reads=[rkv_b], writes=[rkv_b])
                k.op("act", lambda e: e.activation(out=rkv[:, 0:n], in_=rkv[:, 0:n], func=AF.Exp, scale=-0.5), reads=[rkv_b], writes=[rkv_b])
                px, px_b = b_x
                k.op("pe", lambda e: [e.matmul(px[:, 2 * j:2 * j + 2], lhsT=sqk[:, j * 128:(j + 1) * 128], rhs=ones_bf[:, 0:2],
                                               start=True, stop=True) for j in range(nt)][-1],
                     reads=[sqk_b, cb], writes=[px_b])
                k.op("dve", lambda e: e.tensor_scalar(out=rv[:, 0:nt], in0=px[:, 0:2 * nt].rearrange("p (j c) -> p j c", c=2)[:, :, 0],
                                                      scalar1=1.0 / 128, scalar2=EPS,
                                                      op0=ALU.mult, op1=ALU.add), reads=[px_b], writes=[rv_b])
                k.op("pool", lambda e: e.tensor_tensor(out=rv[:, 0:nt], in0=rv[:, 0:nt], in1=negh[:, 0:nt], op=ALU.pow),
                     reads=[rv_b, cb], writes=[rv_b])
                for i2 in range(2):
                    k.op("dve", lambda e: e.tensor_tensor(out=csr[RR, i2, 0:n], in0=cs[RR, i2, 0:n], in1=rq[RR, 0:n], op=ALU.mult),
                         reads=[cs_b, rq_b], writes=[csr_b])
                for h in range(4):
                    pq1, pq1_b = PJ.get()
                    pq2, pq2_b = PJ.get()
                    k.op("pe", lambda e: [e.matmul(pq1[0:96, 0:n], lhsT=wuq[:, j2, h * 96:(h + 1) * 96], rhs=cqn[:, j2, 0:n],
                                                   start=(j2 == 0), stop=(j2 == 1)) for j2 in range(2)][-1],
                         reads=[wp_b, cqn_b], writes=[pq1_b])
                    k.op("pe", lambda e: [e.matmul(pq2[0:96, 0:n], lhsT=wuqs[:, j2, h, :], rhs=cqn[:, j2, 0:n],
                                                   start=(j2 == 0), stop=(j2 == 1)) for j2 in range(2)][-1],
                         reads=[wp_b, cqn_b], writes=[pq2_b])
                    k.op("dve", lambda e: e.tensor_tensor(out=QT[h][0:64, 0:n], in0=pq1[0:64, 0:n], in1=rq[0:64, 0:n], op=ALU.mult),
                         reads=[pq1_b, rq_b], writes=[QT_b[h]])
                    k.op("dve", lambda e: e.tensor_tensor(out=t1[RR, 0:n], in0=pq1[RR, 0:n], in1=csr[RR, 0, 0:n], op=ALU.mult),
                         reads=[pq1_b, csr_b], writes=[t1_b])
                    k.op("dve", lambda e: e.tensor_tensor(out=t2[RR, 0:n], in0=pq2[RR, 0:n], in1=csr[RR, 1, 0:n], op=ALU.mult),
                         reads=[pq2_b, csr_b], writes=[t2_b])
                    k.op("dve", lambda e: e.tensor_tensor(out=QT[h][RR, 0:n], in0=t1[RR, 0:n], in1=t2[RR, 0:n], op=ALU.add),
                         reads=[t1_b, t2_b], writes=[QT_b[h]])
                pk1, pk1_b = PJ.get()
                pk2, pk2_b = PJ.get()
                for i2, (pk, pk_b) in enumerate([(pk1, pk1_b), (pk2, pk2_b)]):
                    k.op("pe", lambda e: [e.matmul(pk[0:96, 0:n], lhsT=wkr[:, i2, kc, :], rhs=hT[:, kc, s0:s0 + n],
                                                   start=(kc == 0), stop=(kc == KC - 1)) for kc in range(KC)][-1],
                         reads=[wp_b] + hb, writes=[pk_b])
                k.op("dve", lambda e: e.tensor_tensor(out=t1[RR, 0:n], in0=pk1[RR, 0:n], in1=cs[RR, 0, 0:n], op=ALU.mult),
                     reads=[pk1_b, cs_b], writes=[t1_b])
                k.op("dve", lambda e: e.tensor_tensor(out=t2[RR, 0:n], in0=pk2[RR, 0:n], in1=cs[RR, 1, 0:n], op=ALU.mult),
                     reads=[pk2_b, cs_b], writes=[t2_b])
                for h in range(4):
                    kts = [KT_b[h][t] for t in tiles]
                    k.op("dve", lambda e: e.tensor_tensor(out=KT[h][RR, s0:s0 + n], in0=t1[RR, 0:n], in1=t2[RR, 0:n], op=ALU.add),
                         reads=[t1_b, t2_b], writes=kts)
                    pkn, pkn_b = PJ.get()
                    k.op("pe", lambda e: e.matmul(pkn[0:64, 0:n], lhsT=wukv[:, h * 128:h * 128 + 64], rhs=ckvn[:, 0:n],
                                                  start=True, stop=True), reads=[wp_b, ckvn_b], writes=[pkn_b])
                    k.op("dve", lambda e: e.tensor_tensor(out=KT[h][0:64, s0:s0 + n], in0=pkn[0:64, 0:n], in1=rkv[0:64, 0:n], op=ALU.mult),
                         reads=[pkn_b, rkv_b], writes=kts)
                for j, t in enumerate(tiles):
                    pv, pv_b = PJ.get()
                    k.op("pe", lambda e: e.matmul(pv[:, 0:256], lhsT=ckvn[:, j * 128:(j + 1) * 128],
                                                  rhs=wv[:, :],
                                                  start=True, stop=True), reads=[wp_b, ckvn_b], writes=[pv_b])
                    k.op("act", lambda e: e.activation(out=V[:, t, :, 0:64], in_=pv[:, 0:256].rearrange("p (h c) -> p h c", h=4),
                                                       func=AF.Identity, scale=rv[:, j:j + 1]), reads=[pv_b, rv_b], writes=[V_b[t]])
                ot, ot_b = otile[ci_ % 2], otile_b[ci_ % 2]
                for h in range(4):
                    po, po_b = attention(KT[h], KT_b[h], QT[h], QT_b[h], V, V_b, 96, scale, tiles, h, ot, ot_b, PT, PT_b, pti, STP, OP)
                    attn_norm(po, po_b, nt, h, ot, ot_b, den, den_b)
                for j, t in enumerate(tiles):
                    r0 = PADR if t == 0 else 0
                    k.dma("sp", mixd[t, r0:128, 768:1024], ot[r0:128, j, :], reads=[ot_b], writes=[mixd_b[t][2]])
        k.barrier()

    def phase_out(l, ci):
        PG = PsPool(banks[0:6])
        with ExitStack() as es:
            A = lambda nm, shp, dt: es.enter_context(nc.sbuf_tensor(un(nm), shp, dt))
            wout = A("wout", [128, KC, D], BF16); wout_b = Buf()
            mt = [A("mt", [128, D], BF16) for _ in range(2)]; mt_b = [Buf(), Buf()]
            mixT = [A("mixT", [128, KC, 128], BF16) for _ in range(2)]; mixT_b = [Buf(), Buf()]
            gb = A("gb", [128, 2, D], F32); gb_b = Buf()
            lnb = make_ln_bufs(A)
            k.dma("pool", wout[:, :, :], W["w_out"][l].rearrange("(kc p) n -> p kc n", p=128), writes=[wout_b])
            k.dma("sp", gb[:, 0, :], W["ln2_g"][l].rearrange("(o d) -> o d", o=1).to_broadcast([128, D]), writes=[gb_b])
            k.dma("sp", gb[:, 1, :], W["ln2_b"][l].rearrange("(o d) -> o d", o=1).to_broadcast([128, D]), writes=[gb_b])

            def load_m(t):
                p = t % 2
                if t == 0:
                    k.op("dve", lambda e: e.memset(mt[p][:, :], 0.0), writes=[mt_b[p]])
                r0 = PADR if t == 0 else 0
                k.dma("sp", mt[p][r0:128, :], mixd[t, r0:128, :], reads=mixd_b[t], writes=[mt_b[p]])
                load_h(t, lnb[p])

            load_m(0)
            for t in range(NT):
                p = t % 2
                if t + 1 < NT:
                    load_m(t + 1)
                tb, tb_b = TP.get()
                tbv = tb[:, :].bitcast(BF16)
                k.op("pe", lambda e: [e.transpose(out=tbv[:, kc * 128:(kc + 1) * 128], in_=mt[p][:, kc * 128:(kc + 1) * 128],
                                                  identity=ident_bf[:, :]) for kc in range(KC)][-1],
                     reads=[mt_b[p], cb], writes=[tb_b])
                k.op("act", lambda e: e.activation(out=mixT[p][:, :, :], in_=tbv.rearrange("p (kc c) -> p kc c", kc=KC),
                                                   func=AF.Identity), reads=[tb_b], writes=[mixT_b[p]])
                ys = []
                for hf in range(2):
                    py, py_b = PG.get()
                    k.op("pe", lambda e: [e.matmul(py[:, :], lhsT=mixT[p][:, kc, :], rhs=wout[:, kc, hf * 512:(hf + 1) * 512],
                                                   start=(kc == 0), stop=(kc == KC - 1)) for kc in range(KC)][-1],
                         reads=[mixT_b[p], wout_b], writes=[py_b])
                    ys.append((py, py_b))
                ln_tile(t, ys, 1.0, gb, gb_b, ci, lnb[p], False)
        k.barrier()

    def run():
        phase_init()
        if only is not None:
            for ph in only:
                try:
                    dict(ssd=phase_ssd, fox=phase_fox, mla=phase_mla)[ph](0)
                except _StopPhase:
                    k.barrier()
            if not _cpstop:
                for ph in only:
                    dump_mix("mix_0", dict(ssd=(0, 512), fox=(512, 768), mla=(768, 1024))[ph])
            return
        for l in range(depth):
            last = (l == depth - 1)
            phase_ffn(l, 1, l * 6 + 0, False)
            dump_h("h1_%d" % l)
            if stop_after == "h1_%d" % l:
                return
            phase_ssd(l)
            phase_fox(l)
            phase_mla(l)
            dump_mix("mix_%d" % l)
            if stop_after == "mix_%d" % l:
                return
            phase_out(l, l * 6 + 2)
            dump_h("h2_%d" % l)
            if stop_after == "h2_%d" % l:
                return
            phase_ffn(l, 2, l * 6 + 4, last)
            dump_h("h3_%d" % l)
            if stop_after == "h3_%d" % l:
                return

    run()
    k.barrier()
    k.finish(out_b + dbg_b)
    build.stats = dict(nins=k.nins, nwait=k.nwait)
    return nc


def host_consts(NT):
    S = NT * 128
    bf = ml_dtypes.bfloat16
    idx = np.arange(128)
    c = {}
    c["c_ident_bf"] = np.eye(128, dtype=np.float32).astype(bf)
    c["c_ident_f"] = np.eye(128, dtype=np.float32)
    c["c_tri"] = (idx[:, None] <= idx[None, :]).astype(np.float32)
    mneg = np.where(idx[:, None] > idx[None, :], -30000.0, 0.0).astype(np.float32)
    c["c_maskneg"] = mneg.astype(bf)
    c["c_maskrep"] = np.tile(mneg, (1, 4)).astype(bf)
    pos = (np.arange(S) - PADR).astype(np.float32)
    inv_freq = (1.0 / (np.float32(10000.0) ** (np.arange(0, 32, 2, dtype=np.float32) / np.float32(32)))).astype(np.float32)
    ang = pos[None, :] * inv_freq[:, None]
    cos = np.cos(ang).astype(np.float32)
    sin = np.sin(ang).astype(np.float32)
    c["c_cos"] = np.concatenate([cos, cos], 0)
    c["c_sin"] = np.concatenate([-sin, sin], 0)
    aug = np.zeros((4, 6), np.float32)
    aug[0, 0] = -8.0
    aug[1, 1] = -8.0
    aug[2, 2] = 1.0
    aug[3, 2] = 1.0
    aug[2, 3] = 8.0
    aug[3, 4] = 8.0
    aug[0, 5] = 1.0
    aug[1, 5] = 1.0
    c["c_aug"] = aug
    return c


_CACHE = {}


def kernel(**inputs):
    x = np.ascontiguousarray(inputs["x"], dtype=np.float32)
    B, SEQ, _ = x.shape
    NT = SEQ // 128 + 1
    key = (NT,)
    if key not in _CACHE:
        _CACHE[key] = build(NT)
    nc = _CACHE[key]
    consts = host_consts(NT)
    shared = {name: np.ascontiguousarray(inputs[name], dtype=np.float32) for name, _ in PARAM_SHAPES}
    shared["meta"] = np.ascontiguousarray(inputs["meta"], dtype=np.float32)
    shared.update(consts)
    in_maps = []
    for b in range(B):
        m = dict(shared)
        m["x"] = x[b]
        in_maps.append(m)
    res = run_bass_kernel_spmd(nc, in_maps, core_ids=list(range(B)))
    out = np.stack([np.asarray(res.results[b]["out"], dtype=np.float32) for b in range(B)], 0)
    return out
```
